# Optimizing a Trainium2 kernel written in Bass

```python
import math
import jax, jax.numpy as jnp
from jax import lax
import numpy as np

D_MODEL = 1024
BATCH = 16
SEQ = 256
DEPTH = 2
DEC_BATCH = 8
DEC_SEQ = 2048
PAST_LEN = 256

GRID_W = 64
N_EVEN = (DEPTH + 1) // 2
N_ODD = DEPTH // 2
N_Q_HEADS = 8
N_KV_HEADS = 2
GROUP = N_Q_HEADS // N_KV_HEADS
HEAD_DIM = 64
ATTN_W = N_Q_HEADS * HEAD_DIM
KV_W = N_KV_HEADS * HEAD_DIM
WINDOW = 128
BLOCK = 128
ATTN_SCALE = HEAD_DIM ** -0.5
ROPE_BASE = 10000.0
POOL_WINDOWS = (2, 4, 8, 16)
N_POOL_GROUPS = 4
POOL_W = D_MODEL // 2
POOL_GW = POOL_W // N_POOL_GROUPS
EVEN_IN = ATTN_W + 2 * KV_W + POOL_W
EVEN_CAT = ATTN_W + POOL_W
LRU_W = D_MODEL
LRU_HEADS = 8
LRU_HD = LRU_W // LRU_HEADS
CONV_W = 4
LRU_C = 8.0
ODD_IN = 2 * LRU_W
N_KEYS = 128
N_EXPERTS = N_KEYS * N_KEYS
PEER_HEADS = 8
PEER_QD = 256
PEER_TOPK = 16
PEER_BLOCK = 128
ALPHA = (2 * DEPTH) ** 0.25
BETA = (8 * DEPTH) ** -0.25
LN_EPS = 1e-5

kernel_name = 'hybrid_diffusion_ctx_prefix_step'


def layer_norm(x, g, b):
    xf = x.astype(jnp.float32)
    mu = xf.mean(-1, keepdims=True)
    var = jnp.square(xf - mu).mean(-1, keepdims=True)
    return ((xf - mu) * lax.rsqrt(var + LN_EPS)).astype(x.dtype) * g + b


def axial_rope(n_tokens):
    rows = n_tokens // GRID_W
    row = jnp.repeat(jnp.arange(rows, dtype=jnp.float32), GRID_W)
    col = (jnp.arange(rows * GRID_W) % GRID_W).astype(jnp.float32)
    n_freq = HEAD_DIM // 4
    inv = ROPE_BASE ** (-jnp.arange(n_freq, dtype=jnp.float32) / n_freq)
    ang = jnp.concatenate([row[:, None] * inv, col[:, None] * inv], axis=-1)
    return jnp.cos(ang), jnp.sin(ang)


def apply_rope(x, cos, sin):
    half = HEAD_DIM // 2
    c = cos[None, :, None, :].astype(x.dtype)
    s = sin[None, :, None, :].astype(x.dtype)
    x1, x2 = x[..., :half], x[..., half:]
    return jnp.concatenate([x1 * c - x2 * s, x2 * c + x1 * s], axis=-1)


def context_attention(q, k, v, sink):
    B, S = q.shape[:2]
    nb = S // BLOCK
    qb = q.reshape(B, nb, BLOCK, N_KV_HEADS, GROUP, HEAD_DIM).transpose(1, 0, 2, 3, 4, 5)
    sk = sink.astype(jnp.float32).reshape(N_KV_HEADS, GROUP)[None, :, :, None]

    def one_block(qblk):
        s = jnp.einsum('bqkgd,bskd->bkgqs', qblk, k).astype(jnp.float32) * ATTN_SCALE
        m = jnp.maximum(s.max(-1), sk)
        p = jnp.exp(s - m[..., None])
        den = p.sum(-1) + jnp.exp(sk - m)
        return jnp.einsum('bkgqs,bskd->bqkgd', (p / den[..., None]).astype(v.dtype), v)

    o = lax.map(one_block, qb)
    return o.transpose(1, 0, 2, 3, 4, 5).reshape(B, S, ATTN_W)


def latent_attention(q, k, v, ck, cv, sink):
    B, S = q.shape[:2]
    nb = S // BLOCK
    qb = q.reshape(B, nb, BLOCK, N_KV_HEADS, GROUP, HEAD_DIM)
    pad = ((0, 0), (BLOCK, BLOCK), (0, 0), (0, 0))
    kp = jnp.pad(k, pad).reshape(B, nb + 2, BLOCK, N_KV_HEADS, HEAD_DIM)
    vp = jnp.pad(v, pad).reshape(B, nb + 2, BLOCK, N_KV_HEADS, HEAD_DIM)
    kw = jnp.concatenate([kp[:, :-2], kp[:, 1:-1], kp[:, 2:]], axis=2)
    vw = jnp.concatenate([vp[:, :-2], vp[:, 1:-1], vp[:, 2:]], axis=2)
    qi = jnp.arange(BLOCK)[:, None]
    kj = jnp.arange(3 * BLOCK)[None, :]
    rel = kj - BLOCK - qi
    jpos = (jnp.arange(nb)[:, None, None] - 1) * BLOCK + kj[None]
    mask = (jnp.abs(rel) <= WINDOW)[None] & (jpos >= 0) & (jpos < S)
    s_lat = jnp.einsum('bnqkgd,bnskd->bnkgqs', qb, kw).astype(jnp.float32) * ATTN_SCALE
    s_lat = jnp.where(mask[None, :, None, None], s_lat, -jnp.inf)
    s_ctx = jnp.einsum('bnqkgd,blkd->bnkgql', qb, ck).astype(jnp.float32) * ATTN_SCALE
    sk = sink.astype(jnp.float32).reshape(N_KV_HEADS, GROUP)[None, None, :, :, None]
    m = jnp.maximum(jnp.maximum(s_lat.max(-1), s_ctx.max(-1)), sk)
    p_lat = jnp.exp(s_lat - m[..., None])
    p_ctx = jnp.exp(s_ctx - m[..., None])
    den = (p_lat.sum(-1) + p_ctx.sum(-1) + jnp.exp(sk - m))[..., None]
    o = (jnp.einsum('bnkgqs,bnskd->bnqkgd', (p_lat / den).astype(v.dtype), vw)
         + jnp.einsum('bnkgql,blkd->bnqkgd', (p_ctx / den).astype(cv.dtype), cv))
    return o.reshape(B, S, ATTN_W)


def multiscale_pool(p, pool_w, pool_scale):
    B, S, _ = p.shape
    pg = p.reshape(B, S, N_POOL_GROUPS, POOL_GW)
    cs = jnp.concatenate([jnp.zeros((B, 1, N_POOL_GROUPS, POOL_GW), jnp.float32),
                          jnp.cumsum(pg.astype(jnp.float32), axis=1)], axis=1)
    t = jnp.arange(S)
    outs = []
    for g, w in enumerate(POOL_WINDOWS):
        lo = w // 2
        hi = w - lo - 1
        start = jnp.clip(t - lo, 0, S)
        end = jnp.clip(t + hi + 1, 0, S)
        mean = (cs[:, end, g] - cs[:, start, g]) / (end - start).astype(jnp.float32)[None, :, None]
        outs.append(mean.astype(p.dtype) - pg[:, :, g])
    pooled = jnp.stack(outs, axis=2)
    y = jnp.einsum('bsgc,gcd->bsgd', pooled, pool_w).reshape(B, S, POOL_W)
    return y * pool_scale


def even_mixer(u, ctx_kv, w_in, sink, pool_w, pool_scale, w_out):
    B, S, _ = u.shape
    h = u @ w_in
    q = h[..., :ATTN_W].reshape(B, S, N_Q_HEADS, HEAD_DIM)
    k = h[..., ATTN_W:ATTN_W + KV_W].reshape(B, S, N_KV_HEADS, HEAD_DIM)
    v = h[..., ATTN_W + KV_W:ATTN_W + 2 * KV_W].reshape(B, S, N_KV_HEADS, HEAD_DIM)
    p = h[..., ATTN_W + 2 * KV_W:]
    if ctx_kv is None:
        attn = context_attention(q, k, v, sink)
        new_kv = (k, v)
    else:
        cos, sin = axial_rope(S)
        attn = latent_attention(apply_rope(q, cos, sin), apply_rope(k, cos, sin), v,
                                ctx_kv[0], ctx_kv[1], sink)
        new_kv = None
    pooled = multiscale_pool(p, pool_w, pool_scale)
    return jnp.concatenate([attn, pooled], axis=-1) @ w_out, new_kv


def centred_dwconv(x, w, b):
    S = x.shape[1]
    left = CONV_W // 2
    right = CONV_W - 1 - left
    xp = jnp.pad(x, ((0, 0), (left, right), (0, 0)))
    acc = xp[:, 0:S] * w[0]
    for j in range(1, CONV_W):
        acc = acc + xp[:, j:j + S] * w[j]
    return acc + b


def rglru_coeffs(x, wa, ba, wx, bx, lam):
    B, S, _ = x.shape
    xh = x.reshape(B, S, LRU_HEADS, LRU_HD)
    r = jax.nn.sigmoid((jnp.einsum('bshi,hij->bshj', xh, wa).reshape(B, S, LRU_W) + ba).astype(jnp.float32))
    i = jax.nn.sigmoid((jnp.einsum('bshi,hij->bshj', xh, wx).reshape(B, S, LRU_W) + bx).astype(jnp.float32))
    log_a = -LRU_C * r * jax.nn.softplus(-lam.astype(jnp.float32))
    a = jnp.exp(log_a)
    b = jnp.sqrt(-jnp.expm1(2.0 * log_a)) * (i * x.astype(jnp.float32))
    return a, b


def linear_scan(a, b, h0, reverse):
    if reverse:
        b = b.at[:, -1].add(a[:, -1] * h0)
    else:
        b = b.at[:, 0].add(a[:, 0] * h0)

    def combine(l, r):
        return l[0] * r[0], r[0] * l[1] + r[1]

    _, h = lax.associative_scan(combine, (a, b), reverse=reverse, axis=1)
    return h


def odd_mixer(u, h0, w_in, conv_w, conv_b, wa, ba, wx, bx, lam, w_out):
    B, S, _ = u.shape
    h = u @ w_in
    xr, xg = h[..., :LRU_W], h[..., LRU_W:]
    xc = centred_dwconv(xr, conv_w, conv_b)
    a_f, b_f = rglru_coeffs(xc, wa[0], ba[0], wx[0], bx[0], lam[0])
    a_b, b_b = rglru_coeffs(xc, wa[1], ba[1], wx[1], bx[1], lam[1])
    is_ctx = h0 is None
    if is_ctx:
        h0 = jnp.zeros((B, 2, LRU_W), jnp.float32)
    hf = linear_scan(a_f, b_f, h0[:, 0].astype(jnp.float32), reverse=False)
    hb = linear_scan(a_b, b_b, h0[:, 1].astype(jnp.float32), reverse=True)
    y = ((hf + hb).astype(u.dtype) * jax.nn.gelu(xg)) @ w_out
    final = jnp.stack([hf[:, -1], hb[:, 0]], axis=1).astype(u.dtype) if is_ctx else None
    return y, final


def peer(x, wq, k1, k2, u_tab, v_tab):
    B, S, _ = x.shape
    T = B * S
    xt = x.reshape(T, D_MODEL)
    q = (xt @ wq).reshape(T, PEER_HEADS, 2, PEER_QD // 2)
    s1 = jnp.einsum('thd,nd->thn', q[:, :, 0], k1).astype(jnp.float32)
    s2 = jnp.einsum('thd,nd->thn', q[:, :, 1], k2).astype(jnp.float32)
    v1, i1 = lax.top_k(s1, PEER_TOPK)
    v2, i2 = lax.top_k(s2, PEER_TOPK)
    n_cand = PEER_TOPK * PEER_TOPK
    cand_s = (v1[..., :, None] + v2[..., None, :]).reshape(T, PEER_HEADS, n_cand)
    cand_i = (i1[..., :, None] * N_KEYS + i2[..., None, :]).reshape(T, PEER_HEADS, n_cand)
    top_s, pos = lax.top_k(cand_s, PEER_TOPK)
    idx = jnp.take_along_axis(cand_i, pos, axis=-1)
    g = jax.nn.softmax(top_s, axis=-1).astype(x.dtype)
    nb = T // PEER_BLOCK

    def expert_block(args):
        xb, ib, gb = args
        act = jax.nn.gelu(jnp.einsum('thkd,td->thk', u_tab[ib], xb))
        return jnp.einsum('thk,thkd->td', gb * act, v_tab[ib])

    out = lax.map(expert_block, (xt.reshape(nb, PEER_BLOCK, D_MODEL),
                                 idx.reshape(nb, PEER_BLOCK, PEER_HEADS, PEER_TOPK),
                                 g.reshape(nb, PEER_BLOCK, PEER_HEADS, PEER_TOPK)))
    return out.reshape(B, S, D_MODEL)


def run_trunk(x, cond, cache_k, cache_v, state_lru, W):
    is_ctx = cache_k is None
    new_k, new_v, new_s = [], [], []
    for layer in range(DEPTH):
        mod = (jax.nn.silu(cond) @ W['ada_w'][layer] + W['ada_b'][layer]).reshape(cond.shape[0], 6, D_MODEL)
        shift1, scale1, gate1, shift2, scale2, gate2 = [mod[:, None, j] for j in range(6)]
        u = x * (1 + scale1) + shift1
        if layer % 2 == 0:
            e = layer // 2
            ctx_kv = None if is_ctx else (cache_k[:, e], cache_v[:, e])
            y, kv = even_mixer(u, ctx_kv, W['even_w_in'][e], W['attn_sink'][e], W['pool_w'][e],
                               W['pool_scale'][e], W['even_w_out'][e])
            if is_ctx:
                new_k.append(kv[0])
                new_v.append(kv[1])
        else:
            o = layer // 2
            h0 = None if is_ctx else state_lru[:, o]
            y, st = odd_mixer(u, h0, W['odd_w_in'][o], W['conv_w'][o], W['conv_b'][o],
                              W['gate_a_w'][o], W['gate_a_b'][o], W['gate_x_w'][o], W['gate_x_b'][o],
                              W['lru_lambda'][o], W['odd_w_out'][o])
            if is_ctx:
                new_s.append(st)
        x = layer_norm(ALPHA * x + gate1 * y, W['ln1_g'][layer], W['ln1_b'][layer])
        u = x * (1 + scale2) + shift2
        y = peer(u, W['peer_wq'][layer], W['peer_k1'][layer], W['peer_k2'][layer],
                 W['peer_u'][layer], W['peer_v'][layer])
        x = layer_norm(ALPHA * x + gate2 * y, W['ln2_g'][layer], W['ln2_b'][layer])
    if is_ctx:
        return x, jnp.stack(new_k, axis=1), jnp.stack(new_v, axis=1), jnp.stack(new_s, axis=1)
    return x


def setup_inputs(seed: int = 0) -> dict:
    key = jax.random.key(seed)
    ks = iter(jax.random.split(key, 48))
    f32 = jnp.float32

    def nrm(shape, s):
        return jax.random.normal(next(ks), shape, f32) * s

    D = D_MODEL
    x_prompt = nrm((BATCH, SEQ, D), 1.0)
    x_sample = nrm((DEC_BATCH, DEC_SEQ, D), 1.0)
    cache_attn_k = nrm((DEC_BATCH, N_EVEN, PAST_LEN, N_KV_HEADS, HEAD_DIM), 1.0)
    cache_attn_v = nrm((DEC_BATCH, N_EVEN, PAST_LEN, N_KV_HEADS, HEAD_DIM), 1.0)
    state_lru = nrm((DEC_BATCH, N_ODD, 2, LRU_W), 0.5)
    c = nrm((DEC_BATCH, D), 1.0)
    c_ctx = nrm((D,), 1.0)
    ada_w = nrm((DEPTH, D, 6 * D), D ** -0.5)
    ada_b = nrm((DEPTH, 6 * D), 0.02)
    ln1_g = 1.0 + nrm((DEPTH, D), 0.05)
    ln1_b = nrm((DEPTH, D), 0.02)
    ln2_g = 1.0 + nrm((DEPTH, D), 0.05)
    ln2_b = nrm((DEPTH, D), 0.02)
    even_w_in = nrm((N_EVEN, D, EVEN_IN), D ** -0.5)
    attn_sink = nrm((N_EVEN, N_Q_HEADS), 0.5)
    pool_w = nrm((N_EVEN, N_POOL_GROUPS, POOL_GW, POOL_GW), POOL_GW ** -0.5)
    pool_scale = 1.0 + nrm((N_EVEN, POOL_W), 0.1)
    even_w_out = nrm((N_EVEN, EVEN_CAT, D), BETA * EVEN_CAT ** -0.5)
    odd_w_in = nrm((N_ODD, D, ODD_IN), D ** -0.5)
    conv_w = nrm((N_ODD, CONV_W, LRU_W), CONV_W ** -0.5)
    conv_b = nrm((N_ODD, LRU_W), 0.02)
    gate_a_w = nrm((N_ODD, 2, LRU_HEADS, LRU_HD, LRU_HD), LRU_HD ** -0.5)
    gate_a_b = nrm((N_ODD, 2, LRU_W), 0.02)
    gate_x_w = nrm((N_ODD, 2, LRU_HEADS, LRU_HD, LRU_HD), LRU_HD ** -0.5)
    gate_x_b = nrm((N_ODD, 2, LRU_W), 0.02)
    a8 = jax.random.uniform(next(ks), (N_ODD, 2, LRU_W), f32, 0.9, 0.999)
    s = a8 ** (1.0 / LRU_C)
    lru_lambda = jnp.log(s) - jnp.log1p(-s)
    odd_w_out = nrm((N_ODD, LRU_W, D), BETA * LRU_W ** -0.5)
    peer_wq = nrm((DEPTH, D, PEER_HEADS * PEER_QD), D ** -0.5)
    peer_k1 = nrm((DEPTH, N_KEYS, PEER_QD // 2), (PEER_QD // 2) ** -0.5)
    peer_k2 = nrm((DEPTH, N_KEYS, PEER_QD // 2), (PEER_QD // 2) ** -0.5)
    peer_u = nrm((DEPTH, N_EXPERTS, D), D ** -0.5)
    peer_v = nrm((DEPTH, N_EXPERTS, D), BETA * PEER_HEADS ** -0.5)
    return {'x_prompt': x_prompt, 'x_sample': x_sample, 'cache_attn_k': cache_attn_k,
            'cache_attn_v': cache_attn_v, 'state_lru': state_lru, 'c': c, 'c_ctx': c_ctx,
            'ada_w': ada_w, 'ada_b': ada_b, 'ln1_g': ln1_g, 'ln1_b': ln1_b, 'ln2_g': ln2_g,
            'ln2_b': ln2_b, 'even_w_in': even_w_in, 'attn_sink': attn_sink, 'pool_w': pool_w,
            'pool_scale': pool_scale, 'even_w_out': even_w_out, 'odd_w_in': odd_w_in,
            'conv_w': conv_w, 'conv_b': conv_b, 'gate_a_w': gate_a_w, 'gate_a_b': gate_a_b,
            'gate_x_w': gate_x_w, 'gate_x_b': gate_x_b, 'lru_lambda': lru_lambda,
            'odd_w_out': odd_w_out, 'peer_wq': peer_wq, 'peer_k1': peer_k1, 'peer_k2': peer_k2,
            'peer_u': peer_u, 'peer_v': peer_v}


def reference(x_prompt, x_sample, cache_attn_k, cache_attn_v, state_lru, c, c_ctx,
              ada_w, ada_b, ln1_g, ln1_b, ln2_g, ln2_b, even_w_in, attn_sink, pool_w,
              pool_scale, even_w_out, odd_w_in, conv_w, conv_b, gate_a_w, gate_a_b,
              gate_x_w, gate_x_b, lru_lambda, odd_w_out, peer_wq, peer_k1, peer_k2,
              peer_u, peer_v):
    W = dict(ada_w=ada_w, ada_b=ada_b, ln1_g=ln1_g, ln1_b=ln1_b, ln2_g=ln2_g, ln2_b=ln2_b,
             even_w_in=even_w_in, attn_sink=attn_sink, pool_w=pool_w, pool_scale=pool_scale,
             even_w_out=even_w_out, odd_w_in=odd_w_in, conv_w=conv_w, conv_b=conv_b,
             gate_a_w=gate_a_w, gate_a_b=gate_a_b, gate_x_w=gate_x_w, gate_x_b=gate_x_b,
             lru_lambda=lru_lambda, odd_w_out=odd_w_out, peer_wq=peer_wq, peer_k1=peer_k1,
             peer_k2=peer_k2, peer_u=peer_u, peer_v=peer_v)
    y_prompt, new_attn_k, new_attn_v, new_state_lru = run_trunk(
        x_prompt, c_ctx[None, :], None, None, None, W)
    y_sample = run_trunk(x_sample, c, cache_attn_k, cache_attn_v, state_lru, W)
    return (y_prompt, y_sample, new_attn_k, new_attn_v, new_state_lru)
```

```python
import math
from contextlib import ExitStack
import numpy as np
import concourse.bass as bass
import concourse.mybir as mybir
from concourse.bass_utils import run_bass_kernel_spmd

F32 = mybir.dt.float32
BF16 = mybir.dt.bfloat16
U32 = mybir.dt.uint32
I32 = mybir.dt.int32
AF = mybir.ActivationFunctionType
ALU = mybir.AluOpType
AX = mybir.AxisListType

D = 1024
KC = 8
NT = 20
NTS = 16
ALPHA = 4 ** 0.25
LN_EPS = 1e-5
NEG = -1.0e30


class Buf:
    __slots__ = ("lw", "rd", "name", "excl")

    def __init__(self, name="", excl=False):
        self.lw = None
        self.rd = {}
        self.name = name
        self.excl = excl


class T:
    def __init__(self, t, name="", excl=False, buf=None):
        self.t = t
        self.b = buf if buf is not None else Buf(name, excl)

    def __getitem__(self, k):
        return self.t[k]


class Eng:
    def __init__(self, name, handle, sem):
        self.name = name
        self.h = handle
        self.sem = sem
        self.count = 0
        self.waited = {}


class KB:
    def __init__(self):
        self.nc = bass.Bass("TRN2", target_bir_lowering=False)
        nc = self.nc
        self.eng = {
            "pe": Eng("pe", nc.tensor, nc.alloc_semaphore("sem_pe")),
            "act": Eng("act", nc.scalar, nc.alloc_semaphore("sem_act")),
            "dve": Eng("dve", nc.vector, nc.alloc_semaphore("sem_dve")),
            "pool": Eng("pool", nc.gpsimd, nc.alloc_semaphore("sem_pool")),
            "sp": Eng("sp", nc.sync, nc.alloc_semaphore("sem_sp")),
        }
        self.semid = {}
        for e in self.eng.values():
            self.semid[id(e.sem)] = e.sem
        self.dsem = {"hw": [[nc.alloc_semaphore(f"sem_dmah{i}"), 0] for i in range(24)],
                     "sw": [[nc.alloc_semaphore(f"sem_dmas{i}"), 0] for i in range(24)]}
        self.dnext = {"hw": 0, "sw": 0}
        self.nalloc = 0

    def sb(self, shape, dtype, name=None, st=None):
        self.nalloc += 1
        nm = f"{name or 't'}_{self.nalloc}"
        if st is not None:
            return T(st.enter_context(self.nc.sbuf_tensor(nm, list(shape), dtype)), nm)
        return T(self.nc.alloc_sbuf_tensor(nm, list(shape), dtype), nm)

    def dram(self, name, shape, dtype, kind):
        return T(self.nc.dram_tensor(name, list(shape), dtype, kind=kind).ap(), name)

    def _collect(self, e, r, w, is_dma):
        deps = {}

        def need(tok, raw):
            if tok is None:
                return
            sem, val, src = tok
            if (not is_dma) and src == e and not raw:
                return
            k = id(sem)
            if deps.get(k, (None, 0))[1] < val:
                deps[k] = (sem, val)

        for t in r:
            need(t.b.lw, True)
            if t.b.excl:
                for tok in t.b.rd.values():
                    need(tok, False)
        for t in w:
            need(t.b.lw, False)
            for tok in t.b.rd.values():
                need(tok, False)
        return deps

    def _wait(self, E, deps):
        for k, (sem, val) in deps.items():
            if E.waited.get(k, 0) < val:
                E.h.wait_ge(sem, val)
                E.waited[k] = val

    def _update(self, tok, r, w):
        for t in w:
            t.b.lw = tok
            t.b.rd = {}
        for t in r:
            k = id(tok[0])
            t.b.rd[k] = tok

    def op(self, e, fn, r=(), w=()):
        E = self.eng[e]
        self._wait(E, self._collect(e, r, w, False))
        inst = fn()
        inst.then_inc(E.sem, 1)
        E.count += 1
        tok = (E.sem, E.count, e)
        self._update(tok, r, w)
        return tok

    def dma(self, q, out, in_, r=(), w=(), **kw):
        Q = self.eng[q]
        self._wait(Q, self._collect(q, r, w, True))
        pool = "sw" if q == "pool" else "hw"
        ent = self.dsem[pool][self.dnext[pool]]
        self.dnext[pool] = (self.dnext[pool] + 1) % len(self.dsem[pool])
        sem, cnt = ent
        if cnt > 0 and Q.waited.get(id(sem), 0) < 16 * cnt:
            Q.h.wait_ge(sem, 16 * cnt)
            Q.waited[id(sem)] = 16 * cnt
        Q.h.dma_start(out=out, in_=in_, **kw).then_inc(sem, 16)
        ent[1] = cnt + 1
        tok = (sem, 16 * (cnt + 1), "dma")
        self._update(tok, r, w)
        return tok

    def barrier(self, engines=None):
        names = engines or list(self.eng.keys())
        for n in names:
            E = self.eng[n]
            deps = {}
            for F in self.eng.values():
                if F is not E and F.count > 0:
                    deps[id(F.sem)] = (F.sem, F.count)
            for sem, cnt in self.dsem["hw"] + self.dsem["sw"]:
                if cnt > 0:
                    deps[id(sem)] = (sem, 16 * cnt)
            self._wait(E, deps)


class Arena:
    def __init__(self, k, words):
        self.k = k
        self.words = words
        self.t = k.nc.alloc_sbuf_tensor("arena", [128, words], F32)
        self.off = 0
        self.peak = 0

    def mark(self):
        return self.off

    def release(self, m):
        self.k.barrier()
        self.off = m

    def alloc(self, shape, dtype, name=""):
        p = shape[0]
        n = int(np.prod(shape[1:]))
        w = n if dtype in (F32, U32, I32) else (n + 1) // 2
        w = (w + 15) // 16 * 16
        assert self.off + w <= self.words, f"arena overflow allocating {name} {shape}: {self.off}+{w}>{self.words}"
        base = self.t[0:p, self.off:self.off + w]
        if dtype != F32:
            base = base.bitcast(dtype)
        v = base[:, 0:n]
        if len(shape) == 3:
            v = v.rearrange("p (a b) -> p a b", b=shape[2])
        elif len(shape) == 4:
            v = v.rearrange("p (a b c) -> p a b c", b=shape[2], c=shape[3])
        self.off += w
        self.peak = max(self.peak, self.off)
        return T(v, name)


def _consts():
    c = {}
    c["c_ident"] = np.eye(128, dtype=np.float32)
    Rm = np.zeros((128, 128), np.float32)
    for i in range(128):
        if (i % 64) < 32:
            Rm[i + 32, i] = -1.0
        else:
            Rm[i - 32, i] = 1.0
    c["c_rot"] = Rm
    S = 2048
    rows = S // 64
    row = np.repeat(np.arange(rows, dtype=np.float32), 64)
    col = (np.arange(S) % 64).astype(np.float32)
    nf = 16
    inv = (np.float32(10000.0) ** (-np.arange(nf, dtype=np.float32) / nf)).astype(np.float32)
    ang = np.concatenate([row[:, None] * inv, col[:, None] * inv], axis=-1).astype(np.float32)
    cs = np.cos(ang).astype(np.float32)
    sn = np.sin(ang).astype(np.float32)
    c["c_cos"] = np.ascontiguousarray(np.tile(cs.T, (4, 1)))
    c["c_sin"] = np.ascontiguousarray(np.tile(sn.T, (4, 1)))
    p = np.arange(128)[:, None]
    f = np.arange(128)[None, :]
    c["c_mge"] = (p >= f).astype(np.float32)
    c["c_mle"] = (p <= f).astype(np.float32)
    edge = np.ones((4, 16), np.float32)
    Sx = 64
    for g, w in enumerate((2, 4, 8, 16)):
        lo = w // 2
        hi = w - lo - 1
        for t in range(8):
            cnt = min(t + hi + 1, Sx) - max(t - lo, 0)
            edge[g, t] = w / cnt
            tt = Sx - 8 + t
            cnt = min(tt + hi + 1, Sx) - max(tt - lo, 0)
            edge[g, 8 + t] = w / cnt
    c["c_edge"] = np.ascontiguousarray(np.broadcast_to(edge.reshape(1, 64), (128, 64)))
    c["c_iota"] = np.ascontiguousarray(np.broadcast_to(np.arange(128, dtype=np.float32)[None], (128, 128)))
    return c


def build(stop_after=None):
    k = KB()
    nc = k.nc

    def din(name, shape, dtype=F32):
        return k.dram(name, shape, dtype, "ExternalInput")

    def dout(name, shape, dtype=F32):
        return k.dram(name, shape, dtype, "ExternalOutput")

    x_in = din("x", [NT * 128, D])
    ck_in = din("ck", [256, 128])
    cv_in = din("cv", [256, 128])
    h0_in = din("h0", [128, 2, 8])
    condT_in = din("condT", [128, KC, 2])
    ada_w = din("ada_w", [2, D, 6 * D])
    ada_b = din("ada_b", [2, 6 * D])
    ada_bT_in = din("ada_bT", [2, 128, 48])
    ln1_g = din("ln1_g", [2, D]); ln1_b = din("ln1_b", [2, D])
    ln2_g = din("ln2_g", [2, D]); ln2_b = din("ln2_b", [2, D])
    w_in_e = din("w_in_e", [D, 1280])
    sink_in = din("sink", [1, 8])
    pool_w = din("pool_w", [4, 128, 128])
    pool_sc = din("pool_sc", [128, 4])
    w_out_e = din("w_out_e", [D, D])
    w_in_o = din("w_in_o", [D, 2 * D])
    conv_w = din("conv_w", [128, 8, 4])
    conv_b = din("conv_b", [128, 8])
    ga_w = din("ga_w", [2, 8, 128, 128]); ga_b = din("ga_b", [128, 2, 8])
    gx_w = din("gx_w", [2, 8, 128, 128]); gx_b = din("gx_b", [128, 2, 8])
    lam_in = din("lam", [128, 2, 8])
    w_out_o = din("w_out_o", [D, D])
    wq_in = din("wq", [2, D, 2048])
    k1_in = din("k1T", [2, 128, 128]); k2_in = din("k2T", [2, 128, 128])
    ut_in = din("peer_ut", [2, 128, 128, KC * 128])
    v_in = din("peer_v", [2, 16384, D])
    cst = {n: din(n, list(a.shape)) for n, a in _consts().items()}

    y_out = dout("y", [NT * 128, D])
    nk_out = dout("new_k", [512, 128])
    nv_out = dout("new_v", [512, 128])
    ns_out = dout("new_s", [2, 2, D])

    modscr = k.dram("modscr", [2, 2, 6 * D], F32, "Internal")
    X1 = k.dram("X1", [NT * 128, D], F32, "Internal")
    X2 = k.dram("X2", [NT * 128, D], F32, "Internal")
    UTb = k.dram("UTb", [128, 128, KC * 128], BF16, "Internal")
    Vb = k.dram("Vb", [16384, D], BF16, "Internal")

    ps = [T(nc.alloc_psum_tensor(f"ps{i}", [128, 512], F32), f"ps{i}", True) for i in range(7)]
    ps7 = T(nc.alloc_psum_tensor("ps7", [128, 512], F32), "ps7", True)
    psb = T(ps7.t[:, :].bitcast(BF16), "psb", buf=ps7.b)

    identF = k.sb([128, 128], F32, "identF")
    identB = k.sb([128, 128], BF16, "identB")
    iotaF = k.sb([128, 128], F32, "iotaF")
    iotaB = k.sb([128, 128], BF16, "iotaB")
    epsT = k.sb([128, 1], F32, "epsT")
    oneT = k.sb([128, 1], F32, "oneT")
    fm = [k.sb([128, 48, 2], F32, f"fm{l}") for l in range(2)]
    fm1p = [k.sb([128, 48, 2], F32, f"fm1p{l}") for l in range(2)]
    stats = k.sb([128, 12], F32, "stats"); mv = k.sb([128, 2], F32, "mv"); rstd = k.sb([128, 1], F32, "rstd")
    stT = k.sb([128, 2, 2, 8], F32, "stT")
    words = (nc.sbuf_bytes_remaining - 2048) // 4
    A = Arena(k, words)

    k.dma("sp", identF[:], cst["c_ident"][:], w=[identF])
    k.dma("sp", iotaF[:], cst["c_iota"][:], w=[iotaF])
    k.op("dve", lambda: nc.vector.tensor_copy(out=identB[:], in_=identF[:]), r=[identF], w=[identB])
    k.op("dve", lambda: nc.vector.tensor_copy(out=iotaB[:], in_=iotaF[:]), r=[iotaF], w=[iotaB])
    k.op("dve", lambda: nc.vector.memset(epsT[:], LN_EPS), w=[epsT])
    k.op("dve", lambda: nc.vector.memset(oneT[:], 1.0), w=[oneT])

    m0 = A.mark()
    condT = A.alloc([128, KC, 2], F32, "condT")
    scT = A.alloc([128, KC, 2], F32, "scT")
    k.dma("sp", condT[:], condT_in[:], w=[condT])
    k.op("act", lambda: nc.scalar.activation(out=scT[:], in_=condT[:], func=AF.Silu), r=[condT], w=[scT])
    adabT = A.alloc([128, 2, 48], F32, "adabT")
    k.dma("sp", adabT[:], ada_bT_in[:].rearrange("l p c -> p l c"), w=[adabT])
    awt = [A.alloc([128, KC, 512], F32, f"awt{i}") for i in range(3)]
    mtt = [A.alloc([48, 128], F32, f"mtt{i}") for i in range(2)]
    it = 0
    for l in range(2):
        aw_v = ada_w[l].rearrange("(kk p) n -> p kk n", p=128)
        for blk in range(12):
            wt = awt[it % 3]
            pf = ps[it % 2]
            k.dma("sp", wt[:], aw_v[:, :, blk * 512:(blk + 1) * 512], w=[wt])
            for q4 in range(4):
                for kk in range(KC):
                    k.op("pe", lambda kk=kk, wt=wt, pf=pf, q4=q4: nc.tensor.matmul(
                        pf[:, q4 * 2:q4 * 2 + 2], lhsT=wt[:, kk, q4 * 128:(q4 + 1) * 128], rhs=scT[:, kk, :],
                        start=(kk == 0), stop=(kk == KC - 1)), r=[scT, wt], w=[pf])
            k.op("dve", lambda pf=pf, l=l, blk=blk: nc.vector.tensor_tensor(
                out=fm[l][:, blk * 4:(blk + 1) * 4, :], in0=pf[:, 0:8].rearrange("p (a b) -> p a b", b=2),
                in1=adabT[:, l, blk * 4:(blk + 1) * 4].unsqueeze(2).to_broadcast([128, 4, 2]), op=ALU.add),
                r=[pf, adabT], w=[fm[l]])
            it += 1
        k.op("dve", lambda l=l: nc.vector.tensor_scalar(
            out=fm1p[l][:], in0=fm[l][:], scalar1=1.0, scalar2=None, op0=ALU.add), r=[fm[l]], w=[fm1p[l]])
        for c in range(2):
            pt_ = ps[2 + c]
            k.op("pe", lambda l=l, c=c, pt_=pt_: nc.tensor.transpose(out=pt_[0:48, 0:128], in_=fm[l][:, :, c], identity=identF[:]),
                 r=[fm[l], identF], w=[pt_])
            k.op("dve", lambda c=c, pt_=pt_: nc.vector.tensor_copy(out=mtt[c][:], in_=pt_[0:48, 0:128]), r=[pt_], w=[mtt[c]])
            k.dma("sp", modscr[l, c].rearrange("(ch p) -> ch p", p=128), mtt[c][:], r=[mtt[c]], w=[modscr])
    A.release(m0)

    if stop_after == "mod":
        return _finish(k, y_out, [(fm[0], fm[0][:].rearrange("p a b -> p (a b)"), 0)])

    rr = {"ev": 0}

    def evac(out_ap, in_ap, r, w):
        rr["ev"] += 1
        if rr["ev"] % 2 == 0:
            k.op("act", lambda: nc.scalar.activation(out=out_ap, in_=in_ap, func=AF.Copy), r=r, w=w)
        else:
            k.op("dve", lambda: nc.vector.tensor_copy(out=out_ap, in_=in_ap), r=r, w=w)

    def layer_norm_tile(r_t, g_bc, b_bc, out_t, em=None):
        em = em or k
        for hh in range(2):
            em.op("dve", lambda hh=hh: nc.vector.bn_stats(out=stats[:, hh * 6:(hh + 1) * 6], in_=r_t[:, hh * 512:(hh + 1) * 512]),
                 r=[r_t], w=[stats])
        em.op("dve", lambda: nc.vector.bn_aggr(out=mv[:], in_=stats[:]), r=[stats], w=[mv])
        em.op("act", lambda: nc.scalar.activation(out=rstd[:], in_=mv[:, 1:2], func=AF.Sqrt, bias=epsT[:, 0:1], scale=1.0),
             r=[mv, epsT], w=[rstd])
        em.op("dve", lambda: nc.vector.reciprocal(out=rstd[:], in_=rstd[:]), r=[rstd], w=[rstd])
        em.op("dve", lambda: nc.vector.tensor_scalar(out=r_t[:], in0=r_t[:], scalar1=mv[:, 0:1], scalar2=rstd[:, 0:1],
                                                    op0=ALU.subtract, op1=ALU.mult), r=[r_t, mv, rstd], w=[r_t])
        em.op("pool", lambda: nc.gpsimd.tensor_tensor(out=r_t[:], in0=r_t[:], in1=g_bc[:], op=ALU.mult), r=[r_t, g_bc], w=[r_t])
        em.op("dve", lambda: nc.vector.tensor_tensor(out=out_t[:], in0=r_t[:], in1=b_bc[:], op=ALU.add), r=[r_t, b_bc], w=[out_t])

    def load_transpose_mod(l, jsh, jsc, c, x_src, row0, xt, uT, col0, pp):
        k.dma("sp", xt[:], x_src[row0:row0 + 128, :], r=[x_src], w=[xt])
        for kk in range(KC):
            pk = pp[kk // 4]
            k.op("pe", lambda kk=kk, pk=pk: nc.tensor.transpose(
                out=pk[:, (kk % 4) * 128:(kk % 4 + 1) * 128], in_=xt[:, kk * 128:(kk + 1) * 128],
                identity=identF[:]), r=[xt, identF], w=[pk])
        for kk in range(KC):
            pk = pp[kk // 4]
            k.op("act", lambda kk=kk, pk=pk: nc.scalar.activation(
                out=uT[:, kk, col0:col0 + 128], in_=pk[:, (kk % 4) * 128:(kk % 4 + 1) * 128],
                func=AF.Identity, scale=fm1p[l][:, jsc * 8 + kk, c:c + 1], bias=fm[l][:, jsh * 8 + kk, c:c + 1]),
                r=[pk, fm1p[l], fm[l]], w=[uT])

    def outproj_ln1(l, catT, w_out_dram, gate_bc, t0, ntl, x_src, dst):
        mo = A.mark()
        w_outB = A.alloc([128, KC, D], BF16, "w_outB")
        k.dma("pool", w_outB[:], w_out_dram[:].rearrange("(kk p) n -> p kk n", p=128), w=[w_outB])
        lng = A.alloc([128, D], F32, "lng"); lnb = A.alloc([128, D], F32, "lnb")
        k.dma("sp", lng[:], ln1_g[l:l + 1, :].partition_broadcast(128), w=[lng])
        k.dma("sp", lnb[:], ln1_b[l:l + 1, :].partition_broadcast(128), w=[lnb])
        xts = [A.alloc([128, D], F32, f"oxt{i}") for i in range(2)]
        rts = [A.alloc([128, D], F32, f"ort{i}") for i in range(2)]
        for t in range(ntl):
            tg = t0 + t
            xt = xts[t % 2]; rt = rts[t % 2]
            pp = (ps[0], ps[1]) if t % 2 == 0 else (ps[2], ps[3])
            k.dma("sp", xt[:], x_src[tg * 128:(tg + 1) * 128, :], r=[x_src], w=[xt])
            for hh in range(2):
                for kk in range(KC):
                    k.op("pe", lambda kk=kk, hh=hh, t=t: nc.tensor.matmul(
                        pp[hh][:, :], lhsT=catT[:, kk, t * 128:(t + 1) * 128], rhs=w_outB[:, kk, hh * 512:(hh + 1) * 512],
                        start=(kk == 0), stop=(kk == KC - 1)), r=[catT, w_outB], w=[pp[hh]])
            for hh in range(2):
                k.op("dve", lambda hh=hh: nc.vector.tensor_tensor(
                    out=rt[:, hh * 512:(hh + 1) * 512], in0=pp[hh][:, :], in1=gate_bc[:, hh * 512:(hh + 1) * 512], op=ALU.mult),
                    r=[pp[hh], gate_bc], w=[rt])
            k.op("dve", lambda: nc.vector.scalar_tensor_tensor(
                out=rt[:], in0=xt[:], scalar=ALPHA, in1=rt[:], op0=ALU.mult, op1=ALU.add), r=[xt, rt], w=[rt])
            layer_norm_tile(rt, lng, lnb, xt)
            k.dma("sp", dst[tg * 128:(tg + 1) * 128, :], xt[:], r=[xt], w=[dst])
        A.release(mo)

    SEQS = [(0, 16, 0, False), (16, 2, 1, True), (18, 2, 1, True)]

    def even_mixer_phase(l, x_src, dst):
        m_ph = A.mark()
        w_inB = A.alloc([128, KC, 1280], BF16, "w_inB")
        k.dma("pool", w_inB[:], w_in_e[:].rearrange("(kk p) n -> p kk n", p=128), w=[w_inB])
        rotB = A.alloc([128, 128], BF16, "rotB")
        k.dma("pool", rotB[:], cst["c_rot"][:], w=[rotB])
        mge = A.alloc([128, 128], BF16, "mge"); mle = A.alloc([128, 128], BF16, "mle")
        k.dma("pool", mge[:], cst["c_mge"][:], w=[mge])
        k.dma("pool", mle[:], cst["c_mle"][:], w=[mle])
        esink = A.alloc([128, 8], F32, "esink")
        k.dma("sp", esink[:], sink_in[0:1, :].partition_broadcast(128), w=[esink])
        k.op("act", lambda: nc.scalar.activation(out=esink[:], in_=esink[:], func=AF.Exp), r=[esink], w=[esink])
        poolwB = A.alloc([128, 4, 128], BF16, "poolwB")
        k.dma("pool", poolwB[:], pool_w[:].rearrange("g c d -> c g d"), w=[poolwB])
        poolsc = A.alloc([128, 4], F32, "poolsc")
        k.dma("sp", poolsc[:], pool_sc[:], w=[poolsc])
        edge = A.alloc([128, 64], F32, "edge")
        k.dma("sp", edge[:], cst["c_edge"][:], w=[edge])
        for si_, (t0, ntl, c, is_ctx) in enumerate(SEQS):
            S = ntl * 128
            TB = min(S, 512)
            ntb = S // TB
            m_seq = A.mark()
            catT = A.alloc([128, KC, S], BF16, "catT")
            gate_bc = A.alloc([128, D], F32, "gate_bc")
            k.dma("sp", gate_bc[:], modscr[l, c:c + 1, 2 * D:3 * D].partition_broadcast(128), r=[modscr], w=[gate_bc])
            m_mid = A.mark()
            qfT = A.alloc([128, 4, S], BF16, "qfT")
            kfT = A.alloc([128, S], BF16, "kfT")
            nslot = ntl + (0 if is_ctx else 2)
            vaug = A.alloc([128, nslot, 2, 65], BF16, "vaug")
            k.op("pool", lambda: nc.gpsimd.memset(vaug[:], 1.0), w=[vaug])
            pT = A.alloc([128, 4, S + 16], F32, "pT")
            k.op("pool", lambda: nc.gpsimd.memset(pT[:], 0.0), w=[pT])
            ckT = A.alloc([128, 256], BF16, "ckT")
            m_in = A.mark()
            if is_ctx:
                qT, kT = qfT, kfT
            else:
                qT = A.alloc([128, 4, S], BF16, "qT")
                kT = A.alloc([128, S], BF16, "kT")
            kvs = [A.alloc([128, 256], F32, f"kvs{i}") for i in range(2)]
            m_u = A.mark()
            uT = A.alloc([128, KC, S], BF16, "uT")
            xts = [A.alloc([128, D], F32, f"xt{i}") for i in range(2)]
            for t in range(ntl):
                pp = (ps[0], ps[1]) if t % 2 == 0 else (ps[2], ps[3])
                load_transpose_mod(l, 0, 1, c, x_src, (t0 + t) * 128, xts[t % 2], uT, t * 128, pp)
            if stop_after == f"E1@{si_}":
                return _finish(k, y_out, [(uT, uT[:, 0, 0:min(S, 1024)], 0)])
            col_tiles = [("q", j, j * 128) for j in range(4)] + [("k", 0, 512)] + [("p", g, 768 + g * 128) for g in range(4)]
            pi = 0
            for (kind, idx, c0) in col_tiles:
                for tb in range(ntb):
                    pk = ps[4 + pi % 3]
                    pi += 1
                    for kk in range(KC):
                        k.op("pe", lambda kk=kk, pk=pk, c0=c0, tb=tb: nc.tensor.matmul(
                            pk[:, 0:TB], lhsT=w_inB[:, kk, c0:c0 + 128], rhs=uT[:, kk, tb * TB:(tb + 1) * TB],
                            start=(kk == 0), stop=(kk == KC - 1)), r=[w_inB, uT], w=[pk])
                    if kind == "q":
                        evac(qT[:, idx, tb * TB:(tb + 1) * TB], pk[:, 0:TB], [pk], [qT])
                    elif kind == "k":
                        evac(kT[:, tb * TB:(tb + 1) * TB], pk[:, 0:TB], [pk], [kT])
                    else:
                        evac(pT[:, idx, 8 + tb * TB:8 + (tb + 1) * TB], pk[:, 0:TB], [pk], [pT])
            if stop_after == f"E2a@{si_}":
                return _finish(k, y_out, [(qT, qT[:, 0, 0:min(S, 1024)], 0)])
            for t in range(ntl):
                pk = ps[4 + pi % 3]
                pi += 1
                for kk in range(KC):
                    k.op("pe", lambda kk=kk, pk=pk, t=t: nc.tensor.matmul(
                        pk[:, 0:256], lhsT=uT[:, kk, t * 128:(t + 1) * 128], rhs=w_inB[:, kk, 512:768],
                        start=(kk == 0), stop=(kk == KC - 1)), r=[w_inB, uT], w=[pk])
                k.op("dve", lambda pk=pk, t=t: nc.vector.tensor_copy(
                    out=vaug[:, t, :, 0:64], in_=pk[:, 128:256].rearrange("p (g d) -> p g d", g=2)), r=[pk], w=[vaug])
                if is_ctx:
                    kv = kvs[t % 2]
                    k.op("act", lambda pk=pk, kv=kv: nc.scalar.activation(out=kv[:], in_=pk[:, 0:256], func=AF.Copy), r=[pk], w=[kv])
                    row0 = (t0 - 16 + t) * 128
                    k.dma("sp", nk_out[row0:row0 + 128, :], kv[:, 0:128], r=[kv], w=[nk_out])
                    k.dma("sp", nv_out[row0:row0 + 128, :], kv[:, 128:256], r=[kv], w=[nv_out])
            A.release(m_u)
            if not is_ctx:
                for blk in range(2):
                    ckt = kvs[blk]
                    k.dma("sp", ckt[:, 0:128], ck_in[blk * 128:(blk + 1) * 128, :], w=[ckt])
                    k.dma("sp", ckt[:, 128:256], cv_in[blk * 128:(blk + 1) * 128, :], w=[ckt])
                    pk = ps[4 + pi % 3]
                    pi += 1
                    k.op("pe", lambda pk=pk, ckt=ckt: nc.tensor.transpose(out=pk[:, 0:128], in_=ckt[:, 0:128], identity=identF[:]),
                         r=[ckt, identF], w=[pk])
                    evac(ckT[:, blk * 128:(blk + 1) * 128], pk[:, 0:128], [pk], [ckT])
                    k.op("dve", lambda ckt=ckt, blk=blk: nc.vector.tensor_copy(
                        out=vaug[:, ntl + blk, :, 0:64], in_=ckt[:, 128:256].rearrange("p (g d) -> p g d", g=2)), r=[ckt], w=[vaug])
                cosb = [A.alloc([128, TB], F32, f"cosb{i}") for i in range(2)]
                sinb = [A.alloc([128, TB], F32, f"sinb{i}") for i in range(2)]
                tmp1 = [A.alloc([128, TB], F32, f"rt1_{i}") for i in range(2)]
                tmp2 = [A.alloc([128, TB], F32, f"rt2_{i}") for i in range(2)]
                ri = 0
                for tb in range(ntb):
                    cb = cosb[tb % 2]; sb_ = sinb[tb % 2]
                    k.dma("sp", cb[:], cst["c_cos"][:, tb * TB:(tb + 1) * TB], w=[cb])
                    k.dma("sp", sb_[:], cst["c_sin"][:, tb * TB:(tb + 1) * TB], w=[sb_])
                    for j in range(5):
                        src = qT[:, j, tb * TB:(tb + 1) * TB] if j < 4 else kT[:, tb * TB:(tb + 1) * TB]
                        dstq = qfT[:, j, tb * TB:(tb + 1) * TB] if j < 4 else kfT[:, tb * TB:(tb + 1) * TB]
                        srcT = qT if j < 4 else kT
                        dstT = qfT if j < 4 else kfT
                        pk = ps[4 + pi % 3]
                        pi += 1
                        t1 = tmp1[ri % 2]; t2 = tmp2[ri % 2]
                        ri += 1
                        k.op("pe", lambda pk=pk, src=src: nc.tensor.matmul(pk[:, 0:TB], lhsT=rotB[:], rhs=src, start=True, stop=True),
                             r=[rotB, srcT], w=[pk])
                        k.op("dve", lambda src=src, t1=t1, cb=cb: nc.vector.tensor_tensor(
                            out=t1[:], in0=src, in1=cb[:], op=ALU.mult), r=[srcT, cb], w=[t1])
                        k.op("dve", lambda pk=pk, t2=t2, sb_=sb_: nc.vector.tensor_tensor(
                            out=t2[:], in0=pk[:, 0:TB], in1=sb_[:], op=ALU.mult), r=[pk, sb_], w=[t2])
                        k.op("pool", lambda dstq=dstq, t1=t1, t2=t2: nc.gpsimd.tensor_tensor(out=dstq, in0=t1[:], in1=t2[:], op=ALU.add),
                             r=[t1, t2], w=[dstT])
            A.release(m_in)
            if stop_after == f"E2@{si_}":
                return _finish(k, y_out, [(qfT, qfT[:, 0, 0:min(S, 1024)], 0), (kfT, kfT[:, 0:min(S, 1024)], 1), (pT, pT[:, 0, 8:8 + min(S, 1024)], 2)])
            pexs = [A.alloc([128, 10, 512], BF16, f"pex{i}") for i in range(2)]
            attn_tm = [A.alloc([128, 512], BF16, f"attn_tm{i}") for i in range(2)]
            den = A.alloc([128, 2, 4], F32, "den")
            for n in range(ntl):
                if is_ctx:
                    kbs = [(kfT[:, kb * 128:(kb + 1) * 128], kfT, kb, None) for kb in range(ntl)]
                else:
                    kbs = []
                    for kb, msk in ((n - 1, mge), (n, None), (n + 1, mle)):
                        if 0 <= kb < ntl:
                            kbs.append((kfT[:, kb * 128:(kb + 1) * 128], kfT, kb, msk))
                    kbs.append((ckT[:, 0:128], ckT, ntl, None))
                    kbs.append((ckT[:, 128:256], ckT, ntl + 1, None))
                pex = pexs[n % 2]; atm = attn_tm[n % 2]
                si = 0
                for i, (kap, kbuf, slot, msk) in enumerate(kbs):
                    for g in range(2):
                        pk = ps[si % 2]
                        si += 1
                        k.op("pe", lambda pk=pk, kap=kap, g=g: nc.tensor.matmul(
                            pk[:, :].rearrange("p (j q) -> p j q", j=4), lhsT=kap[g * 64:(g + 1) * 64, :],
                            rhs=qfT[g * 64:(g + 1) * 64, :, n * 128:(n + 1) * 128], start=True, stop=True),
                            r=[kbuf, qfT], w=[pk])
                        k.op("act", lambda pk=pk, i=i, g=g: nc.scalar.activation(
                            out=pex[:, i * 2 + g, :], in_=pk[:, :], func=AF.Exp, scale=0.125), r=[pk], w=[pex])
                        if msk is not None:
                            k.op("pool", lambda i=i, g=g, msk=msk: nc.gpsimd.tensor_tensor(
                                out=pex[:, i * 2 + g, :].rearrange("p (j q) -> p j q", j=4),
                                in0=pex[:, i * 2 + g, :].rearrange("p (j q) -> p j q", j=4),
                                in1=msk[:].unsqueeze(1).to_broadcast([128, 4, 128]), op=ALU.mult), r=[pex, msk], w=[pex])
                for g in range(2):
                    po = ps[2 + g]
                    for j in range(4):
                        for i, (kap, kbuf, slot, msk) in enumerate(kbs):
                            k.op("pe", lambda po=po, j=j, i=i, g=g, slot=slot: nc.tensor.matmul(
                                po[:, j * 65:(j + 1) * 65], lhsT=pex[:, i * 2 + g, j * 128:(j + 1) * 128],
                                rhs=vaug[:, slot, g, :], start=(i == 0), stop=(i == len(kbs) - 1)),
                                r=[pex, vaug], w=[po])
                    pov = po[:, 0:260].rearrange("p (j e) -> p j e", e=65)
                    k.op("dve", lambda pov=pov, g=g: nc.vector.tensor_tensor(
                        out=den[:, g, :], in0=pov[:, :, 64], in1=esink[:, g * 4:(g + 1) * 4], op=ALU.add), r=[po, esink], w=[den])
                    k.op("dve", lambda g=g: nc.vector.reciprocal(out=den[:, g, :], in_=den[:, g, :]), r=[den], w=[den])
                    k.op("dve", lambda pov=pov, g=g, atm=atm: nc.vector.tensor_tensor(
                        out=atm[:, g * 256:(g + 1) * 256].rearrange("p (j d) -> p j d", j=4), in0=pov[:, :, 0:64],
                        in1=den[:, g, :].unsqueeze(2).to_broadcast([128, 4, 64]), op=ALU.mult), r=[po, den], w=[atm])
                for cc in range(4):
                    k.op("pe", lambda cc=cc, atm=atm: nc.tensor.transpose(
                        out=psb[:, cc * 128:(cc + 1) * 128], in_=atm[:, cc * 128:(cc + 1) * 128], identity=identB[:]),
                        r=[atm, identB], w=[psb])
                evac(catT[:, 0:4, n * 128:(n + 1) * 128], psb[:, 0:512].rearrange("p (c q) -> p c q", c=4), [psb], [catT])
            if stop_after == f"E4@{si_}":
                return _finish(k, y_out, [(catT, catT[:, cc_, 0:min(S, 1024)], cc_) for cc_ in range(4)])
            tA = A.alloc([128, S + 16], F32, "tA"); tB = A.alloc([128, S + 16], F32, "tB")
            pooledT = A.alloc([128, S], BF16, "pooledT")
            L = S + 16
            for g, wdw in enumerate((2, 4, 8, 16)):
                xg = pT[:, g, :]
                V = nc.vector
                k.op("dve", lambda xg=xg: V.tensor_tensor(out=tA[:, 1:L], in0=xg[:, 1:L], in1=xg[:, 0:L - 1], op=ALU.add), r=[pT], w=[tA])
                sres = tA
                if g >= 1:
                    k.op("dve", lambda: V.tensor_tensor(out=tB[:, 2:L - 1], in0=tA[:, 1:L - 2], in1=tA[:, 3:L], op=ALU.add), r=[tA], w=[tB])
                    sres = tB
                if g >= 2:
                    k.op("dve", lambda: V.tensor_tensor(out=tA[:, 4:L - 3], in0=tB[:, 2:L - 5], in1=tB[:, 6:L - 1], op=ALU.add), r=[tB], w=[tA])
                    sres = tA
                if g >= 3:
                    k.op("dve", lambda: V.tensor_tensor(out=tB[:, 8:8 + S], in0=tA[:, 4:4 + S], in1=tA[:, 12:12 + S], op=ALU.add), r=[tA], w=[tB])
                    sres = tB
                oth = tB if sres is tA else tA
                k.op("dve", lambda sres=sres, oth=oth, wdw=wdw: V.tensor_scalar(
                    out=oth[:, 8:8 + S], in0=sres[:, 8:8 + S], scalar1=1.0 / wdw, scalar2=None, op0=ALU.mult), r=[sres], w=[oth])
                k.op("dve", lambda oth=oth, g=g: V.tensor_tensor(
                    out=oth[:, 8:16], in0=oth[:, 8:16], in1=edge[:, g * 16:g * 16 + 8], op=ALU.mult), r=[oth, edge], w=[oth])
                k.op("dve", lambda oth=oth, g=g: V.tensor_tensor(
                    out=oth[:, S:S + 8], in0=oth[:, S:S + 8], in1=edge[:, g * 16 + 8:g * 16 + 16], op=ALU.mult), r=[oth, edge], w=[oth])
                k.op("dve", lambda oth=oth, xg=xg: V.tensor_tensor(
                    out=pooledT[:], in0=oth[:, 8:8 + S], in1=xg[:, 8:8 + S], op=ALU.subtract), r=[oth, pT], w=[pooledT])
                for tb in range(ntb):
                    pk = ps[4 + pi % 3]
                    pi += 1
                    k.op("pe", lambda pk=pk, g=g, tb=tb: nc.tensor.matmul(
                        pk[:, 0:TB], lhsT=poolwB[:, g, :], rhs=pooledT[:, tb * TB:(tb + 1) * TB], start=True, stop=True),
                        r=[poolwB, pooledT], w=[pk])
                    k.op("act", lambda pk=pk, g=g, tb=tb: nc.scalar.activation(
                        out=catT[:, 4 + g, tb * TB:(tb + 1) * TB], in_=pk[:, 0:TB], func=AF.Identity, scale=poolsc[:, g:g + 1]),
                        r=[pk, poolsc], w=[catT])
            if stop_after == f"E5@{si_}":
                return _finish(k, y_out, [(catT, catT[:, cc_, 0:min(S, 1024)], cc_) for cc_ in range(8)])
            A.release(m_mid)
            outproj_ln1(l, catT, w_out_e, gate_bc, t0, ntl, x_src, dst)
            if stop_after == f"E6@{si_}":
                return _finish(k, y_out, [])
            A.release(m_seq)
        A.release(m_ph)
        return None

    r_ = even_mixer_phase(0, x_in, y_out if (stop_after == "x1_0" or str(stop_after).startswith("E6@")) else X1)
    if r_ is not None:
        return r_
    if stop_after == "x1_0":
        return _finish(k, y_out, [])

    CG = 2
    NSLOT = 3

    class Deferred:
        def __init__(self):
            self.q = []

        COST = {"dve": 0.40, "act": 0.30, "pe": 0.12, "pool": 0.5}

        def op(self, e, *a, **kw):
            self.q.append((self.COST.get(e, 0.3), lambda: k.op(e, *a, **kw)))

        def dma(self, *a, **kw):
            self.q.append((0.1, lambda: k.dma(*a, **kw)))

        def pull(self, budget):
            spent = 0.0
            while self.q and spent < budget:
                c_, fn = self.q.pop(0)
                fn()
                spent += c_

    def evac_em(em, out_ap, in_ap, r, w):
        em.op("dve", lambda: nc.vector.tensor_copy(out=out_ap, in_=in_ap), r=r, w=w)

    def peer_phase(l, src, dst, ngroups=10):
        V_ = nc.vector
        m_ph = A.mark()
        k1B = A.alloc([128, 128], BF16, "k1B"); k2B = A.alloc([128, 128], BF16, "k2B")
        k.dma("pool", k1B[:], k1_in[l], w=[k1B])
        k.dma("pool", k2B[:], k2_in[l], w=[k2B])
        kB = (k1B, k2B)
        ln2g = A.alloc([128, D], F32, "ln2g"); ln2b = A.alloc([128, D], F32, "ln2b")
        k.dma("sp", ln2g[:], ln2_g[l:l + 1, :].partition_broadcast(128), w=[ln2g])
        k.dma("sp", ln2b[:], ln2_b[l:l + 1, :].partition_broadcast(128), w=[ln2b])
        gate2 = [A.alloc([128, D], F32, f"gate2_{c}") for c in range(2)]
        for c in range(2):
            k.dma("sp", gate2[c][:], modscr[l, c:c + 1, 5 * D:6 * D].partition_broadcast(128), r=[modscr], w=[gate2[c]])
        wqB = A.alloc([128, KC, 2048], BF16, "wqB")
        wq_v = wq_in[l].rearrange("(kk p) n -> p kk n", p=128)
        for hh in range(2):
            k.dma("pool", wqB[:, :, hh * 1024:(hh + 1) * 1024], wq_v[:, :, hh * 1024:(hh + 1) * 1024], w=[wqB])
        Wall = A.alloc([128, 128, 256], BF16, "Wall")
        ub = [A.alloc([128, CG, 1024], BF16, f"ub{i}") for i in range(NSLOT)]
        vb = [A.alloc([128, CG, 1024], BF16, f"vb{i}") for i in range(NSLOT)]
        xg = A.alloc([128, D], F32, "xg")
        u2Ts = [A.alloc([128, KC, 256], BF16, f"u2T{i}") for i in range(2)]
        igTs = [A.alloc([128, 3, 256], F32, f"igT{i}") for i in range(2)]
        qT = A.alloc([128, 16, 256], BF16, "pqT")
        s_all = A.alloc([128, 16, 128], F32, "s_all")
        scrB = A.alloc([128, 2048], F32, "scrB")
        cand = A.alloc([128, 8, 256], F32, "cand")
        rg = [T(cand[:, 4 * j:4 * j + 4, :].rearrange("p h n -> p (h n)"), f"rg{j}", buf=cand.b) for j in range(2)]
        xre = [T(scrB[:, j * 1024:(j + 1) * 1024], f"xre{j}", buf=scrB.b) for j in range(2)]
        vals = A.alloc([128, 16, 16], F32, "vals")
        idxu = A.alloc([128, 16, 16], U32, "idxu")
        idxf = A.alloc([128, 16, 16], F32, "idxf")
        tv = A.alloc([128, 8, 16], F32, "tv")
        pos = A.alloc([128, 8, 16], U32, "pos")
        au = A.alloc([128, 8, 16], U32, "au"); bu = A.alloc([128, 8, 16], U32, "bu")
        af = A.alloc([128, 8, 16], F32, "af"); bf = A.alloc([128, 8, 16], F32, "bf")
        sel = A.alloc([128, 3, 128], F32, "sel")
        ssum = A.alloc([128, 8], F32, "ssum")
        Qb = [A.alloc([128, 4, 128], BF16, f"Qb{i}") for i in range(2)]
        Pb = [A.alloc([128, 4, 128], BF16, f"Pb{i}") for i in range(2)]
        Gs = [A.alloc([128, 256], F32, f"Gs{i}") for i in range(2)]
        Zs = [[A.alloc([128, 128], BF16, f"Zs{i}_{j}") for j in range(2)] for i in range(2)]
        swork = scrB[:, :].rearrange("p (m n) -> p m n", n=128)
        cwork = scrB[:, :].rearrange("p (h n) -> p h n", n=256)
        eq = scrB[:, :].rearrange("p (h a b) -> p h a b", a=16, b=16)
        cand4 = cand[:, :, :].rearrange("p h (a b) -> p h a b", b=16)
        vals4 = vals[:, :, :].rearrange("p (h s) a -> p h s a", s=2)
        idxf4 = idxf[:, :, :].rearrange("p (h s) a -> p h s a", s=2)
        iota4 = iotaF[:, 0:16].unsqueeze(1).unsqueeze(1).to_broadcast([128, 8, 16, 16])
        RB = (ps[6], ps7)

        def stream_load(cg, grp):
            sl = cg % NSLOT
            if grp == 0:
                k.dma("pool", ub[sl][:], ut_in[l, cg * CG:(cg + 1) * CG].rearrange("c p n -> p c n"), w=[ub[sl]])
                k.dma("pool", vb[sl][:], v_in[l, cg * CG * 128:(cg + 1) * CG * 128, :].rearrange("(c p) n -> p c n", p=128), w=[vb[sl]])
                k.dma("sp", UTb[cg * CG:(cg + 1) * CG].rearrange("c p n -> p c n"), ub[sl][:], r=[ub[sl]], w=[UTb])
                k.dma("sp", Vb[cg * CG * 128:(cg + 1) * CG * 128, :].rearrange("(c p) n -> p c n", p=128), vb[sl][:], r=[vb[sl]], w=[Vb])
            else:
                k.dma("sp", ub[sl][:], UTb[cg * CG:(cg + 1) * CG].rearrange("c p n -> p c n"), r=[UTb], w=[ub[sl]])
                k.dma("sp", vb[sl][:], Vb[cg * CG * 128:(cg + 1) * CG * 128, :].rearrange("(c p) n -> p c n", p=128), r=[Vb], w=[vb[sl]])

        def R_build(em, grp, par):
            c = 0 if grp < 8 else 1
            u2T = u2Ts[par]; igT = igTs[par]
            for j in range(2):
                row0 = (2 * grp + j) * 128
                em.dma("sp", xg[:], src[row0:row0 + 128, :], r=[src], w=[xg])
                for kk in range(KC):
                    pk = RB[kk // 4]
                    em.op("pe", lambda kk=kk, pk=pk: nc.tensor.transpose(
                        out=pk[:, (kk % 4) * 128:(kk % 4 + 1) * 128], in_=xg[:, kk * 128:(kk + 1) * 128],
                        identity=identF[:]), r=[xg, identF], w=[pk])
                for kk in range(KC):
                    pk = RB[kk // 4]
                    em.op("act", lambda kk=kk, pk=pk, j=j: nc.scalar.activation(
                        out=u2T[:, kk, j * 128:(j + 1) * 128], in_=pk[:, (kk % 4) * 128:(kk % 4 + 1) * 128],
                        func=AF.Identity, scale=fm1p[l][:, 4 * 8 + kk, c:c + 1], bias=fm[l][:, 3 * 8 + kk, c:c + 1]),
                        r=[pk, fm1p[l], fm[l]], w=[u2T])
            for m in range(16):
                pk = RB[m % 2]
                for kk in range(KC):
                    em.op("pe", lambda kk=kk, pk=pk, m=m: nc.tensor.matmul(
                        pk[:, 0:256], lhsT=wqB[:, kk, m * 128:(m + 1) * 128], rhs=u2T[:, kk, :],
                        start=(kk == 0), stop=(kk == KC - 1)), r=[wqB, u2T], w=[pk])
                evac_em(em, qT[:, m, :], pk[:, 0:256], [pk], [qT])
            for j in range(2):
                for mq in range(4):
                    pk = RB[mq % 2]
                    for mm in range(4):
                        m = mq * 4 + mm
                        em.op("pe", lambda pk=pk, m=m, mm=mm, j=j: nc.tensor.matmul(
                            pk[:, mm * 128:(mm + 1) * 128], lhsT=qT[:, m, j * 128:(j + 1) * 128], rhs=kB[m % 2][:],
                            start=True, stop=True), r=[qT, kB[m % 2]], w=[pk])
                    evac_em(em, s_all[:, mq * 4:(mq + 1) * 4, :], pk[:, :].rearrange("p (a b) -> p a b", b=128), [pk], [s_all])
                for m in range(16):
                    em.op("dve", lambda m=m: V_.max(out=vals[:, m, 0:8], in_=s_all[:, m, :]), r=[s_all], w=[vals])
                for m in range(16):
                    em.op("dve", lambda m=m: V_.max_index(out=idxu[:, m, 0:8], in_max=vals[:, m, 0:8], in_values=s_all[:, m, :]),
                          r=[s_all, vals], w=[idxu])
                for m in range(16):
                    em.op("dve", lambda m=m: V_.match_replace(out=swork[:, m, :], in_to_replace=vals[:, m, 0:8],
                                                              in_values=s_all[:, m, :], imm_value=NEG), r=[s_all, vals], w=[scrB])
                for m in range(16):
                    em.op("dve", lambda m=m: V_.max(out=vals[:, m, 8:16], in_=swork[:, m, :]), r=[scrB], w=[vals])
                for m in range(16):
                    em.op("dve", lambda m=m: V_.max_index(out=idxu[:, m, 8:16], in_max=vals[:, m, 8:16], in_values=swork[:, m, :]),
                          r=[scrB, vals], w=[idxu])
                em.op("dve", lambda: V_.tensor_tensor(
                    out=cand4, in0=vals4[:, :, 0, :].unsqueeze(3).to_broadcast([128, 8, 16, 16]),
                    in1=vals4[:, :, 1, :].unsqueeze(2).to_broadcast([128, 8, 16, 16]), op=ALU.add), r=[vals], w=[cand])
                for h in range(8):
                    em.op("dve", lambda h=h: V_.max(out=tv[:, h, 0:8], in_=cand[:, h, :]), r=[cand], w=[tv])
                for h in range(8):
                    em.op("dve", lambda h=h: V_.max_index(out=pos[:, h, 0:8], in_max=tv[:, h, 0:8], in_values=cand[:, h, :]),
                          r=[cand, tv], w=[pos])
                for h in range(8):
                    em.op("dve", lambda h=h: V_.match_replace(out=cwork[:, h, :], in_to_replace=tv[:, h, 0:8],
                                                              in_values=cand[:, h, :], imm_value=NEG), r=[cand, tv], w=[scrB])
                for h in range(8):
                    em.op("dve", lambda h=h: V_.max(out=tv[:, h, 8:16], in_=cwork[:, h, :]), r=[scrB], w=[tv])
                for h in range(8):
                    em.op("dve", lambda h=h: V_.max_index(out=pos[:, h, 8:16], in_max=tv[:, h, 8:16], in_values=cwork[:, h, :]),
                          r=[scrB, tv], w=[pos])
                em.op("dve", lambda: V_.tensor_single_scalar(out=au[:], in_=pos[:], scalar=4, op=ALU.logical_shift_right), r=[pos], w=[au])
                em.op("dve", lambda: V_.tensor_single_scalar(out=bu[:], in_=pos[:], scalar=15, op=ALU.bitwise_and), r=[pos], w=[bu])
                em.op("dve", lambda: V_.tensor_copy(out=af[:], in_=au[:]), r=[au], w=[af])
                em.op("dve", lambda: V_.tensor_copy(out=bf[:], in_=bu[:]), r=[bu], w=[bf])
                em.op("dve", lambda: V_.tensor_copy(out=idxf[:], in_=idxu[:]), r=[idxu], w=[idxf])
                for s_, abf in ((0, af), (1, bf)):
                    em.op("dve", lambda abf=abf: V_.tensor_tensor(
                        out=eq, in0=abf[:, :, :].unsqueeze(3).to_broadcast([128, 8, 16, 16]), in1=iota4, op=ALU.is_equal),
                        r=[abf, iotaF], w=[scrB])
                    em.op("dve", lambda s_=s_: V_.tensor_tensor(
                        out=eq, in0=eq, in1=idxf4[:, :, s_, :].unsqueeze(2).to_broadcast([128, 8, 16, 16]), op=ALU.mult),
                        r=[scrB, idxf], w=[scrB])
                    em.op("dve", lambda s_=s_: V_.tensor_reduce(
                        out=sel[:, s_, :].rearrange("p (h a) -> p h a", a=16), in_=eq, axis=AX.X, op=ALU.add), r=[scrB], w=[sel])
                selg = sel[:, 2, :].rearrange("p (h a) -> p h a", a=16)
                em.op("dve", lambda: V_.tensor_tensor(out=selg, in0=tv[:], in1=tv[:, :, 0:1].to_broadcast([128, 8, 16]), op=ALU.subtract),
                      r=[tv], w=[sel])
                em.op("act", lambda: nc.scalar.activation(out=selg, in_=selg, func=AF.Exp), r=[sel], w=[sel])
                em.op("dve", lambda: V_.tensor_reduce(out=ssum[:], in_=selg, axis=AX.X, op=ALU.add), r=[sel], w=[ssum])
                em.op("dve", lambda: V_.reciprocal(out=ssum[:], in_=ssum[:]), r=[ssum], w=[ssum])
                em.op("dve", lambda: V_.tensor_tensor(out=selg, in0=selg, in1=ssum[:].unsqueeze(2).to_broadcast([128, 8, 16]), op=ALU.mult),
                      r=[sel, ssum], w=[sel])
                for q3 in range(3):
                    em.op("pe", lambda q3=q3: nc.tensor.transpose(out=RB[0][:, q3 * 128:(q3 + 1) * 128], in_=sel[:, q3, :], identity=identF[:]),
                          r=[sel, identF], w=[RB[0]])
                evac_em(em, igT[:, :, j * 128:(j + 1) * 128], RB[0][:, 0:384].rearrange("p (q t) -> p q t", q=3), [RB[0]], [igT])

        def W_build(par, dqF=None):
            igT = igTs[par]
            for t4 in range(64):
                if dqF is not None and t4 >= 1:
                    dqF.pull(0.45)
                qb = Qb[t4 % 2]; pb_ = Pb[t4 % 2]; pw = ps[4 + t4 % 2]
                for tt in range(4):
                    t = t4 * 4 + tt
                    on_pool = False
                    e_ = "pool" if on_pool else "dve"
                    E_ = nc.gpsimd if on_pool else V_
                    k.op(e_, lambda tt=tt, t=t, E_=E_: E_.tensor_scalar(
                        out=qb[:, tt, :], in0=iotaB[:], scalar1=igT[:, 1, t:t + 1], scalar2=None, op0=ALU.is_equal),
                        r=[iotaB, igT], w=[qb])
                    k.op(e_, lambda tt=tt, t=t, E_=E_: E_.tensor_scalar(
                        out=pb_[:, tt, :], in0=iotaB[:], scalar1=igT[:, 0, t:t + 1], scalar2=igT[:, 2, t:t + 1],
                        op0=ALU.is_equal, op1=ALU.mult), r=[iotaB, igT], w=[pb_])
                for tt in range(4):
                    k.op("pe", lambda tt=tt: nc.tensor.matmul(
                        pw[:, tt * 128:(tt + 1) * 128], lhsT=qb[:, tt, :], rhs=pb_[:, tt, :], start=True, stop=True),
                        r=[qb, pb_], w=[pw])
                k.op("act", lambda t4=t4: nc.scalar.activation(
                    out=Wall[:, :, t4 * 4:(t4 + 1) * 4], in_=pw[:, :].rearrange("p (t i) -> p i t", t=4), func=AF.Copy),
                    r=[pw], w=[Wall])
            if dqF is not None:
                dqF.pull(1e9)

        def S_run(grp, par, dq):
            u2T = u2Ts[par]
            ncg = 128 // CG

            def u_side(c_):
                sl = (c_ // CG) % NSLOT; ci = c_ % CG
                pa = ps[4 + c_ % 2]
                for kk in range(KC):
                    k.op("pe", lambda kk=kk: nc.tensor.matmul(
                        pa[:, 0:256], lhsT=ub[sl][:, ci, kk * 128:(kk + 1) * 128], rhs=u2T[:, kk, :],
                        start=(kk == 0), stop=(kk == KC - 1)), r=[ub[sl], u2T], w=[pa])

            def mid(c_):
                pa = ps[4 + c_ % 2]; G = Gs[c_ % 2]; Z = Zs[c_ % 2]
                k.op("act", lambda: nc.scalar.activation(out=G[:], in_=pa[:, 0:256], func=AF.Gelu_apprx_tanh), r=[pa], w=[G])
                k.op("pool", lambda: nc.gpsimd.tensor_tensor(out=Z[0][:], in0=G[:, 0:128], in1=Wall[:, c_, 0:128], op=ALU.mult),
                     r=[G, Wall], w=[Z[0]])
                k.op("pool", lambda: nc.gpsimd.tensor_tensor(out=Z[1][:], in0=G[:, 128:256], in1=Wall[:, c_, 128:256], op=ALU.mult),
                     r=[G, Wall], w=[Z[1]])

            def v_side(c_):
                sl = (c_ // CG) % NSLOT; ci = c_ % CG
                Z = Zs[c_ % 2]
                for j in range(2):
                    for hh in range(2):
                        k.op("pe", lambda j=j, hh=hh: nc.tensor.matmul(
                            ps[j * 2 + hh][:, :], lhsT=Z[j][:], rhs=vb[sl][:, ci, hh * 512:(hh + 1) * 512],
                            start=(c_ == 0), stop=(c_ == 127)), r=[Z[j], vb[sl]], w=[ps[j * 2 + hh]])

            npull = (sum(c__ for c__, _ in dq.q) / 112.0) if dq is not None else 0
            u_side(0)
            mid(0)
            for c_ in range(128):
                if c_ % CG == 0 and c_ // CG + 2 < ncg:
                    stream_load(c_ // CG + 2, grp)
                if c_ + 1 < 128:
                    u_side(c_ + 1)
                v_side(c_)
                if c_ + 1 < 128:
                    mid(c_ + 1)
                if dq is not None:
                    dq.pull(npull)
            if dq is not None:
                dq.pull(1e9)

        def F_build(em, grp):
            c = 0 if grp < 8 else 1
            for j in range(2):
                row0 = (2 * grp + j) * 128
                em.dma("sp", xre[j][:], src[row0:row0 + 128, :], r=[src], w=[xre[j]])
            for j in range(2):
                row0 = (2 * grp + j) * 128
                for hh in range(2):
                    em.op("dve", lambda j=j, hh=hh: V_.tensor_tensor(
                        out=rg[j][:, hh * 512:(hh + 1) * 512], in0=ps[j * 2 + hh][:, :], in1=gate2[c][:, hh * 512:(hh + 1) * 512],
                        op=ALU.mult), r=[ps[j * 2 + hh], gate2[c]], w=[rg[j]])
                em.op("dve", lambda j=j: V_.scalar_tensor_tensor(
                    out=rg[j][:], in0=xre[j][:], scalar=ALPHA, in1=rg[j][:], op0=ALU.mult, op1=ALU.add), r=[xre[j], rg[j]], w=[rg[j]])
                layer_norm_tile(rg[j], ln2g, ln2b, xre[j], em=em)
                em.dma("sp", dst[row0:row0 + 128, :], xre[j][:], r=[xre[j]], w=[dst])

        R_build(k, 0, 0)
        dqF = None
        for grp in range(ngroups):
            par = grp % 2
            stream_load(0, grp)
            stream_load(1, grp)
            W_build(par, dqF)
            dq = None
            if grp + 1 < ngroups:
                dq = Deferred()
                R_build(dq, grp + 1, 1 - par)
            S_run(grp, par, dq)
            dqF = Deferred()
            F_build(dqF, grp)
        dqF.pull(1e9)
        A.release(m_ph)
        return None

    if stop_after == "x2_0g1":
        peer_phase(0, X1, y_out, ngroups=1)
        return _finish(k, y_out, [])
    peer_phase(0, X1, y_out if stop_after == "x2_0" else X2)
    if stop_after == "x2_0":
        return _finish(k, y_out, [])

    def odd_mixer_phase(l, x_src, dst):
        V_ = nc.vector
        m_ph = A.mark()
        w_inoB = A.alloc([128, KC, 2048], BF16, "w_inoB")
        wv = w_in_o[:].rearrange("(kk p) n -> p kk n", p=128)
        for hh in range(2):
            k.dma("pool", w_inoB[:, :, hh * 1024:(hh + 1) * 1024], wv[:, :, hh * 1024:(hh + 1) * 1024], w=[w_inoB])
        convw = A.alloc([128, 8, 4], F32, "convw"); convb = A.alloc([128, 8], F32, "convb")
        k.dma("sp", convw[:], conv_w[:], w=[convw])
        k.dma("sp", convb[:], conv_b[:], w=[convb])
        gaB = A.alloc([128, 2, 8, 128], BF16, "gaB"); gxB = A.alloc([128, 2, 8, 128], BF16, "gxB")
        for d_ in range(2):
            k.dma("pool", gaB[:, d_, :, :], ga_w[d_].rearrange("c i j -> i c j"), w=[gaB])
            k.dma("pool", gxB[:, d_, :, :], gx_w[d_].rearrange("c i j -> i c j"), w=[gxB])
        gab = A.alloc([128, 2, 8], F32, "gab"); gxb = A.alloc([128, 2, 8], F32, "gxb")
        lam = A.alloc([128, 2, 8], F32, "lam"); h0s = A.alloc([128, 2, 8], F32, "h0s")
        k.dma("sp", gab[:], ga_b[:], w=[gab]); k.dma("sp", gxb[:], gx_b[:], w=[gxb])
        k.dma("sp", lam[:], lam_in[:], w=[lam]); k.dma("sp", h0s[:], h0_in[:], w=[h0s])
        nsp8 = A.alloc([128, 2, 8], F32, "nsp8"); nsp16 = A.alloc([128, 2, 8], F32, "nsp16")
        k.op("act", lambda: nc.scalar.activation(out=lam[:], in_=lam[:], func=AF.Exp, scale=-1.0), r=[lam], w=[lam])
        k.op("act", lambda: nc.scalar.activation(out=lam[:], in_=lam[:], func=AF.Ln, bias=oneT[:, 0:1], scale=1.0), r=[lam, oneT], w=[lam])
        k.op("dve", lambda: V_.tensor_scalar(out=nsp8[:], in0=lam[:], scalar1=-8.0, scalar2=None, op0=ALU.mult), r=[lam], w=[nsp8])
        k.op("dve", lambda: V_.tensor_scalar(out=nsp16[:], in0=lam[:], scalar1=-16.0, scalar2=None, op0=ALU.mult), r=[lam], w=[nsp16])
        k.op("dve", lambda: V_.memset(stT[:], 0.0), w=[stT])
        for si_, (t0, ntl, c, is_ctx) in enumerate(SEQS):
            S = ntl * 128
            TB = min(S, 512)
            ntb = S // TB
            m_seq = A.mark()
            zT = A.alloc([128, KC, S], BF16, "zT")
            gate_bc = A.alloc([128, D], F32, "gate_bc")
            k.dma("sp", gate_bc[:], modscr[l, c:c + 1, 2 * D:3 * D].partition_broadcast(128), r=[modscr], w=[gate_bc])
            m_mid = A.mark()
            uT = A.alloc([128, KC, S], BF16, "uT")
            xts = [A.alloc([128, D], F32, f"xt{i}") for i in range(2)]
            for t in range(ntl):
                pp = (ps[0], ps[1]) if t % 2 == 0 else (ps[2], ps[3])
                load_transpose_mod(l, 0, 1, c, x_src, (t0 + t) * 128, xts[t % 2], uT, t * 128, pp)
            xr = A.alloc([128, S + 3], F32, "xr")
            xc = A.alloc([128, S], F32, "xc")
            xcb = A.alloc([128, S], BF16, "xcb")
            gg = A.alloc([128, S], F32, "gg")
            b1 = [A.alloc([128, S], F32, f"b1_{i}") for i in range(2)]
            b2 = A.alloc([128, S], F32, "b2")
            b3 = A.alloc([128, S], F32, "b3")
            k.op("pool", lambda: nc.gpsimd.memset(xr[:], 0.0), w=[xr])
            pi = 0
            for cf in range(8):
                for which, c0 in ((0, cf * 128), (1, 1024 + cf * 128)):
                    for tb in range(ntb):
                        pk = ps[4 + pi % 3]
                        pi += 1
                        for kk in range(KC):
                            k.op("pe", lambda kk=kk, pk=pk, c0=c0, tb=tb: nc.tensor.matmul(
                                pk[:, 0:TB], lhsT=w_inoB[:, kk, c0:c0 + 128], rhs=uT[:, kk, tb * TB:(tb + 1) * TB],
                                start=(kk == 0), stop=(kk == KC - 1)), r=[w_inoB, uT], w=[pk])
                        if which == 0:
                            evac(xr[:, 2 + tb * TB:2 + (tb + 1) * TB], pk[:, 0:TB], [pk], [xr])
                        else:
                            k.op("act", lambda pk=pk, tb=tb: nc.scalar.activation(
                                out=gg[:, tb * TB:(tb + 1) * TB], in_=pk[:, 0:TB], func=AF.Gelu_apprx_tanh), r=[pk], w=[gg])
                k.op("dve", lambda cf=cf: V_.tensor_scalar(out=xc[:], in0=xr[:, 0:S], scalar1=convw[:, cf, 0:1], scalar2=convb[:, cf:cf + 1],
                                                           op0=ALU.mult, op1=ALU.add), r=[xr, convw, convb], w=[xc])
                for jj in range(1, 4):
                    k.op("dve", lambda cf=cf, jj=jj: V_.scalar_tensor_tensor(
                        out=xc[:], in0=xr[:, jj:jj + S], scalar=convw[:, cf, jj:jj + 1], in1=xc[:], op0=ALU.mult, op1=ALU.add),
                        r=[xr, convw, xc], w=[xc])
                k.op("pool", lambda: nc.gpsimd.tensor_copy(out=xcb[:], in_=xc[:]), r=[xc], w=[xcb])
                for d_ in range(2):
                    H = b1[d_]
                    for gw, gb, dstb in ((gaB, gab, H), (gxB, gxb, b2)):
                        for tb in range(ntb):
                            pk = ps[pi % 4]
                            pi += 1
                            k.op("pe", lambda pk=pk, gw=gw, tb=tb, d_=d_, cf=cf: nc.tensor.matmul(
                                pk[:, 0:TB], lhsT=gw[:, d_, cf, :], rhs=xcb[:, tb * TB:(tb + 1) * TB], start=True, stop=True),
                                r=[gw, xcb], w=[pk])
                            k.op("act", lambda pk=pk, gb=gb, dstb=dstb, tb=tb, d_=d_, cf=cf: nc.scalar.activation(
                                out=dstb[:, tb * TB:(tb + 1) * TB], in_=pk[:, 0:TB], func=AF.Sigmoid, bias=gb[:, d_, cf:cf + 1], scale=1.0),
                                r=[pk, gb], w=[dstb])
                    k.op("act", lambda H=H, d_=d_, cf=cf: nc.scalar.activation(out=b3[:], in_=H[:], func=AF.Exp, scale=nsp8[:, d_, cf:cf + 1]),
                         r=[H, nsp8], w=[b3])
                    k.op("act", lambda H=H, d_=d_, cf=cf: nc.scalar.activation(out=H[:], in_=H[:], func=AF.Exp, scale=nsp16[:, d_, cf:cf + 1]),
                         r=[H, nsp16], w=[H])
                    k.op("act", lambda H=H: nc.scalar.activation(out=H[:], in_=H[:], func=AF.Sqrt, bias=oneT[:, 0:1], scale=-1.0),
                         r=[H, oneT], w=[H])
                    k.op("pool", lambda: nc.gpsimd.tensor_tensor(out=b2[:], in0=b2[:], in1=xc[:], op=ALU.mult), r=[b2, xc], w=[b2])
                    k.op("dve", lambda H=H: V_.tensor_tensor(out=b2[:], in0=b2[:], in1=H[:], op=ALU.mult), r=[b2, H], w=[b2])
                    if is_ctx:
                        init = 0.0
                        rinit = []
                    else:
                        init = h0s[:, d_, cf:cf + 1]
                        rinit = [h0s]
                    if d_ == 0:
                        k.op("dve", lambda H=H, init=init: V_.tensor_tensor_scan(
                            out=H[:], data0=b3[:], data1=b2[:], initial=init, op0=ALU.mult, op1=ALU.add), r=[b3, b2] + rinit, w=[H])
                        if is_ctx:
                            k.op("dve", lambda H=H, cf=cf, si_=si_: V_.tensor_copy(out=stT[:, si_ - 1, 0, cf:cf + 1], in_=H[:, S - 1:S]), r=[H], w=[stT])
                    else:
                        k.op("dve", lambda H=H, init=init: V_.tensor_tensor_scan(
                            out=H[:, ::-1], data0=b3[:, ::-1], data1=b2[:, ::-1], initial=init, op0=ALU.mult, op1=ALU.add),
                            r=[b3, b2] + rinit, w=[H])
                        if is_ctx:
                            k.op("dve", lambda H=H, cf=cf, si_=si_: V_.tensor_copy(out=stT[:, si_ - 1, 1, cf:cf + 1], in_=H[:, 0:1]), r=[H], w=[stT])
                k.op("pool", lambda: nc.gpsimd.tensor_tensor(out=b2[:], in0=b1[0][:], in1=b1[1][:], op=ALU.add), r=[b1[0], b1[1]], w=[b2])
                k.op("dve", lambda cf=cf: V_.tensor_tensor(out=zT[:, cf, :], in0=b2[:], in1=gg[:], op=ALU.mult), r=[b2, gg], w=[zT])
            A.release(m_mid)
            outproj_ln1(l, zT, w_out_o, gate_bc, t0, ntl, x_src, dst)
            A.release(m_seq)
        k.op("pe", lambda: nc.tensor.transpose(out=ps[4][0:32, 0:128], in_=stT[:].rearrange("p s d c -> p (s d c)"), identity=identF[:]),
             r=[stT, identF], w=[ps[4]])
        st_o = A.alloc([32, 128], F32, "st_o")
        k.op("dve", lambda: V_.tensor_copy(out=st_o[:], in_=ps[4][0:32, 0:128]), r=[ps[4]], w=[st_o])
        k.dma("sp", ns_out[:].rearrange("s d (c p) -> (s d c) p", p=128), st_o[:], r=[st_o], w=[ns_out])
        A.release(m_ph)

    odd_mixer_phase(1, X2, y_out if stop_after == "x1_1" else X1)
    if stop_after == "x1_1":
        return _finish(k, y_out, [])
    peer_phase(1, X1, y_out)
    return _finish(k, y_out, [])


def _finish(k, y_out, dbg):
    for t, ap, row in dbg:
        p, n = ap.shape
        q = "sp" if ap.dtype == F32 else "pool"
        k.dma(q, y_out[row * 128:row * 128 + p, 0:n], ap, r=[t], w=[y_out])
    k.barrier(["sp"])
    return k


Q_PERM = [0, 4, 1, 5, 2, 6, 3, 7]


def _shared(inp):
    f = lambda a: np.ascontiguousarray(np.asarray(a, dtype=np.float32))
    sh = {}
    sh["ada_w"] = f(inp["ada_w"]); sh["ada_b"] = f(inp["ada_b"])
    sh["ada_bT"] = f(np.asarray(inp["ada_b"]).reshape(2, 48, 128).transpose(0, 2, 1))
    for n in ("ln1_g", "ln1_b", "ln2_g", "ln2_b"):
        sh[n] = f(inp[n])
    wi = np.asarray(inp["even_w_in"][0], np.float32)
    qcols = np.concatenate([np.arange(h * 64, (h + 1) * 64) for h in Q_PERM])
    sh["w_in_e"] = f(np.concatenate([wi[:, qcols], wi[:, 512:]], axis=1))
    sh["sink"] = f(inp["attn_sink"]).reshape(1, 8)
    sh["pool_w"] = f(inp["pool_w"][0])
    sh["pool_sc"] = f(np.asarray(inp["pool_scale"][0]).reshape(4, 128).T)
    sh["w_out_e"] = f(inp["even_w_out"][0])
    sh["w_in_o"] = f(inp["odd_w_in"][0])
    sh["conv_w"] = f(np.asarray(inp["conv_w"][0]).reshape(4, 8, 128).transpose(2, 1, 0))
    sh["conv_b"] = f(np.asarray(inp["conv_b"][0]).reshape(8, 128).T)
    sh["ga_w"] = f(inp["gate_a_w"][0]); sh["gx_w"] = f(inp["gate_x_w"][0])
    sh["ga_b"] = f(np.asarray(inp["gate_a_b"][0]).reshape(2, 8, 128).transpose(2, 0, 1))
    sh["gx_b"] = f(np.asarray(inp["gate_x_b"][0]).reshape(2, 8, 128).transpose(2, 0, 1))
    sh["lam"] = f(np.asarray(inp["lru_lambda"][0]).reshape(2, 8, 128).transpose(2, 0, 1))
    sh["w_out_o"] = f(inp["odd_w_out"][0])
    sh["wq"] = f(inp["peer_wq"])
    sh["k1T"] = f(np.asarray(inp["peer_k1"]).transpose(0, 2, 1))
    sh["k2T"] = f(np.asarray(inp["peer_k2"]).transpose(0, 2, 1))
    U = np.asarray(inp["peer_u"], np.float32)
    sh["peer_ut"] = f(U.reshape(2, 128, 128, 8, 128).transpose(0, 1, 4, 3, 2).reshape(2, 128, 128, 1024))
    sh["peer_v"] = f(inp["peer_v"])
    sh.update(_consts())
    return sh


def _percore(inp, b):
    f = lambda a: np.ascontiguousarray(np.asarray(a, dtype=np.float32))
    m = {}
    m["x"] = f(np.concatenate([inp["x_sample"][b], inp["x_prompt"][2 * b], inp["x_prompt"][2 * b + 1]], axis=0))
    m["ck"] = f(np.asarray(inp["cache_attn_k"][b, 0]).reshape(256, 128))
    m["cv"] = f(np.asarray(inp["cache_attn_v"][b, 0]).reshape(256, 128))
    m["h0"] = f(np.asarray(inp["state_lru"][b, 0]).reshape(2, 8, 128).transpose(2, 0, 1))
    cond = np.stack([np.asarray(inp["c"][b]), np.asarray(inp["c_ctx"])], 0).astype(np.float32)
    m["condT"] = f(cond.reshape(2, 8, 128).transpose(2, 1, 0))
    return m


def kernel(**inputs):
    nk = build()
    sh = _shared(inputs)
    in_maps = []
    for b in range(8):
        m = dict(sh)
        m.update(_percore(inputs, b))
        in_maps.append(m)
    res = run_bass_kernel_spmd(nk.nc, in_maps, core_ids=list(range(8)))
    y_prompt = np.zeros((16, 256, D), np.float32)
    y_sample = np.zeros((8, 2048, D), np.float32)
    new_k = np.zeros((16, 1, 256, 2, 64), np.float32)
    new_v = np.zeros((16, 1, 256, 2, 64), np.float32)
    new_s = np.zeros((16, 1, 2, D), np.float32)
    for b in range(8):
        r = res.results[b]
        y = r["y"]
        y_sample[b] = y[0:2048]
        y_prompt[2 * b] = y[2048:2304]
        y_prompt[2 * b + 1] = y[2304:2560]
        new_k[2 * b, 0] = r["new_k"][0:256].reshape(256, 2, 64)
        new_k[2 * b + 1, 0] = r["new_k"][256:512].reshape(256, 2, 64)
        new_v[2 * b, 0] = r["new_v"][0:256].reshape(256, 2, 64)
        new_v[2 * b + 1, 0] = r["new_v"][256:512].reshape(256, 2, 64)
        new_s[2 * b, 0] = r["new_s"][0]
        new_s[2 * b + 1, 0] = r["new_s"][1]
    return (y_prompt, y_sample, new_k, new_v, new_s)
```

```python
import math
from contextlib import ExitStack
import numpy as np
import concourse.bass as bass
import concourse.mybir as mybir
from concourse.bass_utils import run_bass_kernel_spmd

F32 = mybir.dt.float32
BF16 = mybir.dt.bfloat16
U32 = mybir.dt.uint32
I32 = mybir.dt.int32
AF = mybir.ActivationFunctionType
ALU = mybir.AluOpType
AX = mybir.AxisListType

D = 1024
KC = 8
NT = 20
NTS = 16
ALPHA = 4 ** 0.25
LN_EPS = 1e-5
NEG = -1.0e30


class Buf:
    __slots__ = ("lw", "rd", "name", "excl")

    def __init__(self, name="", excl=False):
        self.lw = None
        self.rd = {}
        self.name = name
        self.excl = excl


class T:
    def __init__(self, t, name="", excl=False, buf=None):
        self.t = t
        self.b = buf if buf is not None else Buf(name, excl)

    def __getitem__(self, k):
        return self.t[k]


class Eng:
    def __init__(self, name, handle, sem):
        self.name = name
        self.h = handle
        self.sem = sem
        self.count = 0
        self.waited = {}


class KB:
    def __init__(self):
        self.nc = bass.Bass("TRN2", target_bir_lowering=False)
        nc = self.nc
        self.eng = {
            "pe": Eng("pe", nc.tensor, nc.alloc_semaphore("sem_pe")),
            "act": Eng("act", nc.scalar, nc.alloc_semaphore("sem_act")),
            "dve": Eng("dve", nc.vector, nc.alloc_semaphore("sem_dve")),
            "pool": Eng("pool", nc.gpsimd, nc.alloc_semaphore("sem_pool")),
            "sp": Eng("sp", nc.sync, nc.alloc_semaphore("sem_sp")),
        }
        self.semid = {}
        for e in self.eng.values():
            self.semid[id(e.sem)] = e.sem
        self.dsem = {"hw": [[nc.alloc_semaphore(f"sem_dmah{i}"), 0] for i in range(24)],
                     "sw": [[nc.alloc_semaphore(f"sem_dmas{i}"), 0] for i in range(24)]}
        self.dnext = {"hw": 0, "sw": 0}
        self.nalloc = 0

    def sb(self, shape, dtype, name=None, st=None):
        self.nalloc += 1
        nm = f"{name or 't'}_{self.nalloc}"
        if st is not None:
            return T(st.enter_context(self.nc.sbuf_tensor(nm, list(shape), dtype)), nm)
        return T(self.nc.alloc_sbuf_tensor(nm, list(shape), dtype), nm)

    def dram(self, name, shape, dtype, kind):
        return T(self.nc.dram_tensor(name, list(shape), dtype, kind=kind).ap(), name)

    def _collect(self, e, r, w, is_dma):
        deps = {}

        def need(tok, raw):
            if tok is None:
                return
            sem, val, src = tok
            if (not is_dma) and src == e and not raw:
                return
            k = id(sem)
            if deps.get(k, (None, 0))[1] < val:
                deps[k] = (sem, val)

        for t in r:
            need(t.b.lw, True)
            if t.b.excl:
                for tok in t.b.rd.values():
                    need(tok, False)
        for t in w:
            need(t.b.lw, False)
            for tok in t.b.rd.values():
                need(tok, False)
        return deps

    def _wait(self, E, deps):
        for k, (sem, val) in deps.items():
            if E.waited.get(k, 0) < val:
                E.h.wait_ge(sem, val)
                E.waited[k] = val

    def _update(self, tok, r, w):
        for t in w:
            t.b.lw = tok
            t.b.rd = {}
        for t in r:
            k = id(tok[0])
            t.b.rd[k] = tok

    def op(self, e, fn, r=(), w=()):
        E = self.eng[e]
        self._wait(E, self._collect(e, r, w, False))
        inst = fn()
        inst.then_inc(E.sem, 1)
        E.count += 1
        tok = (E.sem, E.count, e)
        self._update(tok, r, w)
        return tok

    def dma(self, q, out, in_, r=(), w=(), **kw):
        Q = self.eng[q]
        self._wait(Q, self._collect(q, r, w, True))
        pool = "sw" if q == "pool" else "hw"
        ent = self.dsem[pool][self.dnext[pool]]
        self.dnext[pool] = (self.dnext[pool] + 1) % len(self.dsem[pool])
        sem, cnt = ent
        if cnt > 0 and Q.waited.get(id(sem), 0) < 16 * cnt:
            Q.h.wait_ge(sem, 16 * cnt)
            Q.waited[id(sem)] = 16 * cnt
        Q.h.dma_start(out=out, in_=in_, **kw).then_inc(sem, 16)
        ent[1] = cnt + 1
        tok = (sem, 16 * (cnt + 1), "dma")
        self._update(tok, r, w)
        return tok

    def barrier(self, engines=None):
        names = engines or list(self.eng.keys())
        for n in names:
            E = self.eng[n]
            deps = {}
            for F in self.eng.values():
                if F is not E and F.count > 0:
                    deps[id(F.sem)] = (F.sem, F.count)
            for sem, cnt in self.dsem["hw"] + self.dsem["sw"]:
                if cnt > 0:
                    deps[id(sem)] = (sem, 16 * cnt)
            self._wait(E, deps)


class Arena:
    def __init__(self, k, words):
        self.k = k
        self.words = words
        self.t = k.nc.alloc_sbuf_tensor("arena", [128, words], F32)
        self.off = 0
        self.peak = 0

    def mark(self):
        return self.off

    def release(self, m):
        self.k.barrier()
        self.off = m

    def alloc(self, shape, dtype, name=""):
        p = shape[0]
        n = int(np.prod(shape[1:]))
        w = n if dtype in (F32, U32, I32) else (n + 1) // 2
        w = (w + 15) // 16 * 16
        assert self.off + w <= self.words, f"arena overflow allocating {name} {shape}: {self.off}+{w}>{self.words}"
        base = self.t[0:p, self.off:self.off + w]
        if dtype != F32:
            base = base.bitcast(dtype)
        v = base[:, 0:n]
        if len(shape) == 3:
            v = v.rearrange("p (a b) -> p a b", b=shape[2])
        elif len(shape) == 4:
            v = v.rearrange("p (a b c) -> p a b c", b=shape[2], c=shape[3])
        self.off += w
        self.peak = max(self.peak, self.off)
        return T(v, name)


def _consts():
    c = {}
    c["c_ident"] = np.eye(128, dtype=np.float32)
    Rm = np.zeros((128, 128), np.float32)
    for i in range(128):
        if (i % 64) < 32:
            Rm[i + 32, i] = -1.0
        else:
            Rm[i - 32, i] = 1.0
    c["c_rot"] = Rm
    S = 2048
    rows = S // 64
    row = np.repeat(np.arange(rows, dtype=np.float32), 64)
    col = (np.arange(S) % 64).astype(np.float32)
    nf = 16
    inv = (np.float32(10000.0) ** (-np.arange(nf, dtype=np.float32) / nf)).astype(np.float32)
    ang = np.concatenate([row[:, None] * inv, col[:, None] * inv], axis=-1).astype(np.float32)
    cs = np.cos(ang).astype(np.float32)
    sn = np.sin(ang).astype(np.float32)
    c["c_cos"] = np.ascontiguousarray(np.tile(cs.T, (4, 1)))
    c["c_sin"] = np.ascontiguousarray(np.tile(sn.T, (4, 1)))
    p = np.arange(128)[:, None]
    f = np.arange(128)[None, :]
    c["c_mge"] = (p >= f).astype(np.float32)
    c["c_mle"] = (p <= f).astype(np.float32)
    edge = np.ones((4, 16), np.float32)
    Sx = 64
    for g, w in enumerate((2, 4, 8, 16)):
        lo = w // 2
        hi = w - lo - 1
        for t in range(8):
            cnt = min(t + hi + 1, Sx) - max(t - lo, 0)
            edge[g, t] = w / cnt
            tt = Sx - 8 + t
            cnt = min(tt + hi + 1, Sx) - max(tt - lo, 0)
            edge[g, 8 + t] = w / cnt
    c["c_edge"] = np.ascontiguousarray(np.broadcast_to(edge.reshape(1, 64), (128, 64)))
    c["c_iota"] = np.ascontiguousarray(np.broadcast_to(np.arange(128, dtype=np.float32)[None], (128, 128)))
    return c


def build(stop_after=None):
    k = KB()
    nc = k.nc

    def din(name, shape, dtype=F32):
        return k.dram(name, shape, dtype, "ExternalInput")

    def dout(name, shape, dtype=F32):
        return k.dram(name, shape, dtype, "ExternalOutput")

    x_in = din("x", [NT * 128, D])
    ck_in = din("ck", [256, 128])
    cv_in = din("cv", [256, 128])
    h0_in = din("h0", [128, 2, 8])
    condT_in = din("condT", [128, KC, 2])
    ada_w = din("ada_w", [2, D, 6 * D])
    ada_b = din("ada_b", [2, 6 * D])
    ada_bT_in = din("ada_bT", [2, 128, 48])
    ln1_g = din("ln1_g", [2, D]); ln1_b = din("ln1_b", [2, D])
    ln2_g = din("ln2_g", [2, D]); ln2_b = din("ln2_b", [2, D])
    w_in_e = din("w_in_e", [D, 1280])
    sink_in = din("sink", [1, 8])
    pool_w = din("pool_w", [4, 128, 128])
    pool_sc = din("pool_sc", [128, 4])
    w_out_e = din("w_out_e", [D, D])
    w_in_o = din("w_in_o", [D, 2 * D])
    conv_w = din("conv_w", [128, 8, 4])
    conv_b = din("conv_b", [128, 8])
    ga_w = din("ga_w", [2, 8, 128, 128]); ga_b = din("ga_b", [128, 2, 8])
    gx_w = din("gx_w", [2, 8, 128, 128]); gx_b = din("gx_b", [128, 2, 8])
    lam_in = din("lam", [128, 2, 8])
    w_out_o = din("w_out_o", [D, D])
    wq_in = din("wq", [2, D, 2048])
    k1_in = din("k1T", [2, 128, 128]); k2_in = din("k2T", [2, 128, 128])
    ut_in = din("peer_ut", [2, 128, 128, KC * 128])
    v_in = din("peer_v", [2, 16384, D])
    cst = {n: din(n, list(a.shape)) for n, a in _consts().items()}

    y_out = dout("y", [NT * 128, D])
    nk_out = dout("new_k", [512, 128])
    nv_out = dout("new_v", [512, 128])
    ns_out = dout("new_s", [2, 2, D])

    modscr = k.dram("modscr", [2, 2, 6 * D], F32, "Internal")
    X1 = k.dram("X1", [NT * 128, D], F32, "Internal")
    X2 = k.dram("X2", [NT * 128, D], F32, "Internal")
    UTb = k.dram("UTb", [128, 128, KC * 128], BF16, "Internal")
    Vb = k.dram("Vb", [16384, D], BF16, "Internal")

    ps = [T(nc.alloc_psum_tensor(f"ps{i}", [128, 512], F32), f"ps{i}", True) for i in range(7)]
    ps7 = T(nc.alloc_psum_tensor("ps7", [128, 512], F32), "ps7", True)
    psb = T(ps7.t[:, :].bitcast(BF16), "psb", buf=ps7.b)

    identF = k.sb([128, 128], F32, "identF")
    identB = k.sb([128, 128], BF16, "identB")
    iotaF = k.sb([128, 128], F32, "iotaF")
    iotaB = k.sb([128, 128], BF16, "iotaB")
    epsT = k.sb([128, 1], F32, "epsT")
    oneT = k.sb([128, 1], F32, "oneT")
    fm = [k.sb([128, 48, 2], F32, f"fm{l}") for l in range(2)]
    fm1p = [k.sb([128, 48, 2], F32, f"fm1p{l}") for l in range(2)]
    stats = k.sb([128, 12], F32, "stats"); mv = k.sb([128, 2], F32, "mv"); rstd = k.sb([128, 1], F32, "rstd")
    stT = k.sb([128, 2, 2, 8], F32, "stT")
    sub = [k.sb([128, 2, 1024], BF16, f"sub{i}") for i in range(2)]
    svb = [k.sb([128, 2, 1024], BF16, f"svb{i}") for i in range(2)]
    words = (nc.sbuf_bytes_remaining - 2048) // 4
    A = Arena(k, words)

    k.dma("sp", identF[:], cst["c_ident"][:], w=[identF])
    k.dma("sp", iotaF[:], cst["c_iota"][:], w=[iotaF])
    k.op("dve", lambda: nc.vector.tensor_copy(out=identB[:], in_=identF[:]), r=[identF], w=[identB])
    k.op("dve", lambda: nc.vector.tensor_copy(out=iotaB[:], in_=iotaF[:]), r=[iotaF], w=[iotaB])
    k.op("dve", lambda: nc.vector.memset(epsT[:], LN_EPS), w=[epsT])
    k.op("dve", lambda: nc.vector.memset(oneT[:], 1.0), w=[oneT])

    m0 = A.mark()
    condT = A.alloc([128, KC, 2], F32, "condT")
    scT = A.alloc([128, KC, 2], F32, "scT")
    k.dma("sp", condT[:], condT_in[:], w=[condT])
    k.op("act", lambda: nc.scalar.activation(out=scT[:], in_=condT[:], func=AF.Silu), r=[condT], w=[scT])
    adabT = A.alloc([128, 2, 48], F32, "adabT")
    k.dma("sp", adabT[:], ada_bT_in[:].rearrange("l p c -> p l c"), w=[adabT])
    awt = [A.alloc([128, KC, 512], F32, f"awt{i}") for i in range(3)]
    mtt = [A.alloc([48, 128], F32, f"mtt{i}") for i in range(2)]
    it = 0
    for l in range(2):
        aw_v = ada_w[l].rearrange("(kk p) n -> p kk n", p=128)
        for blk in range(12):
            wt = awt[it % 3]
            pf = ps[it % 2]
            k.dma("sp", wt[:], aw_v[:, :, blk * 512:(blk + 1) * 512], w=[wt])
            for q4 in range(4):
                for kk in range(KC):
                    k.op("pe", lambda kk=kk, wt=wt, pf=pf, q4=q4: nc.tensor.matmul(
                        pf[:, q4 * 2:q4 * 2 + 2], lhsT=wt[:, kk, q4 * 128:(q4 + 1) * 128], rhs=scT[:, kk, :],
                        start=(kk == 0), stop=(kk == KC - 1)), r=[scT, wt], w=[pf])
            k.op("dve", lambda pf=pf, l=l, blk=blk: nc.vector.tensor_tensor(
                out=fm[l][:, blk * 4:(blk + 1) * 4, :], in0=pf[:, 0:8].rearrange("p (a b) -> p a b", b=2),
                in1=adabT[:, l, blk * 4:(blk + 1) * 4].unsqueeze(2).to_broadcast([128, 4, 2]), op=ALU.add),
                r=[pf, adabT], w=[fm[l]])
            it += 1
        k.op("dve", lambda l=l: nc.vector.tensor_scalar(
            out=fm1p[l][:], in0=fm[l][:], scalar1=1.0, scalar2=None, op0=ALU.add), r=[fm[l]], w=[fm1p[l]])
        for c in range(2):
            pt_ = ps[2 + c]
            k.op("pe", lambda l=l, c=c, pt_=pt_: nc.tensor.transpose(out=pt_[0:48, 0:128], in_=fm[l][:, :, c], identity=identF[:]),
                 r=[fm[l], identF], w=[pt_])
            k.op("dve", lambda c=c, pt_=pt_: nc.vector.tensor_copy(out=mtt[c][:], in_=pt_[0:48, 0:128]), r=[pt_], w=[mtt[c]])
            k.dma("sp", modscr[l, c].rearrange("(ch p) -> ch p", p=128), mtt[c][:], r=[mtt[c]], w=[modscr])
    A.release(m0)

    if stop_after == "mod":
        return _finish(k, y_out, [(fm[0], fm[0][:].rearrange("p a b -> p (a b)"), 0)])

    class BgConv:
        def __init__(self, l):
            self.l = l
            self.cg = 0
            self.wb = 0

        def step(self, n=1):
            l = self.l
            for _ in range(n):
                if self.wb < self.cg and (self.cg - self.wb >= 2 or self.cg >= 64):
                    cg = self.wb; sl = cg % 2
                    k.dma("sp", UTb[cg * 2:(cg + 1) * 2].rearrange("c p n -> p c n"), sub[sl][:], r=[sub[sl]], w=[UTb])
                    k.dma("sp", Vb[cg * 256:(cg + 1) * 256, :].rearrange("(c p) n -> p c n", p=128), svb[sl][:], r=[svb[sl]], w=[Vb])
                    self.wb += 1
                if self.cg < 64:
                    cg = self.cg; sl = cg % 2
                    k.dma("pool", sub[sl][:], ut_in[l, cg * 2:(cg + 1) * 2].rearrange("c p n -> p c n"), w=[sub[sl]])
                    k.dma("pool", svb[sl][:], v_in[l, cg * 256:(cg + 1) * 256, :].rearrange("(c p) n -> p c n", p=128), w=[svb[sl]])
                    self.cg += 1

        def finish(self):
            while self.wb < 64:
                self.step(1)

    bg = {"c": None}

    def bg_step(n=1):
        if bg["c"] is not None:
            bg["c"].step(n)

    rr = {"ev": 0}

    def evac(out_ap, in_ap, r, w):
        rr["ev"] += 1
        if rr["ev"] % 2 == 0:
            k.op("act", lambda: nc.scalar.activation(out=out_ap, in_=in_ap, func=AF.Copy), r=r, w=w)
        else:
            k.op("dve", lambda: nc.vector.tensor_copy(out=out_ap, in_=in_ap), r=r, w=w)

    def layer_norm_tile(r_t, g_bc, b_bc, out_t, em=None):
        em = em or k
        for hh in range(2):
            em.op("dve", lambda hh=hh: nc.vector.bn_stats(out=stats[:, hh * 6:(hh + 1) * 6], in_=r_t[:, hh * 512:(hh + 1) * 512]),
                 r=[r_t], w=[stats])
        em.op("dve", lambda: nc.vector.bn_aggr(out=mv[:], in_=stats[:]), r=[stats], w=[mv])
        em.op("act", lambda: nc.scalar.activation(out=rstd[:], in_=mv[:, 1:2], func=AF.Sqrt, bias=epsT[:, 0:1], scale=1.0),
             r=[mv, epsT], w=[rstd])
        em.op("dve", lambda: nc.vector.reciprocal(out=rstd[:], in_=rstd[:]), r=[rstd], w=[rstd])
        em.op("dve", lambda: nc.vector.tensor_scalar(out=r_t[:], in0=r_t[:], scalar1=mv[:, 0:1], scalar2=rstd[:, 0:1],
                                                    op0=ALU.subtract, op1=ALU.mult), r=[r_t, mv, rstd], w=[r_t])
        em.op("pool", lambda: nc.gpsimd.tensor_tensor(out=r_t[:], in0=r_t[:], in1=g_bc[:], op=ALU.mult), r=[r_t, g_bc], w=[r_t])
        em.op("dve", lambda: nc.vector.tensor_tensor(out=out_t[:], in0=r_t[:], in1=b_bc[:], op=ALU.add), r=[r_t, b_bc], w=[out_t])

    def load_transpose_mod(l, jsh, jsc, c, x_src, row0, xt, uT, col0, pp):
        k.dma("sp", xt[:], x_src[row0:row0 + 128, :], r=[x_src], w=[xt])
        for kk in range(KC):
            pk = pp[kk // 4]
            k.op("pe", lambda kk=kk, pk=pk: nc.tensor.transpose(
                out=pk[:, (kk % 4) * 128:(kk % 4 + 1) * 128], in_=xt[:, kk * 128:(kk + 1) * 128],
                identity=identF[:]), r=[xt, identF], w=[pk])
        for kk in range(KC):
            pk = pp[kk // 4]
            k.op("act", lambda kk=kk, pk=pk: nc.scalar.activation(
                out=uT[:, kk, col0:col0 + 128], in_=pk[:, (kk % 4) * 128:(kk % 4 + 1) * 128],
                func=AF.Identity, scale=fm1p[l][:, jsc * 8 + kk, c:c + 1], bias=fm[l][:, jsh * 8 + kk, c:c + 1]),
                r=[pk, fm1p[l], fm[l]], w=[uT])

    def outproj_ln1(l, catT, w_out_dram, gate_bc, t0, ntl, x_src, dst):
        mo = A.mark()
        w_outB = A.alloc([128, KC, D], BF16, "w_outB")
        k.dma("pool", w_outB[:], w_out_dram[:].rearrange("(kk p) n -> p kk n", p=128), w=[w_outB])
        lng = A.alloc([128, D], F32, "lng"); lnb = A.alloc([128, D], F32, "lnb")
        k.dma("sp", lng[:], ln1_g[l:l + 1, :].partition_broadcast(128), w=[lng])
        k.dma("sp", lnb[:], ln1_b[l:l + 1, :].partition_broadcast(128), w=[lnb])
        xts = [A.alloc([128, D], F32, f"oxt{i}") for i in range(2)]
        rts = [A.alloc([128, D], F32, f"ort{i}") for i in range(2)]
        for t in range(ntl):
            bg_step(1)
            tg = t0 + t
            xt = xts[t % 2]; rt = rts[t % 2]
            pp = (ps[0], ps[1]) if t % 2 == 0 else (ps[2], ps[3])
            k.dma("sp", xt[:], x_src[tg * 128:(tg + 1) * 128, :], r=[x_src], w=[xt])
            for hh in range(2):
                for kk in range(KC):
                    k.op("pe", lambda kk=kk, hh=hh, t=t: nc.tensor.matmul(
                        pp[hh][:, :], lhsT=catT[:, kk, t * 128:(t + 1) * 128], rhs=w_outB[:, kk, hh * 512:(hh + 1) * 512],
                        start=(kk == 0), stop=(kk == KC - 1)), r=[catT, w_outB], w=[pp[hh]])
            for hh in range(2):
                k.op("dve", lambda hh=hh: nc.vector.tensor_tensor(
                    out=rt[:, hh * 512:(hh + 1) * 512], in0=pp[hh][:, :], in1=gate_bc[:, hh * 512:(hh + 1) * 512], op=ALU.mult),
                    r=[pp[hh], gate_bc], w=[rt])
            k.op("dve", lambda: nc.vector.scalar_tensor_tensor(
                out=rt[:], in0=xt[:], scalar=ALPHA, in1=rt[:], op0=ALU.mult, op1=ALU.add), r=[xt, rt], w=[rt])
            layer_norm_tile(rt, lng, lnb, xt)
            k.dma("sp", dst[tg * 128:(tg + 1) * 128, :], xt[:], r=[xt], w=[dst])
        A.release(mo)

    SEQS = [(0, 16, 0, False), (16, 2, 1, True), (18, 2, 1, True)]

    def even_mixer_phase(l, x_src, dst):
        m_ph = A.mark()
        w_inB = A.alloc([128, KC, 1280], BF16, "w_inB")
        k.dma("pool", w_inB[:], w_in_e[:].rearrange("(kk p) n -> p kk n", p=128), w=[w_inB])
        rotB = A.alloc([128, 128], BF16, "rotB")
        k.dma("pool", rotB[:], cst["c_rot"][:], w=[rotB])
        mge = A.alloc([128, 128], BF16, "mge"); mle = A.alloc([128, 128], BF16, "mle")
        k.dma("pool", mge[:], cst["c_mge"][:], w=[mge])
        k.dma("pool", mle[:], cst["c_mle"][:], w=[mle])
        esink = A.alloc([128, 8], F32, "esink")
        k.dma("sp", esink[:], sink_in[0:1, :].partition_broadcast(128), w=[esink])
        k.op("act", lambda: nc.scalar.activation(out=esink[:], in_=esink[:], func=AF.Exp), r=[esink], w=[esink])
        poolwB = A.alloc([128, 4, 128], BF16, "poolwB")
        k.dma("pool", poolwB[:], pool_w[:].rearrange("g c d -> c g d"), w=[poolwB])
        poolsc = A.alloc([128, 4], F32, "poolsc")
        k.dma("sp", poolsc[:], pool_sc[:], w=[poolsc])
        edge = A.alloc([128, 64], F32, "edge")
        k.dma("sp", edge[:], cst["c_edge"][:], w=[edge])
        for si_, (t0, ntl, c, is_ctx) in enumerate(SEQS):
            S = ntl * 128
            TB = min(S, 512)
            ntb = S // TB
            m_seq = A.mark()
            catT = A.alloc([128, KC, S], BF16, "catT")
            gate_bc = A.alloc([128, D], F32, "gate_bc")
            k.dma("sp", gate_bc[:], modscr[l, c:c + 1, 2 * D:3 * D].partition_broadcast(128), r=[modscr], w=[gate_bc])
            m_mid = A.mark()
            qfT = A.alloc([128, 4, S], BF16, "qfT")
            kfT = A.alloc([128, S], BF16, "kfT")
            nslot = ntl + (0 if is_ctx else 2)
            vaug = A.alloc([128, nslot, 2, 65], BF16, "vaug")
            k.op("pool", lambda: nc.gpsimd.memset(vaug[:], 1.0), w=[vaug])
            pT = A.alloc([128, 4, S + 16], F32, "pT")
            k.op("pool", lambda: nc.gpsimd.memset(pT[:], 0.0), w=[pT])
            ckT = A.alloc([128, 256], BF16, "ckT")
            m_in = A.mark()
            if is_ctx:
                qT, kT = qfT, kfT
            else:
                qT = A.alloc([128, 4, S], BF16, "qT")
                kT = A.alloc([128, S], BF16, "kT")
            kvs = [A.alloc([128, 256], F32, f"kvs{i}") for i in range(2)]
            m_u = A.mark()
            uT = A.alloc([128, KC, S], BF16, "uT")
            xts = [A.alloc([128, D], F32, f"xt{i}") for i in range(2)]
            for t in range(ntl):
                pp = (ps[0], ps[1]) if t % 2 == 0 else (ps[2], ps[3])
                load_transpose_mod(l, 0, 1, c, x_src, (t0 + t) * 128, xts[t % 2], uT, t * 128, pp)
                bg_step(1)
            if stop_after == f"E1@{si_}":
                return _finish(k, y_out, [(uT, uT[:, 0, 0:min(S, 1024)], 0)])
            col_tiles = [("q", j, j * 128) for j in range(4)] + [("k", 0, 512)] + [("p", g, 768 + g * 128) for g in range(4)]
            pi = 0
            for (kind, idx, c0) in col_tiles:
                bg_step(1)
                for tb in range(ntb):
                    pk = ps[4 + pi % 3]
                    pi += 1
                    for kk in range(KC):
                        k.op("pe", lambda kk=kk, pk=pk, c0=c0, tb=tb: nc.tensor.matmul(
                            pk[:, 0:TB], lhsT=w_inB[:, kk, c0:c0 + 128], rhs=uT[:, kk, tb * TB:(tb + 1) * TB],
                            start=(kk == 0), stop=(kk == KC - 1)), r=[w_inB, uT], w=[pk])
                    if kind == "q":
                        evac(qT[:, idx, tb * TB:(tb + 1) * TB], pk[:, 0:TB], [pk], [qT])
                    elif kind == "k":
                        evac(kT[:, tb * TB:(tb + 1) * TB], pk[:, 0:TB], [pk], [kT])
                    else:
                        evac(pT[:, idx, 8 + tb * TB:8 + (tb + 1) * TB], pk[:, 0:TB], [pk], [pT])
            if stop_after == f"E2a@{si_}":
                return _finish(k, y_out, [(qT, qT[:, 0, 0:min(S, 1024)], 0)])
            for t in range(ntl):
                pk = ps[4 + pi % 3]
                pi += 1
                for kk in range(KC):
                    k.op("pe", lambda kk=kk, pk=pk, t=t: nc.tensor.matmul(
                        pk[:, 0:256], lhsT=uT[:, kk, t * 128:(t + 1) * 128], rhs=w_inB[:, kk, 512:768],
                        start=(kk == 0), stop=(kk == KC - 1)), r=[w_inB, uT], w=[pk])
                k.op("dve", lambda pk=pk, t=t: nc.vector.tensor_copy(
                    out=vaug[:, t, :, 0:64], in_=pk[:, 128:256].rearrange("p (g d) -> p g d", g=2)), r=[pk], w=[vaug])
                if is_ctx:
                    kv = kvs[t % 2]
                    k.op("act", lambda pk=pk, kv=kv: nc.scalar.activation(out=kv[:], in_=pk[:, 0:256], func=AF.Copy), r=[pk], w=[kv])
                    row0 = (t0 - 16 + t) * 128
                    k.dma("sp", nk_out[row0:row0 + 128, :], kv[:, 0:128], r=[kv], w=[nk_out])
                    k.dma("sp", nv_out[row0:row0 + 128, :], kv[:, 128:256], r=[kv], w=[nv_out])
            A.release(m_u)
            if not is_ctx:
                for blk in range(2):
                    ckt = kvs[blk]
                    k.dma("sp", ckt[:, 0:128], ck_in[blk * 128:(blk + 1) * 128, :], w=[ckt])
                    k.dma("sp", ckt[:, 128:256], cv_in[blk * 128:(blk + 1) * 128, :], w=[ckt])
                    pk = ps[4 + pi % 3]
                    pi += 1
                    k.op("pe", lambda pk=pk, ckt=ckt: nc.tensor.transpose(out=pk[:, 0:128], in_=ckt[:, 0:128], identity=identF[:]),
                         r=[ckt, identF], w=[pk])
                    evac(ckT[:, blk * 128:(blk + 1) * 128], pk[:, 0:128], [pk], [ckT])
                    k.op("dve", lambda ckt=ckt, blk=blk: nc.vector.tensor_copy(
                        out=vaug[:, ntl + blk, :, 0:64], in_=ckt[:, 128:256].rearrange("p (g d) -> p g d", g=2)), r=[ckt], w=[vaug])
                cosb = [A.alloc([128, TB], F32, f"cosb{i}") for i in range(2)]
                sinb = [A.alloc([128, TB], F32, f"sinb{i}") for i in range(2)]
                tmp1 = [A.alloc([128, TB], F32, f"rt1_{i}") for i in range(2)]
                tmp2 = [A.alloc([128, TB], F32, f"rt2_{i}") for i in range(2)]
                ri = 0
                for tb in range(ntb):
                    cb = cosb[tb % 2]; sb_ = sinb[tb % 2]
                    k.dma("sp", cb[:], cst["c_cos"][:, tb * TB:(tb + 1) * TB], w=[cb])
                    k.dma("sp", sb_[:], cst["c_sin"][:, tb * TB:(tb + 1) * TB], w=[sb_])
                    for j in range(5):
                        src = qT[:, j, tb * TB:(tb + 1) * TB] if j < 4 else kT[:, tb * TB:(tb + 1) * TB]
                        dstq = qfT[:, j, tb * TB:(tb + 1) * TB] if j < 4 else kfT[:, tb * TB:(tb + 1) * TB]
                        srcT = qT if j < 4 else kT
                        dstT = qfT if j < 4 else kfT
                        pk = ps[4 + pi % 3]
                        pi += 1
                        t1 = tmp1[ri % 2]; t2 = tmp2[ri % 2]
                        ri += 1
                        k.op("pe", lambda pk=pk, src=src: nc.tensor.matmul(pk[:, 0:TB], lhsT=rotB[:], rhs=src, start=True, stop=True),
                             r=[rotB, srcT], w=[pk])
                        k.op("dve", lambda src=src, t1=t1, cb=cb: nc.vector.tensor_tensor(
                            out=t1[:], in0=src, in1=cb[:], op=ALU.mult), r=[srcT, cb], w=[t1])
                        k.op("dve", lambda pk=pk, t2=t2, sb_=sb_: nc.vector.tensor_tensor(
                            out=t2[:], in0=pk[:, 0:TB], in1=sb_[:], op=ALU.mult), r=[pk, sb_], w=[t2])
                        k.op("pool", lambda dstq=dstq, t1=t1, t2=t2: nc.gpsimd.tensor_tensor(out=dstq, in0=t1[:], in1=t2[:], op=ALU.add),
                             r=[t1, t2], w=[dstT])
            A.release(m_in)
            if stop_after == f"E2@{si_}":
                return _finish(k, y_out, [(qfT, qfT[:, 0, 0:min(S, 1024)], 0), (kfT, kfT[:, 0:min(S, 1024)], 1), (pT, pT[:, 0, 8:8 + min(S, 1024)], 2)])
            pexs = [A.alloc([128, 10, 512], BF16, f"pex{i}") for i in range(2)]
            attn_tm = [A.alloc([128, 512], BF16, f"attn_tm{i}") for i in range(2)]
            den = A.alloc([128, 2, 4], F32, "den")
            for n in range(ntl):
                bg_step(1)
                if is_ctx:
                    kbs = [(kfT[:, kb * 128:(kb + 1) * 128], kfT, kb, None) for kb in range(ntl)]
                else:
                    kbs = []
                    for kb, msk in ((n - 1, mge), (n, None), (n + 1, mle)):
                        if 0 <= kb < ntl:
                            kbs.append((kfT[:, kb * 128:(kb + 1) * 128], kfT, kb, msk))
                    kbs.append((ckT[:, 0:128], ckT, ntl, None))
                    kbs.append((ckT[:, 128:256], ckT, ntl + 1, None))
                pex = pexs[n % 2]; atm = attn_tm[n % 2]
                si = 0
                for i, (kap, kbuf, slot, msk) in enumerate(kbs):
                    for g in range(2):
                        pk = ps[si % 2]
                        si += 1
                        k.op("pe", lambda pk=pk, kap=kap, g=g: nc.tensor.matmul(
                            pk[:, :].rearrange("p (j q) -> p j q", j=4), lhsT=kap[g * 64:(g + 1) * 64, :],
                            rhs=qfT[g * 64:(g + 1) * 64, :, n * 128:(n + 1) * 128], start=True, stop=True),
                            r=[kbuf, qfT], w=[pk])
                        k.op("act", lambda pk=pk, i=i, g=g: nc.scalar.activation(
                            out=pex[:, i * 2 + g, :], in_=pk[:, :], func=AF.Exp, scale=0.125), r=[pk], w=[pex])
                        if msk is not None:
                            k.op("pool", lambda i=i, g=g, msk=msk: nc.gpsimd.tensor_tensor(
                                out=pex[:, i * 2 + g, :].rearrange("p (j q) -> p j q", j=4),
                                in0=pex[:, i * 2 + g, :].rearrange("p (j q) -> p j q", j=4),
                                in1=msk[:].unsqueeze(1).to_broadcast([128, 4, 128]), op=ALU.mult), r=[pex, msk], w=[pex])
                for g in range(2):
                    po = ps[2 + g]
                    for j in range(4):
                        for i, (kap, kbuf, slot, msk) in enumerate(kbs):
                            k.op("pe", lambda po=po, j=j, i=i, g=g, slot=slot: nc.tensor.matmul(
                                po[:, j * 65:(j + 1) * 65], lhsT=pex[:, i * 2 + g, j * 128:(j + 1) * 128],
                                rhs=vaug[:, slot, g, :], start=(i == 0), stop=(i == len(kbs) - 1)),
                                r=[pex, vaug], w=[po])
                    pov = po[:, 0:260].rearrange("p (j e) -> p j e", e=65)
                    k.op("dve", lambda pov=pov, g=g: nc.vector.tensor_tensor(
                        out=den[:, g, :], in0=pov[:, :, 64], in1=esink[:, g * 4:(g + 1) * 4], op=ALU.add), r=[po, esink], w=[den])
                    k.op("dve", lambda g=g: nc.vector.reciprocal(out=den[:, g, :], in_=den[:, g, :]), r=[den], w=[den])
                    k.op("dve", lambda pov=pov, g=g, atm=atm: nc.vector.tensor_tensor(
                        out=atm[:, g * 256:(g + 1) * 256].rearrange("p (j d) -> p j d", j=4), in0=pov[:, :, 0:64],
                        in1=den[:, g, :].unsqueeze(2).to_broadcast([128, 4, 64]), op=ALU.mult), r=[po, den], w=[atm])
                for cc in range(4):
                    k.op("pe", lambda cc=cc, atm=atm: nc.tensor.transpose(
                        out=psb[:, cc * 128:(cc + 1) * 128], in_=atm[:, cc * 128:(cc + 1) * 128], identity=identB[:]),
                        r=[atm, identB], w=[psb])
                evac(catT[:, 0:4, n * 128:(n + 1) * 128], psb[:, 0:512].rearrange("p (c q) -> p c q", c=4), [psb], [catT])
            if stop_after == f"E4@{si_}":
                return _finish(k, y_out, [(catT, catT[:, cc_, 0:min(S, 1024)], cc_) for cc_ in range(4)])
            tA = A.alloc([128, S + 16], F32, "tA"); tB = A.alloc([128, S + 16], F32, "tB")
            pooledT = A.alloc([128, S], BF16, "pooledT")
            L = S + 16
            for g, wdw in enumerate((2, 4, 8, 16)):
                xg = pT[:, g, :]
                V = nc.vector
                k.op("dve", lambda xg=xg: V.tensor_tensor(out=tA[:, 1:L], in0=xg[:, 1:L], in1=xg[:, 0:L - 1], op=ALU.add), r=[pT], w=[tA])
                sres = tA
                if g >= 1:
                    k.op("dve", lambda: V.tensor_tensor(out=tB[:, 2:L - 1], in0=tA[:, 1:L - 2], in1=tA[:, 3:L], op=ALU.add), r=[tA], w=[tB])
                    sres = tB
                if g >= 2:
                    k.op("dve", lambda: V.tensor_tensor(out=tA[:, 4:L - 3], in0=tB[:, 2:L - 5], in1=tB[:, 6:L - 1], op=ALU.add), r=[tB], w=[tA])
                    sres = tA
                if g >= 3:
                    k.op("dve", lambda: V.tensor_tensor(out=tB[:, 8:8 + S], in0=tA[:, 4:4 + S], in1=tA[:, 12:12 + S], op=ALU.add), r=[tA], w=[tB])
                    sres = tB
                oth = tB if sres is tA else tA
                k.op("dve", lambda sres=sres, oth=oth, wdw=wdw: V.tensor_scalar(
                    out=oth[:, 8:8 + S], in0=sres[:, 8:8 + S], scalar1=1.0 / wdw, scalar2=None, op0=ALU.mult), r=[sres], w=[oth])
                k.op("dve", lambda oth=oth, g=g: V.tensor_tensor(
                    out=oth[:, 8:16], in0=oth[:, 8:16], in1=edge[:, g * 16:g * 16 + 8], op=ALU.mult), r=[oth, edge], w=[oth])
                k.op("dve", lambda oth=oth, g=g: V.tensor_tensor(
                    out=oth[:, S:S + 8], in0=oth[:, S:S + 8], in1=edge[:, g * 16 + 8:g * 16 + 16], op=ALU.mult), r=[oth, edge], w=[oth])
                k.op("dve", lambda oth=oth, xg=xg: V.tensor_tensor(
                    out=pooledT[:], in0=oth[:, 8:8 + S], in1=xg[:, 8:8 + S], op=ALU.subtract), r=[oth, pT], w=[pooledT])
                for tb in range(ntb):
                    pk = ps[4 + pi % 3]
                    pi += 1
                    k.op("pe", lambda pk=pk, g=g, tb=tb: nc.tensor.matmul(
                        pk[:, 0:TB], lhsT=poolwB[:, g, :], rhs=pooledT[:, tb * TB:(tb + 1) * TB], start=True, stop=True),
                        r=[poolwB, pooledT], w=[pk])
                    k.op("act", lambda pk=pk, g=g, tb=tb: nc.scalar.activation(
                        out=catT[:, 4 + g, tb * TB:(tb + 1) * TB], in_=pk[:, 0:TB], func=AF.Identity, scale=poolsc[:, g:g + 1]),
                        r=[pk, poolsc], w=[catT])
            if stop_after == f"E5@{si_}":
                return _finish(k, y_out, [(catT, catT[:, cc_, 0:min(S, 1024)], cc_) for cc_ in range(8)])
            A.release(m_mid)
            outproj_ln1(l, catT, w_out_e, gate_bc, t0, ntl, x_src, dst)
            if stop_after == f"E6@{si_}":
                return _finish(k, y_out, [])
            A.release(m_seq)
        A.release(m_ph)
        return None

    bg["c"] = BgConv(0)
    r_ = even_mixer_phase(0, x_in, y_out if (stop_after == "x1_0" or str(stop_after).startswith("E6@")) else X1)
    if r_ is not None:
        return r_
    if stop_after == "x1_0":
        return _finish(k, y_out, [])

    CG = 2
    NSLOT = 3

    class Deferred:
        def __init__(self):
            self.q = []

        COST = {"dve": 0.40, "act": 0.30, "pe": 0.12, "pool": 0.5}

        def op(self, e, *a, **kw):
            self.q.append((self.COST.get(e, 0.3), lambda: k.op(e, *a, **kw)))

        def dma(self, *a, **kw):
            self.q.append((0.1, lambda: k.dma(*a, **kw)))

        def pull(self, budget):
            spent = 0.0
            while self.q and spent < budget:
                c_, fn = self.q.pop(0)
                fn()
                spent += c_

    def evac_em(em, out_ap, in_ap, r, w):
        em.op("dve", lambda: nc.vector.tensor_copy(out=out_ap, in_=in_ap), r=r, w=w)

    def peer_phase(l, src, dst, ngroups=10):
        V_ = nc.vector
        m_ph = A.mark()
        k1B = A.alloc([128, 128], BF16, "k1B"); k2B = A.alloc([128, 128], BF16, "k2B")
        k.dma("pool", k1B[:], k1_in[l], w=[k1B])
        k.dma("pool", k2B[:], k2_in[l], w=[k2B])
        kB = (k1B, k2B)
        ln2g = A.alloc([128, D], F32, "ln2g"); ln2b = A.alloc([128, D], F32, "ln2b")
        k.dma("sp", ln2g[:], ln2_g[l:l + 1, :].partition_broadcast(128), w=[ln2g])
        k.dma("sp", ln2b[:], ln2_b[l:l + 1, :].partition_broadcast(128), w=[ln2b])
        gate2 = [A.alloc([128, D], F32, f"gate2_{c}") for c in range(2)]
        for c in range(2):
            k.dma("sp", gate2[c][:], modscr[l, c:c + 1, 5 * D:6 * D].partition_broadcast(128), r=[modscr], w=[gate2[c]])
        wqB = A.alloc([128, KC, 2048], BF16, "wqB")
        wq_v = wq_in[l].rearrange("(kk p) n -> p kk n", p=128)
        for hh in range(2):
            k.dma("pool", wqB[:, :, hh * 1024:(hh + 1) * 1024], wq_v[:, :, hh * 1024:(hh + 1) * 1024], w=[wqB])
        Wall = A.alloc([128, 128, 256], BF16, "Wall")
        if bg["c"] is not None:
            bg["c"].finish()
            bg["c"] = None
        ub = [sub[0], sub[1], A.alloc([128, CG, 1024], BF16, "ub2")]
        vb = [svb[0], svb[1], A.alloc([128, CG, 1024], BF16, "vb2")]
        xg = A.alloc([128, D], F32, "xg")
        u2Ts = [A.alloc([128, KC, 256], BF16, f"u2T{i}") for i in range(2)]
        igTs = [A.alloc([128, 3, 256], F32, f"igT{i}") for i in range(2)]
        qT = A.alloc([128, 16, 256], BF16, "pqT")
        s_all = A.alloc([128, 16, 128], F32, "s_all")
        scrB = A.alloc([128, 2048], F32, "scrB")
        cand = A.alloc([128, 8, 256], F32, "cand")
        rg = [T(cand[:, 4 * j:4 * j + 4, :].rearrange("p h n -> p (h n)"), f"rg{j}", buf=cand.b) for j in range(2)]
        xre = [T(scrB[:, j * 1024:(j + 1) * 1024], f"xre{j}", buf=scrB.b) for j in range(2)]
        vals = A.alloc([128, 16, 16], F32, "vals")
        idxu = A.alloc([128, 16, 16], U32, "idxu")
        idxf = A.alloc([128, 16, 16], F32, "idxf")
        tv = A.alloc([128, 8, 16], F32, "tv")
        pos = A.alloc([128, 8, 16], U32, "pos")
        au = A.alloc([128, 8, 16], U32, "au"); bu = A.alloc([128, 8, 16], U32, "bu")
        af = A.alloc([128, 8, 16], F32, "af"); bf = A.alloc([128, 8, 16], F32, "bf")
        sel = A.alloc([128, 3, 128], F32, "sel")
        ssum = A.alloc([128, 8], F32, "ssum")
        Qb = [A.alloc([128, 4, 128], BF16, f"Qb{i}") for i in range(2)]
        Pb = [A.alloc([128, 4, 128], BF16, f"Pb{i}") for i in range(2)]
        Gs = [A.alloc([128, 256], F32, f"Gs{i}") for i in range(2)]
        Zs = [[A.alloc([128, 128], BF16, f"Zs{i}_{j}") for j in range(2)] for i in range(2)]
        swork = scrB[:, :].rearrange("p (m n) -> p m n", n=128)
        cwork = scrB[:, :].rearrange("p (h n) -> p h n", n=256)
        eq = scrB[:, :].rearrange("p (h a b) -> p h a b", a=16, b=16)
        cand4 = cand[:, :, :].rearrange("p h (a b) -> p h a b", b=16)
        vals4 = vals[:, :, :].rearrange("p (h s) a -> p h s a", s=2)
        idxf4 = idxf[:, :, :].rearrange("p (h s) a -> p h s a", s=2)
        iota4 = iotaF[:, 0:16].unsqueeze(1).unsqueeze(1).to_broadcast([128, 8, 16, 16])
        RB = (ps[6], ps7)

        def stream_load(cg, grp):
            sl = cg % NSLOT
            if False:
                k.dma("pool", ub[sl][:], ut_in[l, cg * CG:(cg + 1) * CG].rearrange("c p n -> p c n"), w=[ub[sl]])
                k.dma("pool", vb[sl][:], v_in[l, cg * CG * 128:(cg + 1) * CG * 128, :].rearrange("(c p) n -> p c n", p=128), w=[vb[sl]])
                k.dma("sp", UTb[cg * CG:(cg + 1) * CG].rearrange("c p n -> p c n"), ub[sl][:], r=[ub[sl]], w=[UTb])
                k.dma("sp", Vb[cg * CG * 128:(cg + 1) * CG * 128, :].rearrange("(c p) n -> p c n", p=128), vb[sl][:], r=[vb[sl]], w=[Vb])
            else:
                k.dma("sp", ub[sl][:], UTb[cg * CG:(cg + 1) * CG].rearrange("c p n -> p c n"), r=[UTb], w=[ub[sl]])
                k.dma("sp", vb[sl][:], Vb[cg * CG * 128:(cg + 1) * CG * 128, :].rearrange("(c p) n -> p c n", p=128), r=[Vb], w=[vb[sl]])

        def R_build(em, grp, par):
            c = 0 if grp < 8 else 1
            u2T = u2Ts[par]; igT = igTs[par]
            for j in range(2):
                row0 = (2 * grp + j) * 128
                em.dma("sp", xg[:], src[row0:row0 + 128, :], r=[src], w=[xg])
                for kk in range(KC):
                    pk = RB[kk // 4]
                    em.op("pe", lambda kk=kk, pk=pk: nc.tensor.transpose(
                        out=pk[:, (kk % 4) * 128:(kk % 4 + 1) * 128], in_=xg[:, kk * 128:(kk + 1) * 128],
                        identity=identF[:]), r=[xg, identF], w=[pk])
                for kk in range(KC):
                    pk = RB[kk // 4]
                    em.op("act", lambda kk=kk, pk=pk, j=j: nc.scalar.activation(
                        out=u2T[:, kk, j * 128:(j + 1) * 128], in_=pk[:, (kk % 4) * 128:(kk % 4 + 1) * 128],
                        func=AF.Identity, scale=fm1p[l][:, 4 * 8 + kk, c:c + 1], bias=fm[l][:, 3 * 8 + kk, c:c + 1]),
                        r=[pk, fm1p[l], fm[l]], w=[u2T])
            for m in range(16):
                pk = RB[m % 2]
                for kk in range(KC):
                    em.op("pe", lambda kk=kk, pk=pk, m=m: nc.tensor.matmul(
                        pk[:, 0:256], lhsT=wqB[:, kk, m * 128:(m + 1) * 128], rhs=u2T[:, kk, :],
                        start=(kk == 0), stop=(kk == KC - 1)), r=[wqB, u2T], w=[pk])
                evac_em(em, qT[:, m, :], pk[:, 0:256], [pk], [qT])
            for j in range(2):
                for mq in range(4):
                    pk = RB[mq % 2]
                    for mm in range(4):
                        m = mq * 4 + mm
                        em.op("pe", lambda pk=pk, m=m, mm=mm, j=j: nc.tensor.matmul(
                            pk[:, mm * 128:(mm + 1) * 128], lhsT=qT[:, m, j * 128:(j + 1) * 128], rhs=kB[m % 2][:],
                            start=True, stop=True), r=[qT, kB[m % 2]], w=[pk])
                    evac_em(em, s_all[:, mq * 4:(mq + 1) * 4, :], pk[:, :].rearrange("p (a b) -> p a b", b=128), [pk], [s_all])
                for m in range(16):
                    em.op("dve", lambda m=m: V_.max(out=vals[:, m, 0:8], in_=s_all[:, m, :]), r=[s_all], w=[vals])
                for m in range(16):
                    em.op("dve", lambda m=m: V_.max_index(out=idxu[:, m, 0:8], in_max=vals[:, m, 0:8], in_values=s_all[:, m, :]),
                          r=[s_all, vals], w=[idxu])
                for m in range(16):
                    em.op("dve", lambda m=m: V_.match_replace(out=swork[:, m, :], in_to_replace=vals[:, m, 0:8],
                                                              in_values=s_all[:, m, :], imm_value=NEG), r=[s_all, vals], w=[scrB])
                for m in range(16):
                    em.op("dve", lambda m=m: V_.max(out=vals[:, m, 8:16], in_=swork[:, m, :]), r=[scrB], w=[vals])
                for m in range(16):
                    em.op("dve", lambda m=m: V_.max_index(out=idxu[:, m, 8:16], in_max=vals[:, m, 8:16], in_values=swork[:, m, :]),
                          r=[scrB, vals], w=[idxu])
                em.op("dve", lambda: V_.tensor_tensor(
                    out=cand4, in0=vals4[:, :, 0, :].unsqueeze(3).to_broadcast([128, 8, 16, 16]),
                    in1=vals4[:, :, 1, :].unsqueeze(2).to_broadcast([128, 8, 16, 16]), op=ALU.add), r=[vals], w=[cand])
                for h in range(8):
                    em.op("dve", lambda h=h: V_.max(out=tv[:, h, 0:8], in_=cand[:, h, :]), r=[cand], w=[tv])
                for h in range(8):
                    em.op("dve", lambda h=h: V_.max_index(out=pos[:, h, 0:8], in_max=tv[:, h, 0:8], in_values=cand[:, h, :]),
                          r=[cand, tv], w=[pos])
                for h in range(8):
                    em.op("dve", lambda h=h: V_.match_replace(out=cwork[:, h, :], in_to_replace=tv[:, h, 0:8],
                                                              in_values=cand[:, h, :], imm_value=NEG), r=[cand, tv], w=[scrB])
                for h in range(8):
                    em.op("dve", lambda h=h: V_.max(out=tv[:, h, 8:16], in_=cwork[:, h, :]), r=[scrB], w=[tv])
                for h in range(8):
                    em.op("dve", lambda h=h: V_.max_index(out=pos[:, h, 8:16], in_max=tv[:, h, 8:16], in_values=cwork[:, h, :]),
                          r=[scrB, tv], w=[pos])
                em.op("dve", lambda: V_.tensor_single_scalar(out=au[:], in_=pos[:], scalar=4, op=ALU.logical_shift_right), r=[pos], w=[au])
                em.op("dve", lambda: V_.tensor_single_scalar(out=bu[:], in_=pos[:], scalar=15, op=ALU.bitwise_and), r=[pos], w=[bu])
                em.op("dve", lambda: V_.tensor_copy(out=af[:], in_=au[:]), r=[au], w=[af])
                em.op("dve", lambda: V_.tensor_copy(out=bf[:], in_=bu[:]), r=[bu], w=[bf])
                em.op("dve", lambda: V_.tensor_copy(out=idxf[:], in_=idxu[:]), r=[idxu], w=[idxf])
                for s_, abf in ((0, af), (1, bf)):
                    em.op("dve", lambda abf=abf: V_.tensor_tensor(
                        out=eq, in0=abf[:, :, :].unsqueeze(3).to_broadcast([128, 8, 16, 16]), in1=iota4, op=ALU.is_equal),
                        r=[abf, iotaF], w=[scrB])
                    em.op("dve", lambda s_=s_: V_.tensor_tensor(
                        out=eq, in0=eq, in1=idxf4[:, :, s_, :].unsqueeze(2).to_broadcast([128, 8, 16, 16]), op=ALU.mult),
                        r=[scrB, idxf], w=[scrB])
                    em.op("dve", lambda s_=s_: V_.tensor_reduce(
                        out=sel[:, s_, :].rearrange("p (h a) -> p h a", a=16), in_=eq, axis=AX.X, op=ALU.add), r=[scrB], w=[sel])
                selg = sel[:, 2, :].rearrange("p (h a) -> p h a", a=16)
                em.op("dve", lambda: V_.tensor_tensor(out=selg, in0=tv[:], in1=tv[:, :, 0:1].to_broadcast([128, 8, 16]), op=ALU.subtract),
                      r=[tv], w=[sel])
                em.op("act", lambda: nc.scalar.activation(out=selg, in_=selg, func=AF.Exp), r=[sel], w=[sel])
                em.op("dve", lambda: V_.tensor_reduce(out=ssum[:], in_=selg, axis=AX.X, op=ALU.add), r=[sel], w=[ssum])
                em.op("dve", lambda: V_.reciprocal(out=ssum[:], in_=ssum[:]), r=[ssum], w=[ssum])
                em.op("dve", lambda: V_.tensor_tensor(out=selg, in0=selg, in1=ssum[:].unsqueeze(2).to_broadcast([128, 8, 16]), op=ALU.mult),
                      r=[sel, ssum], w=[sel])
                for q3 in range(3):
                    em.op("pe", lambda q3=q3: nc.tensor.transpose(out=RB[0][:, q3 * 128:(q3 + 1) * 128], in_=sel[:, q3, :], identity=identF[:]),
                          r=[sel, identF], w=[RB[0]])
                evac_em(em, igT[:, :, j * 128:(j + 1) * 128], RB[0][:, 0:384].rearrange("p (q t) -> p q t", q=3), [RB[0]], [igT])

        def W_build(par, dqF=None):
            igT = igTs[par]
            for t4 in range(64):
                if dqF is not None and t4 >= 1:
                    dqF.pull(0.45)
                qb = Qb[t4 % 2]; pb_ = Pb[t4 % 2]; pw = ps[4 + t4 % 2]
                for tt in range(4):
                    t = t4 * 4 + tt
                    on_pool = False
                    e_ = "pool" if on_pool else "dve"
                    E_ = nc.gpsimd if on_pool else V_
                    k.op(e_, lambda tt=tt, t=t, E_=E_: E_.tensor_scalar(
                        out=qb[:, tt, :], in0=iotaB[:], scalar1=igT[:, 1, t:t + 1], scalar2=None, op0=ALU.is_equal),
                        r=[iotaB, igT], w=[qb])
                    k.op(e_, lambda tt=tt, t=t, E_=E_: E_.tensor_scalar(
                        out=pb_[:, tt, :], in0=iotaB[:], scalar1=igT[:, 0, t:t + 1], scalar2=igT[:, 2, t:t + 1],
                        op0=ALU.is_equal, op1=ALU.mult), r=[iotaB, igT], w=[pb_])
                for tt in range(4):
                    k.op("pe", lambda tt=tt: nc.tensor.matmul(
                        pw[:, tt * 128:(tt + 1) * 128], lhsT=qb[:, tt, :], rhs=pb_[:, tt, :], start=True, stop=True),
                        r=[qb, pb_], w=[pw])
                k.op("act", lambda t4=t4: nc.scalar.activation(
                    out=Wall[:, :, t4 * 4:(t4 + 1) * 4], in_=pw[:, :].rearrange("p (t i) -> p i t", t=4), func=AF.Copy),
                    r=[pw], w=[Wall])
            if dqF is not None:
                dqF.pull(1e9)

        def S_run(grp, par, dq):
            u2T = u2Ts[par]
            ncg = 128 // CG

            def u_side(c_):
                sl = (c_ // CG) % NSLOT; ci = c_ % CG
                pa = ps[4 + c_ % 2]
                for kk in range(KC):
                    k.op("pe", lambda kk=kk: nc.tensor.matmul(
                        pa[:, 0:256], lhsT=ub[sl][:, ci, kk * 128:(kk + 1) * 128], rhs=u2T[:, kk, :],
                        start=(kk == 0), stop=(kk == KC - 1)), r=[ub[sl], u2T], w=[pa])

            def mid(c_):
                pa = ps[4 + c_ % 2]; G = Gs[c_ % 2]; Z = Zs[c_ % 2]
                k.op("act", lambda: nc.scalar.activation(out=G[:], in_=pa[:, 0:256], func=AF.Gelu_apprx_tanh), r=[pa], w=[G])
                k.op("pool", lambda: nc.gpsimd.tensor_tensor(out=Z[0][:], in0=G[:, 0:128], in1=Wall[:, c_, 0:128], op=ALU.mult),
                     r=[G, Wall], w=[Z[0]])
                k.op("pool", lambda: nc.gpsimd.tensor_tensor(out=Z[1][:], in0=G[:, 128:256], in1=Wall[:, c_, 128:256], op=ALU.mult),
                     r=[G, Wall], w=[Z[1]])

            def v_side(c_):
                sl = (c_ // CG) % NSLOT; ci = c_ % CG
                Z = Zs[c_ % 2]
                for j in range(2):
                    for hh in range(2):
                        k.op("pe", lambda j=j, hh=hh: nc.tensor.matmul(
                            ps[j * 2 + hh][:, :], lhsT=Z[j][:], rhs=vb[sl][:, ci, hh * 512:(hh + 1) * 512],
                            start=(c_ == 0), stop=(c_ == 127)), r=[Z[j], vb[sl]], w=[ps[j * 2 + hh]])

            npull = (sum(c__ for c__, _ in dq.q) / 112.0) if dq is not None else 0
            u_side(0)
            mid(0)
            for c_ in range(128):
                if c_ % CG == 0 and c_ // CG + 2 < ncg:
                    stream_load(c_ // CG + 2, grp)
                if c_ + 1 < 128:
                    u_side(c_ + 1)
                v_side(c_)
                if c_ + 1 < 128:
                    mid(c_ + 1)
                if dq is not None:
                    dq.pull(npull)
            if dq is not None:
                dq.pull(1e9)

        def F_build(em, grp):
            c = 0 if grp < 8 else 1
            for j in range(2):
                row0 = (2 * grp + j) * 128
                em.dma("sp", xre[j][:], src[row0:row0 + 128, :], r=[src], w=[xre[j]])
            for j in range(2):
                row0 = (2 * grp + j) * 128
                for hh in range(2):
                    em.op("dve", lambda j=j, hh=hh: V_.tensor_tensor(
                        out=rg[j][:, hh * 512:(hh + 1) * 512], in0=ps[j * 2 + hh][:, :], in1=gate2[c][:, hh * 512:(hh + 1) * 512],
                        op=ALU.mult), r=[ps[j * 2 + hh], gate2[c]], w=[rg[j]])
                em.op("dve", lambda j=j: V_.scalar_tensor_tensor(
                    out=rg[j][:], in0=xre[j][:], scalar=ALPHA, in1=rg[j][:], op0=ALU.mult, op1=ALU.add), r=[xre[j], rg[j]], w=[rg[j]])
                layer_norm_tile(rg[j], ln2g, ln2b, xre[j], em=em)
                em.dma("sp", dst[row0:row0 + 128, :], xre[j][:], r=[xre[j]], w=[dst])

        R_build(k, 0, 0)
        dqF = None
        for grp in range(ngroups):
            par = grp % 2
            stream_load(0, grp)
            stream_load(1, grp)
            W_build(par, dqF)
            dq = None
            if grp + 1 < ngroups:
                dq = Deferred()
                R_build(dq, grp + 1, 1 - par)
            S_run(grp, par, dq)
            dqF = Deferred()
            F_build(dqF, grp)
        dqF.pull(1e9)
        A.release(m_ph)
        return None

    if stop_after == "x2_0g1":
        peer_phase(0, X1, y_out, ngroups=1)
        return _finish(k, y_out, [])
    peer_phase(0, X1, y_out if stop_after == "x2_0" else X2)
    if stop_after == "x2_0":
        return _finish(k, y_out, [])

    def odd_mixer_phase(l, x_src, dst):
        V_ = nc.vector
        m_ph = A.mark()
        w_inoB = A.alloc([128, KC, 2048], BF16, "w_inoB")
        wv = w_in_o[:].rearrange("(kk p) n -> p kk n", p=128)
        for hh in range(2):
            k.dma("pool", w_inoB[:, :, hh * 1024:(hh + 1) * 1024], wv[:, :, hh * 1024:(hh + 1) * 1024], w=[w_inoB])
        convw = A.alloc([128, 8, 4], F32, "convw"); convb = A.alloc([128, 8], F32, "convb")
        k.dma("sp", convw[:], conv_w[:], w=[convw])
        k.dma("sp", convb[:], conv_b[:], w=[convb])
        gaB = A.alloc([128, 2, 8, 128], BF16, "gaB"); gxB = A.alloc([128, 2, 8, 128], BF16, "gxB")
        for d_ in range(2):
            k.dma("pool", gaB[:, d_, :, :], ga_w[d_].rearrange("c i j -> i c j"), w=[gaB])
            k.dma("pool", gxB[:, d_, :, :], gx_w[d_].rearrange("c i j -> i c j"), w=[gxB])
        gab = A.alloc([128, 2, 8], F32, "gab"); gxb = A.alloc([128, 2, 8], F32, "gxb")
        lam = A.alloc([128, 2, 8], F32, "lam"); h0s = A.alloc([128, 2, 8], F32, "h0s")
        k.dma("sp", gab[:], ga_b[:], w=[gab]); k.dma("sp", gxb[:], gx_b[:], w=[gxb])
        k.dma("sp", lam[:], lam_in[:], w=[lam]); k.dma("sp", h0s[:], h0_in[:], w=[h0s])
        nsp8 = A.alloc([128, 2, 8], F32, "nsp8"); nsp16 = A.alloc([128, 2, 8], F32, "nsp16")
        k.op("act", lambda: nc.scalar.activation(out=lam[:], in_=lam[:], func=AF.Exp, scale=-1.0), r=[lam], w=[lam])
        k.op("act", lambda: nc.scalar.activation(out=lam[:], in_=lam[:], func=AF.Ln, bias=oneT[:, 0:1], scale=1.0), r=[lam, oneT], w=[lam])
        k.op("dve", lambda: V_.tensor_scalar(out=nsp8[:], in0=lam[:], scalar1=-8.0, scalar2=None, op0=ALU.mult), r=[lam], w=[nsp8])
        k.op("dve", lambda: V_.tensor_scalar(out=nsp16[:], in0=lam[:], scalar1=-16.0, scalar2=None, op0=ALU.mult), r=[lam], w=[nsp16])
        k.op("dve", lambda: V_.memset(stT[:], 0.0), w=[stT])
        for si_, (t0, ntl, c, is_ctx) in enumerate(SEQS):
            S = ntl * 128
            TB = min(S, 512)
            ntb = S // TB
            m_seq = A.mark()
            zT = A.alloc([128, KC, S], BF16, "zT")
            gate_bc = A.alloc([128, D], F32, "gate_bc")
            k.dma("sp", gate_bc[:], modscr[l, c:c + 1, 2 * D:3 * D].partition_broadcast(128), r=[modscr], w=[gate_bc])
            m_mid = A.mark()
            uT = A.alloc([128, KC, S], BF16, "uT")
            xts = [A.alloc([128, D], F32, f"xt{i}") for i in range(2)]
            for t in range(ntl):
                pp = (ps[0], ps[1]) if t % 2 == 0 else (ps[2], ps[3])
                load_transpose_mod(l, 0, 1, c, x_src, (t0 + t) * 128, xts[t % 2], uT, t * 128, pp)
                bg_step(1)
            xr = A.alloc([128, S + 3], F32, "xr")
            xc = A.alloc([128, S], F32, "xc")
            xcb = A.alloc([128, S], BF16, "xcb")
            gg = A.alloc([128, S], F32, "gg")
            b1 = [A.alloc([128, S], F32, f"b1_{i}") for i in range(2)]
            b2 = A.alloc([128, S], F32, "b2")
            b3 = A.alloc([128, S], F32, "b3")
            k.op("pool", lambda: nc.gpsimd.memset(xr[:], 0.0), w=[xr])
            pi = 0
            for cf in range(8):
                bg_step(3 if not is_ctx else 1)
                for which, c0 in ((0, cf * 128), (1, 1024 + cf * 128)):
                    for tb in range(ntb):
                        pk = ps[4 + pi % 3]
                        pi += 1
                        for kk in range(KC):
                            k.op("pe", lambda kk=kk, pk=pk, c0=c0, tb=tb: nc.tensor.matmul(
                                pk[:, 0:TB], lhsT=w_inoB[:, kk, c0:c0 + 128], rhs=uT[:, kk, tb * TB:(tb + 1) * TB],
                                start=(kk == 0), stop=(kk == KC - 1)), r=[w_inoB, uT], w=[pk])
                        if which == 0:
                            evac(xr[:, 2 + tb * TB:2 + (tb + 1) * TB], pk[:, 0:TB], [pk], [xr])
                        else:
                            k.op("act", lambda pk=pk, tb=tb: nc.scalar.activation(
                                out=gg[:, tb * TB:(tb + 1) * TB], in_=pk[:, 0:TB], func=AF.Gelu_apprx_tanh), r=[pk], w=[gg])
                k.op("dve", lambda cf=cf: V_.tensor_scalar(out=xc[:], in0=xr[:, 0:S], scalar1=convw[:, cf, 0:1], scalar2=convb[:, cf:cf + 1],
                                                           op0=ALU.mult, op1=ALU.add), r=[xr, convw, convb], w=[xc])
                for jj in range(1, 4):
                    k.op("dve", lambda cf=cf, jj=jj: V_.scalar_tensor_tensor(
                        out=xc[:], in0=xr[:, jj:jj + S], scalar=convw[:, cf, jj:jj + 1], in1=xc[:], op0=ALU.mult, op1=ALU.add),
                        r=[xr, convw, xc], w=[xc])
                k.op("pool", lambda: nc.gpsimd.tensor_copy(out=xcb[:], in_=xc[:]), r=[xc], w=[xcb])
                for d_ in range(2):
                    H = b1[d_]
                    for gw, gb, dstb in ((gaB, gab, H), (gxB, gxb, b2)):
                        for tb in range(ntb):
                            pk = ps[pi % 4]
                            pi += 1
                            k.op("pe", lambda pk=pk, gw=gw, tb=tb, d_=d_, cf=cf: nc.tensor.matmul(
                                pk[:, 0:TB], lhsT=gw[:, d_, cf, :], rhs=xcb[:, tb * TB:(tb + 1) * TB], start=True, stop=True),
                                r=[gw, xcb], w=[pk])
                            k.op("act", lambda pk=pk, gb=gb, dstb=dstb, tb=tb, d_=d_, cf=cf: nc.scalar.activation(
                                out=dstb[:, tb * TB:(tb + 1) * TB], in_=pk[:, 0:TB], func=AF.Sigmoid, bias=gb[:, d_, cf:cf + 1], scale=1.0),
                                r=[pk, gb], w=[dstb])
                    k.op("act", lambda H=H, d_=d_, cf=cf: nc.scalar.activation(out=b3[:], in_=H[:], func=AF.Exp, scale=nsp8[:, d_, cf:cf + 1]),
                         r=[H, nsp8], w=[b3])
                    k.op("act", lambda H=H, d_=d_, cf=cf: nc.scalar.activation(out=H[:], in_=H[:], func=AF.Exp, scale=nsp16[:, d_, cf:cf + 1]),
                         r=[H, nsp16], w=[H])
                    k.op("act", lambda H=H: nc.scalar.activation(out=H[:], in_=H[:], func=AF.Sqrt, bias=oneT[:, 0:1], scale=-1.0),
                         r=[H, oneT], w=[H])
                    k.op("pool", lambda: nc.gpsimd.tensor_tensor(out=b2[:], in0=b2[:], in1=xc[:], op=ALU.mult), r=[b2, xc], w=[b2])
                    k.op("dve", lambda H=H: V_.tensor_tensor(out=b2[:], in0=b2[:], in1=H[:], op=ALU.mult), r=[b2, H], w=[b2])
                    if is_ctx:
                        init = 0.0
                        rinit = []
                    else:
                        init = h0s[:, d_, cf:cf + 1]
                        rinit = [h0s]
                    if d_ == 0:
                        k.op("dve", lambda H=H, init=init: V_.tensor_tensor_scan(
                            out=H[:], data0=b3[:], data1=b2[:], initial=init, op0=ALU.mult, op1=ALU.add), r=[b3, b2] + rinit, w=[H])
                        if is_ctx:
                            k.op("dve", lambda H=H, cf=cf, si_=si_: V_.tensor_copy(out=stT[:, si_ - 1, 0, cf:cf + 1], in_=H[:, S - 1:S]), r=[H], w=[stT])
                    else:
                        k.op("dve", lambda H=H, init=init: V_.tensor_tensor_scan(
                            out=H[:, ::-1], data0=b3[:, ::-1], data1=b2[:, ::-1], initial=init, op0=ALU.mult, op1=ALU.add),
                            r=[b3, b2] + rinit, w=[H])
                        if is_ctx:
                            k.op("dve", lambda H=H, cf=cf, si_=si_: V_.tensor_copy(out=stT[:, si_ - 1, 1, cf:cf + 1], in_=H[:, 0:1]), r=[H], w=[stT])
                k.op("pool", lambda: nc.gpsimd.tensor_tensor(out=b2[:], in0=b1[0][:], in1=b1[1][:], op=ALU.add), r=[b1[0], b1[1]], w=[b2])
                k.op("dve", lambda cf=cf: V_.tensor_tensor(out=zT[:, cf, :], in0=b2[:], in1=gg[:], op=ALU.mult), r=[b2, gg], w=[zT])
            A.release(m_mid)
            outproj_ln1(l, zT, w_out_o, gate_bc, t0, ntl, x_src, dst)
            A.release(m_seq)
        k.op("pe", lambda: nc.tensor.transpose(out=ps[4][0:32, 0:128], in_=stT[:].rearrange("p s d c -> p (s d c)"), identity=identF[:]),
             r=[stT, identF], w=[ps[4]])
        st_o = A.alloc([32, 128], F32, "st_o")
        k.op("dve", lambda: V_.tensor_copy(out=st_o[:], in_=ps[4][0:32, 0:128]), r=[ps[4]], w=[st_o])
        k.dma("sp", ns_out[:].rearrange("s d (c p) -> (s d c) p", p=128), st_o[:], r=[st_o], w=[ns_out])
        A.release(m_ph)

    bg["c"] = BgConv(1)
    odd_mixer_phase(1, X2, y_out if stop_after == "x1_1" else X1)
    if stop_after == "x1_1":
        return _finish(k, y_out, [])
    peer_phase(1, X1, y_out)
    return _finish(k, y_out, [])


def _finish(k, y_out, dbg):
    for t, ap, row in dbg:
        p, n = ap.shape
        q = "sp" if ap.dtype == F32 else "pool"
        k.dma(q, y_out[row * 128:row * 128 + p, 0:n], ap, r=[t], w=[y_out])
    k.barrier(["sp"])
    return k


Q_PERM = [0, 4, 1, 5, 2, 6, 3, 7]


def _shared(inp):
    f = lambda a: np.ascontiguousarray(np.asarray(a, dtype=np.float32))
    sh = {}
    sh["ada_w"] = f(inp["ada_w"]); sh["ada_b"] = f(inp["ada_b"])
    sh["ada_bT"] = f(np.asarray(inp["ada_b"]).reshape(2, 48, 128).transpose(0, 2, 1))
    for n in ("ln1_g", "ln1_b", "ln2_g", "ln2_b"):
        sh[n] = f(inp[n])
    wi = np.asarray(inp["even_w_in"][0], np.float32)
    qcols = np.concatenate([np.arange(h * 64, (h + 1) * 64) for h in Q_PERM])
    sh["w_in_e"] = f(np.concatenate([wi[:, qcols], wi[:, 512:]], axis=1))
    sh["sink"] = f(inp["attn_sink"]).reshape(1, 8)
    sh["pool_w"] = f(inp["pool_w"][0])
    sh["pool_sc"] = f(np.asarray(inp["pool_scale"][0]).reshape(4, 128).T)
    sh["w_out_e"] = f(inp["even_w_out"][0])
    sh["w_in_o"] = f(inp["odd_w_in"][0])
    sh["conv_w"] = f(np.asarray(inp["conv_w"][0]).reshape(4, 8, 128).transpose(2, 1, 0))
    sh["conv_b"] = f(np.asarray(inp["conv_b"][0]).reshape(8, 128).T)
    sh["ga_w"] = f(inp["gate_a_w"][0]); sh["gx_w"] = f(inp["gate_x_w"][0])
    sh["ga_b"] = f(np.asarray(inp["gate_a_b"][0]).reshape(2, 8, 128).transpose(2, 0, 1))
    sh["gx_b"] = f(np.asarray(inp["gate_x_b"][0]).reshape(2, 8, 128).transpose(2, 0, 1))
    sh["lam"] = f(np.asarray(inp["lru_lambda"][0]).reshape(2, 8, 128).transpose(2, 0, 1))
    sh["w_out_o"] = f(inp["odd_w_out"][0])
    sh["wq"] = f(inp["peer_wq"])
    sh["k1T"] = f(np.asarray(inp["peer_k1"]).transpose(0, 2, 1))
    sh["k2T"] = f(np.asarray(inp["peer_k2"]).transpose(0, 2, 1))
    U = np.asarray(inp["peer_u"], np.float32)
    sh["peer_ut"] = f(U.reshape(2, 128, 128, 8, 128).transpose(0, 1, 4, 3, 2).reshape(2, 128, 128, 1024))
    sh["peer_v"] = f(inp["peer_v"])
    sh.update(_consts())
    return sh


def _percore(inp, b):
    f = lambda a: np.ascontiguousarray(np.asarray(a, dtype=np.float32))
    m = {}
    m["x"] = f(np.concatenate([inp["x_sample"][b], inp["x_prompt"][2 * b], inp["x_prompt"][2 * b + 1]], axis=0))
    m["ck"] = f(np.asarray(inp["cache_attn_k"][b, 0]).reshape(256, 128))
    m["cv"] = f(np.asarray(inp["cache_attn_v"][b, 0]).reshape(256, 128))
    m["h0"] = f(np.asarray(inp["state_lru"][b, 0]).reshape(2, 8, 128).transpose(2, 0, 1))
    cond = np.stack([np.asarray(inp["c"][b]), np.asarray(inp["c_ctx"])], 0).astype(np.float32)
    m["condT"] = f(cond.reshape(2, 8, 128).transpose(2, 1, 0))
    return m


def kernel(**inputs):
    nk = build()
    sh = _shared(inputs)
    in_maps = []
    for b in range(8):
        m = dict(sh)
        m.update(_percore(inputs, b))
        in_maps.append(m)
    res = run_bass_kernel_spmd(nk.nc, in_maps, core_ids=list(range(8)))
    y_prompt = np.zeros((16, 256, D), np.float32)
    y_sample = np.zeros((8, 2048, D), np.float32)
    new_k = np.zeros((16, 1, 256, 2, 64), np.float32)
    new_v = np.zeros((16, 1, 256, 2, 64), np.float32)
    new_s = np.zeros((16, 1, 2, D), np.float32)
    for b in range(8):
        r = res.results[b]
        y = r["y"]
        y_sample[b] = y[0:2048]
        y_prompt[2 * b] = y[2048:2304]
        y_prompt[2 * b + 1] = y[2304:2560]
        new_k[2 * b, 0] = r["new_k"][0:256].reshape(256, 2, 64)
        new_k[2 * b + 1, 0] = r["new_k"][256:512].reshape(256, 2, 64)
        new_v[2 * b, 0] = r["new_v"][0:256].reshape(256, 2, 64)
        new_v[2 * b + 1, 0] = r["new_v"][256:512].reshape(256, 2, 64)
        new_s[2 * b, 0] = r["new_s"][0]
        new_s[2 * b + 1, 0] = r["new_s"][1]
    return (y_prompt, y_sample, new_k, new_v, new_s)
```

```python
import math
from contextlib import ExitStack
import numpy as np
import concourse.bass as bass
import concourse.mybir as mybir
from concourse.bass_utils import run_bass_kernel_spmd

F32 = mybir.dt.float32
BF16 = mybir.dt.bfloat16
U32 = mybir.dt.uint32
I32 = mybir.dt.int32
AF = mybir.ActivationFunctionType
ALU = mybir.AluOpType
AX = mybir.AxisListType

D = 1024
KC = 8
NT = 20
NTS = 16
ALPHA = 4 ** 0.25
LN_EPS = 1e-5
NEG = -1.0e30


class Buf:
    __slots__ = ("lw", "rd", "name", "excl")

    def __init__(self, name="", excl=False):
        self.lw = None
        self.rd = {}
        self.name = name
        self.excl = excl


class T:
    def __init__(self, t, name="", excl=False, buf=None):
        self.t = t
        self.b = buf if buf is not None else Buf(name, excl)

    def __getitem__(self, k):
        return self.t[k]


class Eng:
    def __init__(self, name, handle, sem):
        self.name = name
        self.h = handle
        self.sem = sem
        self.count = 0
        self.waited = {}


class KB:
    def __init__(self):
        self.nc = bass.Bass("TRN2", target_bir_lowering=False)
        nc = self.nc
        self.eng = {
            "pe": Eng("pe", nc.tensor, nc.alloc_semaphore("sem_pe")),
            "act": Eng("act", nc.scalar, nc.alloc_semaphore("sem_act")),
            "dve": Eng("dve", nc.vector, nc.alloc_semaphore("sem_dve")),
            "pool": Eng("pool", nc.gpsimd, nc.alloc_semaphore("sem_pool")),
            "sp": Eng("sp", nc.sync, nc.alloc_semaphore("sem_sp")),
        }
        self.semid = {}
        for e in self.eng.values():
            self.semid[id(e.sem)] = e.sem
        self.dsem = {"hw": [[nc.alloc_semaphore(f"sem_dmah{i}"), 0] for i in range(24)],
                     "sw": [[nc.alloc_semaphore(f"sem_dmas{i}"), 0] for i in range(24)]}
        self.dnext = {"hw": 0, "sw": 0}
        self.nalloc = 0

    def sb(self, shape, dtype, name=None, st=None):
        self.nalloc += 1
        nm = f"{name or 't'}_{self.nalloc}"
        if st is not None:
            return T(st.enter_context(self.nc.sbuf_tensor(nm, list(shape), dtype)), nm)
        return T(self.nc.alloc_sbuf_tensor(nm, list(shape), dtype), nm)

    def dram(self, name, shape, dtype, kind):
        return T(self.nc.dram_tensor(name, list(shape), dtype, kind=kind).ap(), name)

    def _collect(self, e, r, w, is_dma):
        deps = {}

        def need(tok, raw):
            if tok is None:
                return
            sem, val, src = tok
            if (not is_dma) and src == e and not raw:
                return
            k = id(sem)
            if deps.get(k, (None, 0))[1] < val:
                deps[k] = (sem, val)

        for t in r:
            need(t.b.lw, True)
            if t.b.excl:
                for tok in t.b.rd.values():
                    need(tok, False)
        for t in w:
            need(t.b.lw, False)
            for tok in t.b.rd.values():
                need(tok, False)
        return deps

    def _wait(self, E, deps):
        for k, (sem, val) in deps.items():
            if E.waited.get(k, 0) < val:
                E.h.wait_ge(sem, val)
                E.waited[k] = val

    def _update(self, tok, r, w):
        for t in w:
            t.b.lw = tok
            t.b.rd = {}
        for t in r:
            k = id(tok[0])
            t.b.rd[k] = tok

    def op(self, e, fn, r=(), w=()):
        E = self.eng[e]
        self._wait(E, self._collect(e, r, w, False))
        inst = fn()
        inst.then_inc(E.sem, 1)
        E.count += 1
        tok = (E.sem, E.count, e)
        self._update(tok, r, w)
        return tok

    def dma(self, q, out, in_, r=(), w=(), **kw):
        Q = self.eng[q]
        self._wait(Q, self._collect(q, r, w, True))
        pool = "sw" if q == "pool" else "hw"
        ent = self.dsem[pool][self.dnext[pool]]
        self.dnext[pool] = (self.dnext[pool] + 1) % len(self.dsem[pool])
        sem, cnt = ent
        if cnt > 0 and Q.waited.get(id(sem), 0) < 16 * cnt:
            Q.h.wait_ge(sem, 16 * cnt)
            Q.waited[id(sem)] = 16 * cnt
        Q.h.dma_start(out=out, in_=in_, **kw).then_inc(sem, 16)
        ent[1] = cnt + 1
        tok = (sem, 16 * (cnt + 1), "dma")
        self._update(tok, r, w)
        return tok

    def barrier(self, engines=None):
        names = engines or list(self.eng.keys())
        for n in names:
            E = self.eng[n]
            deps = {}
            for F in self.eng.values():
                if F is not E and F.count > 0:
                    deps[id(F.sem)] = (F.sem, F.count)
            for sem, cnt in self.dsem["hw"] + self.dsem["sw"]:
                if cnt > 0:
                    deps[id(sem)] = (sem, 16 * cnt)
            self._wait(E, deps)


class Arena:
    def __init__(self, k, words):
        self.k = k
        self.words = words
        self.t = k.nc.alloc_sbuf_tensor("arena", [128, words], F32)
        self.off = 0
        self.peak = 0

    def mark(self):
        return self.off

    def release(self, m):
        self.k.barrier()
        self.off = m

    def alloc(self, shape, dtype, name=""):
        p = shape[0]
        n = int(np.prod(shape[1:]))
        w = n if dtype in (F32, U32, I32) else (n + 1) // 2
        w = (w + 15) // 16 * 16
        assert self.off + w <= self.words, f"arena overflow allocating {name} {shape}: {self.off}+{w}>{self.words}"
        base = self.t[0:p, self.off:self.off + w]
        if dtype != F32:
            base = base.bitcast(dtype)
        v = base[:, 0:n]
        if len(shape) == 3:
            v = v.rearrange("p (a b) -> p a b", b=shape[2])
        elif len(shape) == 4:
            v = v.rearrange("p (a b c) -> p a b c", b=shape[2], c=shape[3])
        self.off += w
        self.peak = max(self.peak, self.off)
        return T(v, name)


def _consts():
    c = {}
    c["c_ident"] = np.eye(128, dtype=np.float32)
    Rm = np.zeros((128, 128), np.float32)
    for i in range(128):
        if (i % 64) < 32:
            Rm[i + 32, i] = -1.0
        else:
            Rm[i - 32, i] = 1.0
    c["c_rot"] = Rm
    S = 2048
    rows = S // 64
    row = np.repeat(np.arange(rows, dtype=np.float32), 64)
    col = (np.arange(S) % 64).astype(np.float32)
    nf = 16
    inv = (np.float32(10000.0) ** (-np.arange(nf, dtype=np.float32) / nf)).astype(np.float32)
    ang = np.concatenate([row[:, None] * inv, col[:, None] * inv], axis=-1).astype(np.float32)
    cs = np.cos(ang).astype(np.float32)
    sn = np.sin(ang).astype(np.float32)
    c["c_cos"] = np.ascontiguousarray(np.tile(cs.T, (4, 1)))
    c["c_sin"] = np.ascontiguousarray(np.tile(sn.T, (4, 1)))
    p = np.arange(128)[:, None]
    f = np.arange(128)[None, :]
    c["c_mge"] = (p >= f).astype(np.float32)
    c["c_mle"] = (p <= f).astype(np.float32)
    edge = np.ones((4, 16), np.float32)
    Sx = 64
    for g, w in enumerate((2, 4, 8, 16)):
        lo = w // 2
        hi = w - lo - 1
        for t in range(8):
            cnt = min(t + hi + 1, Sx) - max(t - lo, 0)
            edge[g, t] = w / cnt
            tt = Sx - 8 + t
            cnt = min(tt + hi + 1, Sx) - max(tt - lo, 0)
            edge[g, 8 + t] = w / cnt
    c["c_edge"] = np.ascontiguousarray(np.broadcast_to(edge.reshape(1, 64), (128, 64)))
    c["c_iota"] = np.ascontiguousarray(np.broadcast_to(np.arange(128, dtype=np.float32)[None], (128, 128)))
    return c


def build(stop_after=None):
    k = KB()
    nc = k.nc

    def din(name, shape, dtype=F32):
        return k.dram(name, shape, dtype, "ExternalInput")

    def dout(name, shape, dtype=F32):
        return k.dram(name, shape, dtype, "ExternalOutput")

    x_in = din("x", [NT * 128, D])
    ck_in = din("ck", [256, 128])
    cv_in = din("cv", [256, 128])
    h0_in = din("h0", [128, 2, 8])
    condT_in = din("condT", [128, KC, 2])
    ada_w = din("ada_w", [2, D, 6 * D])
    ada_b = din("ada_b", [2, 6 * D])
    ada_bT_in = din("ada_bT", [2, 128, 48])
    ln1_g = din("ln1_g", [2, D]); ln1_b = din("ln1_b", [2, D])
    ln2_g = din("ln2_g", [2, D]); ln2_b = din("ln2_b", [2, D])
    w_in_e = din("w_in_e", [D, 1280])
    sink_in = din("sink", [1, 8])
    pool_w = din("pool_w", [4, 128, 128])
    pool_sc = din("pool_sc", [128, 4])
    w_out_e = din("w_out_e", [D, D])
    w_in_o = din("w_in_o", [D, 2 * D])
    conv_w = din("conv_w", [128, 8, 4])
    conv_b = din("conv_b", [128, 8])
    ga_w = din("ga_w", [2, 8, 128, 128]); ga_b = din("ga_b", [128, 2, 8])
    gx_w = din("gx_w", [2, 8, 128, 128]); gx_b = din("gx_b", [128, 2, 8])
    lam_in = din("lam", [128, 2, 8])
    w_out_o = din("w_out_o", [D, D])
    wq_in = din("wq", [2, D, 2048])
    k1_in = din("k1T", [2, 128, 128]); k2_in = din("k2T", [2, 128, 128])
    ut_in = din("peer_ut", [2, 128, 128, KC * 128])
    v_in = din("peer_v", [2, 16384, D])
    cst = {n: din(n, list(a.shape)) for n, a in _consts().items()}

    y_out = dout("y", [NT * 128, D])
    nk_out = dout("new_k", [512, 128])
    nv_out = dout("new_v", [512, 128])
    ns_out = dout("new_s", [2, 2, D])

    modscr = k.dram("modscr", [2, 2, 6 * D], F32, "Internal")
    X1 = k.dram("X1", [NT * 128, D], F32, "Internal")
    X2 = k.dram("X2", [NT * 128, D], F32, "Internal")
    UTb = k.dram("UTb", [128, 128, KC * 128], BF16, "Internal")
    Vb = k.dram("Vb", [16384, D], BF16, "Internal")

    ps = [T(nc.alloc_psum_tensor(f"ps{i}", [128, 512], F32), f"ps{i}", True) for i in range(7)]
    ps7 = T(nc.alloc_psum_tensor("ps7", [128, 512], F32), "ps7", True)
    psb = T(ps7.t[:, :].bitcast(BF16), "psb", buf=ps7.b)

    identF = k.sb([128, 128], F32, "identF")
    identB = k.sb([128, 128], BF16, "identB")
    iotaF = k.sb([128, 128], F32, "iotaF")
    iotaB = k.sb([128, 128], BF16, "iotaB")
    epsT = k.sb([128, 1], F32, "epsT")
    oneT = k.sb([128, 1], F32, "oneT")
    fm = [k.sb([128, 48, 2], F32, f"fm{l}") for l in range(2)]
    fm1p = [k.sb([128, 48, 2], F32, f"fm1p{l}") for l in range(2)]
    stats = k.sb([128, 12], F32, "stats"); mv = k.sb([128, 2], F32, "mv"); rstd = k.sb([128, 1], F32, "rstd")
    stT = k.sb([128, 2, 2, 8], F32, "stT")
    sub = [k.sb([128, 2, 1024], BF16, f"sub{i}") for i in range(2)]
    svb = [k.sb([128, 2, 1024], BF16, f"svb{i}") for i in range(2)]
    words = (nc.sbuf_bytes_remaining - 2048) // 4
    A = Arena(k, words)

    k.dma("sp", identF[:], cst["c_ident"][:], w=[identF])
    k.dma("sp", iotaF[:], cst["c_iota"][:], w=[iotaF])
    k.op("dve", lambda: nc.vector.tensor_copy(out=identB[:], in_=identF[:]), r=[identF], w=[identB])
    k.op("dve", lambda: nc.vector.tensor_copy(out=iotaB[:], in_=iotaF[:]), r=[iotaF], w=[iotaB])
    k.op("dve", lambda: nc.vector.memset(epsT[:], LN_EPS), w=[epsT])
    k.op("dve", lambda: nc.vector.memset(oneT[:], 1.0), w=[oneT])

    m0 = A.mark()
    condT = A.alloc([128, KC, 2], F32, "condT")
    scT = A.alloc([128, KC, 2], F32, "scT")
    k.dma("sp", condT[:], condT_in[:], w=[condT])
    k.op("act", lambda: nc.scalar.activation(out=scT[:], in_=condT[:], func=AF.Silu), r=[condT], w=[scT])
    adabT = A.alloc([128, 2, 48], F32, "adabT")
    k.dma("sp", adabT[:], ada_bT_in[:].rearrange("l p c -> p l c"), w=[adabT])
    awt = [A.alloc([128, KC, 512], F32, f"awt{i}") for i in range(3)]
    mtt = [A.alloc([48, 128], F32, f"mtt{i}") for i in range(2)]
    it = 0
    for l in range(2):
        aw_v = ada_w[l].rearrange("(kk p) n -> p kk n", p=128)
        for blk in range(12):
            wt = awt[it % 3]
            pf = ps[it % 2]
            k.dma("sp", wt[:], aw_v[:, :, blk * 512:(blk + 1) * 512], w=[wt])
            for q4 in range(4):
                for kk in range(KC):
                    k.op("pe", lambda kk=kk, wt=wt, pf=pf, q4=q4: nc.tensor.matmul(
                        pf[:, q4 * 2:q4 * 2 + 2], lhsT=wt[:, kk, q4 * 128:(q4 + 1) * 128], rhs=scT[:, kk, :],
                        start=(kk == 0), stop=(kk == KC - 1)), r=[scT, wt], w=[pf])
            k.op("dve", lambda pf=pf, l=l, blk=blk: nc.vector.tensor_tensor(
                out=fm[l][:, blk * 4:(blk + 1) * 4, :], in0=pf[:, 0:8].rearrange("p (a b) -> p a b", b=2),
                in1=adabT[:, l, blk * 4:(blk + 1) * 4].unsqueeze(2).to_broadcast([128, 4, 2]), op=ALU.add),
                r=[pf, adabT], w=[fm[l]])
            it += 1
        k.op("dve", lambda l=l: nc.vector.tensor_scalar(
            out=fm1p[l][:], in0=fm[l][:], scalar1=1.0, scalar2=None, op0=ALU.add), r=[fm[l]], w=[fm1p[l]])
        for c in range(2):
            pt_ = ps[2 + c]
            k.op("pe", lambda l=l, c=c, pt_=pt_: nc.tensor.transpose(out=pt_[0:48, 0:128], in_=fm[l][:, :, c], identity=identF[:]),
                 r=[fm[l], identF], w=[pt_])
            k.op("dve", lambda c=c, pt_=pt_: nc.vector.tensor_copy(out=mtt[c][:], in_=pt_[0:48, 0:128]), r=[pt_], w=[mtt[c]])
            k.dma("sp", modscr[l, c].rearrange("(ch p) -> ch p", p=128), mtt[c][:], r=[mtt[c]], w=[modscr])
    A.release(m0)

    if stop_after == "mod":
        return _finish(k, y_out, [(fm[0], fm[0][:].rearrange("p a b -> p (a b)"), 0)])

    class BgConv:
        def __init__(self, l):
            self.l = l
            self.cg = 0
            self.wb = 0

        def step(self, n=1):
            l = self.l
            for _ in range(n):
                if self.wb < self.cg and (self.cg - self.wb >= 2 or self.cg >= 64):
                    cg = self.wb; sl = cg % 2
                    k.dma("sp", UTb[cg * 2:(cg + 1) * 2].rearrange("c p n -> p c n"), sub[sl][:], r=[sub[sl]], w=[UTb])
                    k.dma("sp", Vb[cg * 256:(cg + 1) * 256, :].rearrange("(c p) n -> p c n", p=128), svb[sl][:], r=[svb[sl]], w=[Vb])
                    self.wb += 1
                if self.cg < 64:
                    cg = self.cg; sl = cg % 2
                    k.dma("pool", sub[sl][:], ut_in[l, cg * 2:(cg + 1) * 2].rearrange("c p n -> p c n"), w=[sub[sl]])
                    k.dma("pool", svb[sl][:], v_in[l, cg * 256:(cg + 1) * 256, :].rearrange("(c p) n -> p c n", p=128), w=[svb[sl]])
                    self.cg += 1

        def finish(self):
            while self.wb < 64:
                self.step(1)

    bg = {"c": None}

    def bg_step(n=1):
        if bg["c"] is not None:
            bg["c"].step(n)

    rr = {"ev": 0}

    def evac(out_ap, in_ap, r, w):
        rr["ev"] += 1
        if rr["ev"] % 2 == 0:
            k.op("act", lambda: nc.scalar.activation(out=out_ap, in_=in_ap, func=AF.Copy), r=r, w=w)
        else:
            k.op("dve", lambda: nc.vector.tensor_copy(out=out_ap, in_=in_ap), r=r, w=w)

    def layer_norm_tile(r_t, g_bc, b_bc, out_t, em=None):
        em = em or k
        for hh in range(2):
            em.op("dve", lambda hh=hh: nc.vector.bn_stats(out=stats[:, hh * 6:(hh + 1) * 6], in_=r_t[:, hh * 512:(hh + 1) * 512]),
                 r=[r_t], w=[stats])
        em.op("dve", lambda: nc.vector.bn_aggr(out=mv[:], in_=stats[:]), r=[stats], w=[mv])
        em.op("act", lambda: nc.scalar.activation(out=rstd[:], in_=mv[:, 1:2], func=AF.Sqrt, bias=epsT[:, 0:1], scale=1.0),
             r=[mv, epsT], w=[rstd])
        em.op("dve", lambda: nc.vector.reciprocal(out=rstd[:], in_=rstd[:]), r=[rstd], w=[rstd])
        em.op("dve", lambda: nc.vector.tensor_scalar(out=r_t[:], in0=r_t[:], scalar1=mv[:, 0:1], scalar2=rstd[:, 0:1],
                                                    op0=ALU.subtract, op1=ALU.mult), r=[r_t, mv, rstd], w=[r_t])
        if em is k:
            em.op("dve", lambda: nc.vector.tensor_tensor(out=r_t[:], in0=r_t[:], in1=g_bc[:], op=ALU.mult), r=[r_t, g_bc], w=[r_t])
        else:
            em.op("pool", lambda: nc.gpsimd.tensor_tensor(out=r_t[:], in0=r_t[:], in1=g_bc[:], op=ALU.mult), r=[r_t, g_bc], w=[r_t])
        em.op("dve", lambda: nc.vector.tensor_tensor(out=out_t[:], in0=r_t[:], in1=b_bc[:], op=ALU.add), r=[r_t, b_bc], w=[out_t])

    def load_transpose_mod(l, jsh, jsc, c, x_src, row0, xt, uT, col0, pp):
        k.dma("sp", xt[:], x_src[row0:row0 + 128, :], r=[x_src], w=[xt])
        for kk in range(KC):
            pk = pp[kk // 4]
            k.op("pe", lambda kk=kk, pk=pk: nc.tensor.transpose(
                out=pk[:, (kk % 4) * 128:(kk % 4 + 1) * 128], in_=xt[:, kk * 128:(kk + 1) * 128],
                identity=identF[:]), r=[xt, identF], w=[pk])
        for kk in range(KC):
            pk = pp[kk // 4]
            k.op("act", lambda kk=kk, pk=pk: nc.scalar.activation(
                out=uT[:, kk, col0:col0 + 128], in_=pk[:, (kk % 4) * 128:(kk % 4 + 1) * 128],
                func=AF.Identity, scale=fm1p[l][:, jsc * 8 + kk, c:c + 1], bias=fm[l][:, jsh * 8 + kk, c:c + 1]),
                r=[pk, fm1p[l], fm[l]], w=[uT])

    def outproj_ln1(l, catT, w_out_dram, gate_bc, t0, ntl, x_src, dst):
        mo = A.mark()
        w_outB = A.alloc([128, KC, D], BF16, "w_outB")
        k.dma("pool", w_outB[:], w_out_dram[:].rearrange("(kk p) n -> p kk n", p=128), w=[w_outB])
        lng = A.alloc([128, D], F32, "lng"); lnb = A.alloc([128, D], F32, "lnb")
        k.dma("sp", lng[:], ln1_g[l:l + 1, :].partition_broadcast(128), w=[lng])
        k.dma("sp", lnb[:], ln1_b[l:l + 1, :].partition_broadcast(128), w=[lnb])
        xts = [A.alloc([128, D], F32, f"oxt{i}") for i in range(2)]
        rts = [A.alloc([128, D], F32, f"ort{i}") for i in range(2)]
        for t in range(ntl):
            bg_step(1)
            tg = t0 + t
            xt = xts[t % 2]; rt = rts[t % 2]
            pp = (ps[0], ps[1]) if t % 2 == 0 else (ps[2], ps[3])
            k.dma("sp", xt[:], x_src[tg * 128:(tg + 1) * 128, :], r=[x_src], w=[xt])
            for hh in range(2):
                for kk in range(KC):
                    k.op("pe", lambda kk=kk, hh=hh, t=t: nc.tensor.matmul(
                        pp[hh][:, :], lhsT=catT[:, kk, t * 128:(t + 1) * 128], rhs=w_outB[:, kk, hh * 512:(hh + 1) * 512],
                        start=(kk == 0), stop=(kk == KC - 1)), r=[catT, w_outB], w=[pp[hh]])
            for hh in range(2):
                k.op("dve", lambda hh=hh: nc.vector.tensor_tensor(
                    out=rt[:, hh * 512:(hh + 1) * 512], in0=pp[hh][:, :], in1=gate_bc[:, hh * 512:(hh + 1) * 512], op=ALU.mult),
                    r=[pp[hh], gate_bc], w=[rt])
            k.op("dve", lambda: nc.vector.scalar_tensor_tensor(
                out=rt[:], in0=xt[:], scalar=ALPHA, in1=rt[:], op0=ALU.mult, op1=ALU.add), r=[xt, rt], w=[rt])
            layer_norm_tile(rt, lng, lnb, xt)
            k.dma("sp", dst[tg * 128:(tg + 1) * 128, :], xt[:], r=[xt], w=[dst])
        A.release(mo)

    SEQS = [(0, 16, 0, False), (16, 2, 1, True), (18, 2, 1, True)]

    def even_mixer_phase(l, x_src, dst):
        m_ph = A.mark()
        w_inB = A.alloc([128, KC, 1280], BF16, "w_inB")
        k.dma("pool", w_inB[:], w_in_e[:].rearrange("(kk p) n -> p kk n", p=128), w=[w_inB])
        rotB = A.alloc([128, 128], BF16, "rotB")
        k.dma("pool", rotB[:], cst["c_rot"][:], w=[rotB])
        mge = A.alloc([128, 128], BF16, "mge"); mle = A.alloc([128, 128], BF16, "mle")
        k.dma("pool", mge[:], cst["c_mge"][:], w=[mge])
        k.dma("pool", mle[:], cst["c_mle"][:], w=[mle])
        esink = A.alloc([128, 8], F32, "esink")
        k.dma("sp", esink[:], sink_in[0:1, :].partition_broadcast(128), w=[esink])
        k.op("act", lambda: nc.scalar.activation(out=esink[:], in_=esink[:], func=AF.Exp), r=[esink], w=[esink])
        poolwB = A.alloc([128, 4, 128], BF16, "poolwB")
        k.dma("pool", poolwB[:], pool_w[:].rearrange("g c d -> c g d"), w=[poolwB])
        poolsc = A.alloc([128, 4], F32, "poolsc")
        k.dma("sp", poolsc[:], pool_sc[:], w=[poolsc])
        edge = A.alloc([128, 64], F32, "edge")
        k.dma("sp", edge[:], cst["c_edge"][:], w=[edge])
        for si_, (t0, ntl, c, is_ctx) in enumerate(SEQS):
            S = ntl * 128
            TB = min(S, 512)
            ntb = S // TB
            m_seq = A.mark()
            catT = A.alloc([128, KC, S], BF16, "catT")
            gate_bc = A.alloc([128, D], F32, "gate_bc")
            k.dma("sp", gate_bc[:], modscr[l, c:c + 1, 2 * D:3 * D].partition_broadcast(128), r=[modscr], w=[gate_bc])
            m_mid = A.mark()
            qfT = A.alloc([128, 4, S], BF16, "qfT")
            kfT = A.alloc([128, S], BF16, "kfT")
            nslot = ntl + (0 if is_ctx else 2)
            vaug = A.alloc([128, nslot, 2, 65], BF16, "vaug")
            k.op("pool", lambda: nc.gpsimd.memset(vaug[:], 1.0), w=[vaug])
            pT = A.alloc([128, 4, S + 16], F32, "pT")
            k.op("pool", lambda: nc.gpsimd.memset(pT[:], 0.0), w=[pT])
            ckT = A.alloc([128, 256], BF16, "ckT")
            m_in = A.mark()
            if is_ctx:
                qT, kT = qfT, kfT
            else:
                qT = A.alloc([128, 4, S], BF16, "qT")
                kT = A.alloc([128, S], BF16, "kT")
            kvs = [A.alloc([128, 256], F32, f"kvs{i}") for i in range(2)]
            m_u = A.mark()
            uT = A.alloc([128, KC, S], BF16, "uT")
            xts = [A.alloc([128, D], F32, f"xt{i}") for i in range(2)]
            for t in range(ntl):
                pp = (ps[0], ps[1]) if t % 2 == 0 else (ps[2], ps[3])
                load_transpose_mod(l, 0, 1, c, x_src, (t0 + t) * 128, xts[t % 2], uT, t * 128, pp)
                bg_step(1)
            if stop_after == f"E1@{si_}":
                return _finish(k, y_out, [(uT, uT[:, 0, 0:min(S, 1024)], 0)])
            col_tiles = [("q", j, j * 128) for j in range(4)] + [("k", 0, 512)] + [("p", g, 768 + g * 128) for g in range(4)]
            pi = 0
            for (kind, idx, c0) in col_tiles:
                bg_step(1)
                for tb in range(ntb):
                    pk = ps[4 + pi % 3]
                    pi += 1
                    for kk in range(KC):
                        k.op("pe", lambda kk=kk, pk=pk, c0=c0, tb=tb: nc.tensor.matmul(
                            pk[:, 0:TB], lhsT=w_inB[:, kk, c0:c0 + 128], rhs=uT[:, kk, tb * TB:(tb + 1) * TB],
                            start=(kk == 0), stop=(kk == KC - 1)), r=[w_inB, uT], w=[pk])
                    if kind == "q":
                        evac(qT[:, idx, tb * TB:(tb + 1) * TB], pk[:, 0:TB], [pk], [qT])
                    elif kind == "k":
                        evac(kT[:, tb * TB:(tb + 1) * TB], pk[:, 0:TB], [pk], [kT])
                    else:
                        evac(pT[:, idx, 8 + tb * TB:8 + (tb + 1) * TB], pk[:, 0:TB], [pk], [pT])
            if stop_after == f"E2a@{si_}":
                return _finish(k, y_out, [(qT, qT[:, 0, 0:min(S, 1024)], 0)])
            for t in range(ntl):
                pk = ps[4 + pi % 3]
                pi += 1
                for kk in range(KC):
                    k.op("pe", lambda kk=kk, pk=pk, t=t: nc.tensor.matmul(
                        pk[:, 0:256], lhsT=uT[:, kk, t * 128:(t + 1) * 128], rhs=w_inB[:, kk, 512:768],
                        start=(kk == 0), stop=(kk == KC - 1)), r=[w_inB, uT], w=[pk])
                k.op("dve", lambda pk=pk, t=t: nc.vector.tensor_copy(
                    out=vaug[:, t, :, 0:64], in_=pk[:, 128:256].rearrange("p (g d) -> p g d", g=2)), r=[pk], w=[vaug])
                if is_ctx:
                    kv = kvs[t % 2]
                    k.op("act", lambda pk=pk, kv=kv: nc.scalar.activation(out=kv[:], in_=pk[:, 0:256], func=AF.Copy), r=[pk], w=[kv])
                    row0 = (t0 - 16 + t) * 128
                    k.dma("sp", nk_out[row0:row0 + 128, :], kv[:, 0:128], r=[kv], w=[nk_out])
                    k.dma("sp", nv_out[row0:row0 + 128, :], kv[:, 128:256], r=[kv], w=[nv_out])
            A.release(m_u)
            if not is_ctx:
                for blk in range(2):
                    ckt = kvs[blk]
                    k.dma("sp", ckt[:, 0:128], ck_in[blk * 128:(blk + 1) * 128, :], w=[ckt])
                    k.dma("sp", ckt[:, 128:256], cv_in[blk * 128:(blk + 1) * 128, :], w=[ckt])
                    pk = ps[4 + pi % 3]
                    pi += 1
                    k.op("pe", lambda pk=pk, ckt=ckt: nc.tensor.transpose(out=pk[:, 0:128], in_=ckt[:, 0:128], identity=identF[:]),
                         r=[ckt, identF], w=[pk])
                    evac(ckT[:, blk * 128:(blk + 1) * 128], pk[:, 0:128], [pk], [ckT])
                    k.op("dve", lambda ckt=ckt, blk=blk: nc.vector.tensor_copy(
                        out=vaug[:, ntl + blk, :, 0:64], in_=ckt[:, 128:256].rearrange("p (g d) -> p g d", g=2)), r=[ckt], w=[vaug])
                cosb = [A.alloc([128, TB], F32, f"cosb{i}") for i in range(2)]
                sinb = [A.alloc([128, TB], F32, f"sinb{i}") for i in range(2)]
                tmp1 = [A.alloc([128, TB], F32, f"rt1_{i}") for i in range(2)]
                tmp2 = [A.alloc([128, TB], F32, f"rt2_{i}") for i in range(2)]
                ri = 0
                for tb in range(ntb):
                    cb = cosb[tb % 2]; sb_ = sinb[tb % 2]
                    k.dma("sp", cb[:], cst["c_cos"][:, tb * TB:(tb + 1) * TB], w=[cb])
                    k.dma("sp", sb_[:], cst["c_sin"][:, tb * TB:(tb + 1) * TB], w=[sb_])
                    for j in range(5):
                        src = qT[:, j, tb * TB:(tb + 1) * TB] if j < 4 else kT[:, tb * TB:(tb + 1) * TB]
                        dstq = qfT[:, j, tb * TB:(tb + 1) * TB] if j < 4 else kfT[:, tb * TB:(tb + 1) * TB]
                        srcT = qT if j < 4 else kT
                        dstT = qfT if j < 4 else kfT
                        pk = ps[4 + pi % 3]
                        pi += 1
                        t1 = tmp1[ri % 2]; t2 = tmp2[ri % 2]
                        ri += 1
                        k.op("pe", lambda pk=pk, src=src: nc.tensor.matmul(pk[:, 0:TB], lhsT=rotB[:], rhs=src, start=True, stop=True),
                             r=[rotB, srcT], w=[pk])
                        k.op("dve", lambda src=src, t1=t1, cb=cb: nc.vector.tensor_tensor(
                            out=t1[:], in0=src, in1=cb[:], op=ALU.mult), r=[srcT, cb], w=[t1])
                        k.op("dve", lambda pk=pk, t2=t2, sb_=sb_: nc.vector.tensor_tensor(
                            out=t2[:], in0=pk[:, 0:TB], in1=sb_[:], op=ALU.mult), r=[pk, sb_], w=[t2])
                        k.op("pool", lambda dstq=dstq, t1=t1, t2=t2: nc.gpsimd.tensor_tensor(out=dstq, in0=t1[:], in1=t2[:], op=ALU.add),
                             r=[t1, t2], w=[dstT])
            A.release(m_in)
            if stop_after == f"E2@{si_}":
                return _finish(k, y_out, [(qfT, qfT[:, 0, 0:min(S, 1024)], 0), (kfT, kfT[:, 0:min(S, 1024)], 1), (pT, pT[:, 0, 8:8 + min(S, 1024)], 2)])
            pexs = [A.alloc([128, 10, 512], BF16, f"pex{i}") for i in range(2)]
            attn_tm = [A.alloc([128, 512], BF16, f"attn_tm{i}") for i in range(2)]
            den = A.alloc([128, 2, 4], F32, "den")
            for n in range(ntl):
                bg_step(1)
                if is_ctx:
                    kbs = [(kfT[:, kb * 128:(kb + 1) * 128], kfT, kb, None) for kb in range(ntl)]
                else:
                    kbs = []
                    for kb, msk in ((n - 1, mge), (n, None), (n + 1, mle)):
                        if 0 <= kb < ntl:
                            kbs.append((kfT[:, kb * 128:(kb + 1) * 128], kfT, kb, msk))
                    kbs.append((ckT[:, 0:128], ckT, ntl, None))
                    kbs.append((ckT[:, 128:256], ckT, ntl + 1, None))
                pex = pexs[n % 2]; atm = attn_tm[n % 2]
                si = 0
                for i, (kap, kbuf, slot, msk) in enumerate(kbs):
                    for g in range(2):
                        pk = ps[si % 2]
                        si += 1
                        k.op("pe", lambda pk=pk, kap=kap, g=g: nc.tensor.matmul(
                            pk[:, :].rearrange("p (j q) -> p j q", j=4), lhsT=kap[g * 64:(g + 1) * 64, :],
                            rhs=qfT[g * 64:(g + 1) * 64, :, n * 128:(n + 1) * 128], start=True, stop=True),
                            r=[kbuf, qfT], w=[pk])
                        k.op("act", lambda pk=pk, i=i, g=g: nc.scalar.activation(
                            out=pex[:, i * 2 + g, :], in_=pk[:, :], func=AF.Exp, scale=0.125), r=[pk], w=[pex])
                        if msk is not None:
                            k.op("pool", lambda i=i, g=g, msk=msk: nc.gpsimd.tensor_tensor(
                                out=pex[:, i * 2 + g, :].rearrange("p (j q) -> p j q", j=4),
                                in0=pex[:, i * 2 + g, :].rearrange("p (j q) -> p j q", j=4),
                                in1=msk[:].unsqueeze(1).to_broadcast([128, 4, 128]), op=ALU.mult), r=[pex, msk], w=[pex])
                for g in range(2):
                    po = ps[2 + g]
                    for j in range(4):
                        for i, (kap, kbuf, slot, msk) in enumerate(kbs):
                            k.op("pe", lambda po=po, j=j, i=i, g=g, slot=slot: nc.tensor.matmul(
                                po[:, j * 65:(j + 1) * 65], lhsT=pex[:, i * 2 + g, j * 128:(j + 1) * 128],
                                rhs=vaug[:, slot, g, :], start=(i == 0), stop=(i == len(kbs) - 1)),
                                r=[pex, vaug], w=[po])
                    pov = po[:, 0:260].rearrange("p (j e) -> p j e", e=65)
                    k.op("dve", lambda pov=pov, g=g: nc.vector.tensor_tensor(
                        out=den[:, g, :], in0=pov[:, :, 64], in1=esink[:, g * 4:(g + 1) * 4], op=ALU.add), r=[po, esink], w=[den])
                    k.op("dve", lambda g=g: nc.vector.reciprocal(out=den[:, g, :], in_=den[:, g, :]), r=[den], w=[den])
                    k.op("dve", lambda pov=pov, g=g, atm=atm: nc.vector.tensor_tensor(
                        out=atm[:, g * 256:(g + 1) * 256].rearrange("p (j d) -> p j d", j=4), in0=pov[:, :, 0:64],
                        in1=den[:, g, :].unsqueeze(2).to_broadcast([128, 4, 64]), op=ALU.mult), r=[po, den], w=[atm])
                for cc in range(4):
                    k.op("pe", lambda cc=cc, atm=atm: nc.tensor.transpose(
                        out=psb[:, cc * 128:(cc + 1) * 128], in_=atm[:, cc * 128:(cc + 1) * 128], identity=identB[:]),
                        r=[atm, identB], w=[psb])
                evac(catT[:, 0:4, n * 128:(n + 1) * 128], psb[:, 0:512].rearrange("p (c q) -> p c q", c=4), [psb], [catT])
            if stop_after == f"E4@{si_}":
                return _finish(k, y_out, [(catT, catT[:, cc_, 0:min(S, 1024)], cc_) for cc_ in range(4)])
            tA = A.alloc([128, S + 16], F32, "tA"); tB = A.alloc([128, S + 16], F32, "tB")
            pooledT = A.alloc([128, S], BF16, "pooledT")
            L = S + 16
            for g, wdw in enumerate((2, 4, 8, 16)):
                xg = pT[:, g, :]
                V = nc.vector
                k.op("dve", lambda xg=xg: V.tensor_tensor(out=tA[:, 1:L], in0=xg[:, 1:L], in1=xg[:, 0:L - 1], op=ALU.add), r=[pT], w=[tA])
                sres = tA
                if g >= 1:
                    k.op("dve", lambda: V.tensor_tensor(out=tB[:, 2:L - 1], in0=tA[:, 1:L - 2], in1=tA[:, 3:L], op=ALU.add), r=[tA], w=[tB])
                    sres = tB
                if g >= 2:
                    k.op("dve", lambda: V.tensor_tensor(out=tA[:, 4:L - 3], in0=tB[:, 2:L - 5], in1=tB[:, 6:L - 1], op=ALU.add), r=[tB], w=[tA])
                    sres = tA
                if g >= 3:
                    k.op("dve", lambda: V.tensor_tensor(out=tB[:, 8:8 + S], in0=tA[:, 4:4 + S], in1=tA[:, 12:12 + S], op=ALU.add), r=[tA], w=[tB])
                    sres = tB
                oth = tB if sres is tA else tA
                k.op("dve", lambda sres=sres, oth=oth, wdw=wdw: V.tensor_scalar(
                    out=oth[:, 8:8 + S], in0=sres[:, 8:8 + S], scalar1=1.0 / wdw, scalar2=None, op0=ALU.mult), r=[sres], w=[oth])
                k.op("dve", lambda oth=oth, g=g: V.tensor_tensor(
                    out=oth[:, 8:16], in0=oth[:, 8:16], in1=edge[:, g * 16:g * 16 + 8], op=ALU.mult), r=[oth, edge], w=[oth])
                k.op("dve", lambda oth=oth, g=g: V.tensor_tensor(
                    out=oth[:, S:S + 8], in0=oth[:, S:S + 8], in1=edge[:, g * 16 + 8:g * 16 + 16], op=ALU.mult), r=[oth, edge], w=[oth])
                k.op("dve", lambda oth=oth, xg=xg: V.tensor_tensor(
                    out=pooledT[:], in0=oth[:, 8:8 + S], in1=xg[:, 8:8 + S], op=ALU.subtract), r=[oth, pT], w=[pooledT])
                for tb in range(ntb):
                    pk = ps[4 + pi % 3]
                    pi += 1
                    k.op("pe", lambda pk=pk, g=g, tb=tb: nc.tensor.matmul(
                        pk[:, 0:TB], lhsT=poolwB[:, g, :], rhs=pooledT[:, tb * TB:(tb + 1) * TB], start=True, stop=True),
                        r=[poolwB, pooledT], w=[pk])
                    k.op("act", lambda pk=pk, g=g, tb=tb: nc.scalar.activation(
                        out=catT[:, 4 + g, tb * TB:(tb + 1) * TB], in_=pk[:, 0:TB], func=AF.Identity, scale=poolsc[:, g:g + 1]),
                        r=[pk, poolsc], w=[catT])
            if stop_after == f"E5@{si_}":
                return _finish(k, y_out, [(catT, catT[:, cc_, 0:min(S, 1024)], cc_) for cc_ in range(8)])
            A.release(m_mid)
            outproj_ln1(l, catT, w_out_e, gate_bc, t0, ntl, x_src, dst)
            if stop_after == f"E6@{si_}":
                return _finish(k, y_out, [])
            A.release(m_seq)
        A.release(m_ph)
        return None

    bg["c"] = BgConv(0)
    r_ = even_mixer_phase(0, x_in, y_out if (stop_after == "x1_0" or str(stop_after).startswith("E6@")) else X1)
    if r_ is not None:
        return r_
    if stop_after == "x1_0":
        return _finish(k, y_out, [])

    CG = 2
    NSLOT = 3

    class Deferred:
        def __init__(self):
            self.q = []

        COST = {"dve": 0.40, "act": 0.30, "pe": 0.12, "pool": 0.5}

        def op(self, e, *a, **kw):
            self.q.append((self.COST.get(e, 0.3), lambda: k.op(e, *a, **kw)))

        def dma(self, *a, **kw):
            self.q.append((0.1, lambda: k.dma(*a, **kw)))

        def pull(self, budget):
            spent = 0.0
            while self.q and spent < budget:
                c_, fn = self.q.pop(0)
                fn()
                spent += c_

    def evac_em(em, out_ap, in_ap, r, w):
        em.op("dve", lambda: nc.vector.tensor_copy(out=out_ap, in_=in_ap), r=r, w=w)

    def peer_phase(l, src, dst, ngroups=10):
        V_ = nc.vector
        m_ph = A.mark()
        k1B = A.alloc([128, 128], BF16, "k1B"); k2B = A.alloc([128, 128], BF16, "k2B")
        k.dma("pool", k1B[:], k1_in[l], w=[k1B])
        k.dma("pool", k2B[:], k2_in[l], w=[k2B])
        kB = (k1B, k2B)
        ln2g = A.alloc([128, D], F32, "ln2g"); ln2b = A.alloc([128, D], F32, "ln2b")
        k.dma("sp", ln2g[:], ln2_g[l:l + 1, :].partition_broadcast(128), w=[ln2g])
        k.dma("sp", ln2b[:], ln2_b[l:l + 1, :].partition_broadcast(128), w=[ln2b])
        gate2 = [A.alloc([128, D], F32, f"gate2_{c}") for c in range(2)]
        for c in range(2):
            k.dma("sp", gate2[c][:], modscr[l, c:c + 1, 5 * D:6 * D].partition_broadcast(128), r=[modscr], w=[gate2[c]])
        wqB = A.alloc([128, KC, 2048], BF16, "wqB")
        wq_v = wq_in[l].rearrange("(kk p) n -> p kk n", p=128)
        for hh in range(2):
            k.dma("pool", wqB[:, :, hh * 1024:(hh + 1) * 1024], wq_v[:, :, hh * 1024:(hh + 1) * 1024], w=[wqB])
        Wall = A.alloc([128, 128, 256], BF16, "Wall")
        if bg["c"] is not None:
            bg["c"].finish()
            bg["c"] = None
        ub = [sub[0], sub[1], A.alloc([128, CG, 1024], BF16, "ub2")]
        vb = [svb[0], svb[1], A.alloc([128, CG, 1024], BF16, "vb2")]
        xg = A.alloc([128, D], F32, "xg")
        u2Ts = [A.alloc([128, KC, 256], BF16, f"u2T{i}") for i in range(2)]
        igTs = [A.alloc([128, 3, 256], F32, f"igT{i}") for i in range(2)]
        qT = A.alloc([128, 16, 256], BF16, "pqT")
        s_all = A.alloc([128, 16, 128], F32, "s_all")
        scrB = A.alloc([128, 2048], F32, "scrB")
        cand = A.alloc([128, 8, 256], F32, "cand")
        rg = [T(cand[:, 4 * j:4 * j + 4, :].rearrange("p h n -> p (h n)"), f"rg{j}", buf=cand.b) for j in range(2)]
        xre = [T(scrB[:, j * 1024:(j + 1) * 1024], f"xre{j}", buf=scrB.b) for j in range(2)]
        vals = A.alloc([128, 16, 16], F32, "vals")
        idxu = A.alloc([128, 16, 16], U32, "idxu")
        idxf = A.alloc([128, 16, 16], F32, "idxf")
        tv = A.alloc([128, 8, 16], F32, "tv")
        pos = A.alloc([128, 8, 16], U32, "pos")
        au = A.alloc([128, 8, 16], U32, "au"); bu = A.alloc([128, 8, 16], U32, "bu")
        af = A.alloc([128, 8, 16], F32, "af"); bf = A.alloc([128, 8, 16], F32, "bf")
        sel = A.alloc([128, 3, 128], F32, "sel")
        ssum = A.alloc([128, 8], F32, "ssum")
        Qb = [A.alloc([128, 4, 128], BF16, f"Qb{i}") for i in range(2)]
        Pb = [A.alloc([128, 4, 128], BF16, f"Pb{i}") for i in range(2)]
        Gs = [A.alloc([128, 256], F32, f"Gs{i}") for i in range(2)]
        Zs = [[A.alloc([128, 128], BF16, f"Zs{i}_{j}") for j in range(2)] for i in range(2)]
        swork = scrB[:, :].rearrange("p (m n) -> p m n", n=128)
        cwork = scrB[:, :].rearrange("p (h n) -> p h n", n=256)
        eq = scrB[:, :].rearrange("p (h a b) -> p h a b", a=16, b=16)
        cand4 = cand[:, :, :].rearrange("p h (a b) -> p h a b", b=16)
        vals4 = vals[:, :, :].rearrange("p (h s) a -> p h s a", s=2)
        idxf4 = idxf[:, :, :].rearrange("p (h s) a -> p h s a", s=2)
        iota4 = iotaF[:, 0:16].unsqueeze(1).unsqueeze(1).to_broadcast([128, 8, 16, 16])
        RB = (ps[6], ps7)

        def stream_load(cg, grp):
            sl = cg % NSLOT
            if False:
                k.dma("pool", ub[sl][:], ut_in[l, cg * CG:(cg + 1) * CG].rearrange("c p n -> p c n"), w=[ub[sl]])
                k.dma("pool", vb[sl][:], v_in[l, cg * CG * 128:(cg + 1) * CG * 128, :].rearrange("(c p) n -> p c n", p=128), w=[vb[sl]])
                k.dma("sp", UTb[cg * CG:(cg + 1) * CG].rearrange("c p n -> p c n"), ub[sl][:], r=[ub[sl]], w=[UTb])
                k.dma("sp", Vb[cg * CG * 128:(cg + 1) * CG * 128, :].rearrange("(c p) n -> p c n", p=128), vb[sl][:], r=[vb[sl]], w=[Vb])
            else:
                k.dma("sp", ub[sl][:], UTb[cg * CG:(cg + 1) * CG].rearrange("c p n -> p c n"), r=[UTb], w=[ub[sl]])
                k.dma("sp", vb[sl][:], Vb[cg * CG * 128:(cg + 1) * CG * 128, :].rearrange("(c p) n -> p c n", p=128), r=[Vb], w=[vb[sl]])

        def R_build(em, grp, par):
            c = 0 if grp < 8 else 1
            u2T = u2Ts[par]; igT = igTs[par]
            for j in range(2):
                row0 = (2 * grp + j) * 128
                em.dma("sp", xg[:], src[row0:row0 + 128, :], r=[src], w=[xg])
                for kk in range(KC):
                    pk = RB[kk // 4]
                    em.op("pe", lambda kk=kk, pk=pk: nc.tensor.transpose(
                        out=pk[:, (kk % 4) * 128:(kk % 4 + 1) * 128], in_=xg[:, kk * 128:(kk + 1) * 128],
                        identity=identF[:]), r=[xg, identF], w=[pk])
                for kk in range(KC):
                    pk = RB[kk // 4]
                    em.op("act", lambda kk=kk, pk=pk, j=j: nc.scalar.activation(
                        out=u2T[:, kk, j * 128:(j + 1) * 128], in_=pk[:, (kk % 4) * 128:(kk % 4 + 1) * 128],
                        func=AF.Identity, scale=fm1p[l][:, 4 * 8 + kk, c:c + 1], bias=fm[l][:, 3 * 8 + kk, c:c + 1]),
                        r=[pk, fm1p[l], fm[l]], w=[u2T])
            for m in range(16):
                pk = RB[m % 2]
                for kk in range(KC):
                    em.op("pe", lambda kk=kk, pk=pk, m=m: nc.tensor.matmul(
                        pk[:, 0:256], lhsT=wqB[:, kk, m * 128:(m + 1) * 128], rhs=u2T[:, kk, :],
                        start=(kk == 0), stop=(kk == KC - 1)), r=[wqB, u2T], w=[pk])
                evac_em(em, qT[:, m, :], pk[:, 0:256], [pk], [qT])
            for j in range(2):
                for mq in range(4):
                    pk = RB[mq % 2]
                    for mm in range(4):
                        m = mq * 4 + mm
                        em.op("pe", lambda pk=pk, m=m, mm=mm, j=j: nc.tensor.matmul(
                            pk[:, mm * 128:(mm + 1) * 128], lhsT=qT[:, m, j * 128:(j + 1) * 128], rhs=kB[m % 2][:],
                            start=True, stop=True), r=[qT, kB[m % 2]], w=[pk])
                    evac_em(em, s_all[:, mq * 4:(mq + 1) * 4, :], pk[:, :].rearrange("p (a b) -> p a b", b=128), [pk], [s_all])
                for m in range(16):
                    em.op("dve", lambda m=m: V_.max(out=vals[:, m, 0:8], in_=s_all[:, m, :]), r=[s_all], w=[vals])
                for m in range(16):
                    em.op("dve", lambda m=m: V_.max_index(out=idxu[:, m, 0:8], in_max=vals[:, m, 0:8], in_values=s_all[:, m, :]),
                          r=[s_all, vals], w=[idxu])
                for m in range(16):
                    em.op("dve", lambda m=m: V_.match_replace(out=swork[:, m, :], in_to_replace=vals[:, m, 0:8],
                                                              in_values=s_all[:, m, :], imm_value=NEG), r=[s_all, vals], w=[scrB])
                for m in range(16):
                    em.op("dve", lambda m=m: V_.max(out=vals[:, m, 8:16], in_=swork[:, m, :]), r=[scrB], w=[vals])
                for m in range(16):
                    em.op("dve", lambda m=m: V_.max_index(out=idxu[:, m, 8:16], in_max=vals[:, m, 8:16], in_values=swork[:, m, :]),
                          r=[scrB, vals], w=[idxu])
                em.op("dve", lambda: V_.tensor_tensor(
                    out=cand4, in0=vals4[:, :, 0, :].unsqueeze(3).to_broadcast([128, 8, 16, 16]),
                    in1=vals4[:, :, 1, :].unsqueeze(2).to_broadcast([128, 8, 16, 16]), op=ALU.add), r=[vals], w=[cand])
                for h in range(8):
                    em.op("dve", lambda h=h: V_.max(out=tv[:, h, 0:8], in_=cand[:, h, :]), r=[cand], w=[tv])
                for h in range(8):
                    em.op("dve", lambda h=h: V_.max_index(out=pos[:, h, 0:8], in_max=tv[:, h, 0:8], in_values=cand[:, h, :]),
                          r=[cand, tv], w=[pos])
                for h in range(8):
                    em.op("dve", lambda h=h: V_.match_replace(out=cwork[:, h, :], in_to_replace=tv[:, h, 0:8],
                                                              in_values=cand[:, h, :], imm_value=NEG), r=[cand, tv], w=[scrB])
                for h in range(8):
                    em.op("dve", lambda h=h: V_.max(out=tv[:, h, 8:16], in_=cwork[:, h, :]), r=[scrB], w=[tv])
                for h in range(8):
                    em.op("dve", lambda h=h: V_.max_index(out=pos[:, h, 8:16], in_max=tv[:, h, 8:16], in_values=cwork[:, h, :]),
                          r=[scrB, tv], w=[pos])
                em.op("dve", lambda: V_.tensor_single_scalar(out=au[:], in_=pos[:], scalar=4, op=ALU.logical_shift_right), r=[pos], w=[au])
                em.op("dve", lambda: V_.tensor_single_scalar(out=bu[:], in_=pos[:], scalar=15, op=ALU.bitwise_and), r=[pos], w=[bu])
                em.op("dve", lambda: V_.tensor_copy(out=af[:], in_=au[:]), r=[au], w=[af])
                em.op("dve", lambda: V_.tensor_copy(out=bf[:], in_=bu[:]), r=[bu], w=[bf])
                em.op("dve", lambda: V_.tensor_copy(out=idxf[:], in_=idxu[:]), r=[idxu], w=[idxf])
                for s_, abf in ((0, af), (1, bf)):
                    em.op("dve", lambda abf=abf: V_.tensor_tensor(
                        out=eq, in0=abf[:, :, :].unsqueeze(3).to_broadcast([128, 8, 16, 16]), in1=iota4, op=ALU.is_equal),
                        r=[abf, iotaF], w=[scrB])
                    em.op("dve", lambda s_=s_: V_.tensor_tensor(
                        out=eq, in0=eq, in1=idxf4[:, :, s_, :].unsqueeze(2).to_broadcast([128, 8, 16, 16]), op=ALU.mult),
                        r=[scrB, idxf], w=[scrB])
                    em.op("dve", lambda s_=s_: V_.tensor_reduce(
                        out=sel[:, s_, :].rearrange("p (h a) -> p h a", a=16), in_=eq, axis=AX.X, op=ALU.add), r=[scrB], w=[sel])
                selg = sel[:, 2, :].rearrange("p (h a) -> p h a", a=16)
                em.op("dve", lambda: V_.tensor_tensor(out=selg, in0=tv[:], in1=tv[:, :, 0:1].to_broadcast([128, 8, 16]), op=ALU.subtract),
                      r=[tv], w=[sel])
                em.op("act", lambda: nc.scalar.activation(out=selg, in_=selg, func=AF.Exp), r=[sel], w=[sel])
                em.op("dve", lambda: V_.tensor_reduce(out=ssum[:], in_=selg, axis=AX.X, op=ALU.add), r=[sel], w=[ssum])
                em.op("dve", lambda: V_.reciprocal(out=ssum[:], in_=ssum[:]), r=[ssum], w=[ssum])
                em.op("dve", lambda: V_.tensor_tensor(out=selg, in0=selg, in1=ssum[:].unsqueeze(2).to_broadcast([128, 8, 16]), op=ALU.mult),
                      r=[sel, ssum], w=[sel])
                for q3 in range(3):
                    em.op("pe", lambda q3=q3: nc.tensor.transpose(out=RB[0][:, q3 * 128:(q3 + 1) * 128], in_=sel[:, q3, :], identity=identF[:]),
                          r=[sel, identF], w=[RB[0]])
                evac_em(em, igT[:, :, j * 128:(j + 1) * 128], RB[0][:, 0:384].rearrange("p (q t) -> p q t", q=3), [RB[0]], [igT])

        def W_build(par, dqF=None):
            igT = igTs[par]
            for t4 in range(64):
                if dqF is not None and t4 >= 1:
                    dqF.pull(0.45)
                qb = Qb[t4 % 2]; pb_ = Pb[t4 % 2]; pw = ps[4 + t4 % 2]
                for tt in range(4):
                    t = t4 * 4 + tt
                    on_pool = False
                    e_ = "pool" if on_pool else "dve"
                    E_ = nc.gpsimd if on_pool else V_
                    k.op(e_, lambda tt=tt, t=t, E_=E_: E_.tensor_scalar(
                        out=qb[:, tt, :], in0=iotaB[:], scalar1=igT[:, 1, t:t + 1], scalar2=None, op0=ALU.is_equal),
                        r=[iotaB, igT], w=[qb])
                    k.op(e_, lambda tt=tt, t=t, E_=E_: E_.tensor_scalar(
                        out=pb_[:, tt, :], in0=iotaB[:], scalar1=igT[:, 0, t:t + 1], scalar2=igT[:, 2, t:t + 1],
                        op0=ALU.is_equal, op1=ALU.mult), r=[iotaB, igT], w=[pb_])
                for tt in range(4):
                    k.op("pe", lambda tt=tt: nc.tensor.matmul(
                        pw[:, tt * 128:(tt + 1) * 128], lhsT=qb[:, tt, :], rhs=pb_[:, tt, :], start=True, stop=True),
                        r=[qb, pb_], w=[pw])
                k.op("act", lambda t4=t4: nc.scalar.activation(
                    out=Wall[:, :, t4 * 4:(t4 + 1) * 4], in_=pw[:, :].rearrange("p (t i) -> p i t", t=4), func=AF.Copy),
                    r=[pw], w=[Wall])
            if dqF is not None:
                dqF.pull(1e9)

        def S_run(grp, par, dq):
            u2T = u2Ts[par]
            ncg = 128 // CG

            def u_side(c_):
                sl = (c_ // CG) % NSLOT; ci = c_ % CG
                pa = ps[4 + c_ % 2]
                for kk in range(KC):
                    k.op("pe", lambda kk=kk: nc.tensor.matmul(
                        pa[:, 0:256], lhsT=ub[sl][:, ci, kk * 128:(kk + 1) * 128], rhs=u2T[:, kk, :],
                        start=(kk == 0), stop=(kk == KC - 1)), r=[ub[sl], u2T], w=[pa])

            def mid(c_):
                pa = ps[4 + c_ % 2]; G = Gs[c_ % 2]; Z = Zs[c_ % 2]
                k.op("act", lambda: nc.scalar.activation(out=G[:], in_=pa[:, 0:256], func=AF.Gelu_apprx_tanh), r=[pa], w=[G])
                k.op("pool", lambda: nc.gpsimd.tensor_tensor(out=Z[0][:], in0=G[:, 0:128], in1=Wall[:, c_, 0:128], op=ALU.mult),
                     r=[G, Wall], w=[Z[0]])
                k.op("pool", lambda: nc.gpsimd.tensor_tensor(out=Z[1][:], in0=G[:, 128:256], in1=Wall[:, c_, 128:256], op=ALU.mult),
                     r=[G, Wall], w=[Z[1]])

            def v_side(c_):
                sl = (c_ // CG) % NSLOT; ci = c_ % CG
                Z = Zs[c_ % 2]
                for j in range(2):
                    for hh in range(2):
                        k.op("pe", lambda j=j, hh=hh: nc.tensor.matmul(
                            ps[j * 2 + hh][:, :], lhsT=Z[j][:], rhs=vb[sl][:, ci, hh * 512:(hh + 1) * 512],
                            start=(c_ == 0), stop=(c_ == 127)), r=[Z[j], vb[sl]], w=[ps[j * 2 + hh]])

            npull = (sum(c__ for c__, _ in dq.q) / 112.0) if dq is not None else 0
            u_side(0)
            mid(0)
            for c_ in range(128):
                if c_ % CG == 0 and c_ // CG + 2 < ncg:
                    stream_load(c_ // CG + 2, grp)
                if c_ + 1 < 128:
                    u_side(c_ + 1)
                v_side(c_)
                if c_ + 1 < 128:
                    mid(c_ + 1)
                if dq is not None:
                    dq.pull(npull)
            if dq is not None:
                dq.pull(1e9)

        def F_build(em, grp):
            c = 0 if grp < 8 else 1
            for j in range(2):
                row0 = (2 * grp + j) * 128
                em.dma("sp", xre[j][:], src[row0:row0 + 128, :], r=[src], w=[xre[j]])
            for j in range(2):
                row0 = (2 * grp + j) * 128
                for hh in range(2):
                    em.op("dve", lambda j=j, hh=hh: V_.tensor_tensor(
                        out=rg[j][:, hh * 512:(hh + 1) * 512], in0=ps[j * 2 + hh][:, :], in1=gate2[c][:, hh * 512:(hh + 1) * 512],
                        op=ALU.mult), r=[ps[j * 2 + hh], gate2[c]], w=[rg[j]])
                em.op("dve", lambda j=j: V_.scalar_tensor_tensor(
                    out=rg[j][:], in0=xre[j][:], scalar=ALPHA, in1=rg[j][:], op0=ALU.mult, op1=ALU.add), r=[xre[j], rg[j]], w=[rg[j]])
                layer_norm_tile(rg[j], ln2g, ln2b, xre[j], em=em)
                em.dma("sp", dst[row0:row0 + 128, :], xre[j][:], r=[xre[j]], w=[dst])

        R_build(k, 0, 0)
        dqF = None
        for grp in range(ngroups):
            par = grp % 2
            stream_load(0, grp)
            stream_load(1, grp)
            W_build(par, dqF)
            dq = None
            if grp + 1 < ngroups:
                dq = Deferred()
                R_build(dq, grp + 1, 1 - par)
            S_run(grp, par, dq)
            dqF = Deferred()
            F_build(dqF, grp)
        dqF.pull(1e9)
        A.release(m_ph)
        return None

    if stop_after == "x2_0g1":
        peer_phase(0, X1, y_out, ngroups=1)
        return _finish(k, y_out, [])
    peer_phase(0, X1, y_out if stop_after == "x2_0" else X2)
    if stop_after == "x2_0":
        return _finish(k, y_out, [])

    def odd_mixer_phase(l, x_src, dst):
        V_ = nc.vector
        m_ph = A.mark()
        w_inoB = A.alloc([128, KC, 2048], BF16, "w_inoB")
        wv = w_in_o[:].rearrange("(kk p) n -> p kk n", p=128)
        for hh in range(2):
            k.dma("pool", w_inoB[:, :, hh * 1024:(hh + 1) * 1024], wv[:, :, hh * 1024:(hh + 1) * 1024], w=[w_inoB])
        convw = A.alloc([128, 8, 4], F32, "convw"); convb = A.alloc([128, 8], F32, "convb")
        k.dma("sp", convw[:], conv_w[:], w=[convw])
        k.dma("sp", convb[:], conv_b[:], w=[convb])
        gaB = A.alloc([128, 2, 8, 128], BF16, "gaB"); gxB = A.alloc([128, 2, 8, 128], BF16, "gxB")
        for d_ in range(2):
            k.dma("pool", gaB[:, d_, :, :], ga_w[d_].rearrange("c i j -> i c j"), w=[gaB])
            k.dma("pool", gxB[:, d_, :, :], gx_w[d_].rearrange("c i j -> i c j"), w=[gxB])
        gab = A.alloc([128, 2, 8], F32, "gab"); gxb = A.alloc([128, 2, 8], F32, "gxb")
        lam = A.alloc([128, 2, 8], F32, "lam"); h0s = A.alloc([128, 2, 8], F32, "h0s")
        k.dma("sp", gab[:], ga_b[:], w=[gab]); k.dma("sp", gxb[:], gx_b[:], w=[gxb])
        k.dma("sp", lam[:], lam_in[:], w=[lam]); k.dma("sp", h0s[:], h0_in[:], w=[h0s])
        nsp8 = A.alloc([128, 2, 8], F32, "nsp8"); nsp16 = A.alloc([128, 2, 8], F32, "nsp16")
        k.op("act", lambda: nc.scalar.activation(out=lam[:], in_=lam[:], func=AF.Exp, scale=-1.0), r=[lam], w=[lam])
        k.op("act", lambda: nc.scalar.activation(out=lam[:], in_=lam[:], func=AF.Ln, bias=oneT[:, 0:1], scale=1.0), r=[lam, oneT], w=[lam])
        k.op("dve", lambda: V_.tensor_scalar(out=nsp8[:], in0=lam[:], scalar1=-8.0, scalar2=None, op0=ALU.mult), r=[lam], w=[nsp8])
        k.op("dve", lambda: V_.tensor_scalar(out=nsp16[:], in0=lam[:], scalar1=-16.0, scalar2=None, op0=ALU.mult), r=[lam], w=[nsp16])
        k.op("dve", lambda: V_.memset(stT[:], 0.0), w=[stT])
        for si_, (t0, ntl, c, is_ctx) in enumerate(SEQS):
            S = ntl * 128
            TB = min(S, 512)
            ntb = S // TB
            m_seq = A.mark()
            zT = A.alloc([128, KC, S], BF16, "zT")
            gate_bc = A.alloc([128, D], F32, "gate_bc")
            k.dma("sp", gate_bc[:], modscr[l, c:c + 1, 2 * D:3 * D].partition_broadcast(128), r=[modscr], w=[gate_bc])
            m_mid = A.mark()
            uT = A.alloc([128, KC, S], BF16, "uT")
            xts = [A.alloc([128, D], F32, f"xt{i}") for i in range(2)]
            for t in range(ntl):
                pp = (ps[0], ps[1]) if t % 2 == 0 else (ps[2], ps[3])
                load_transpose_mod(l, 0, 1, c, x_src, (t0 + t) * 128, xts[t % 2], uT, t * 128, pp)
                bg_step(1)
            xr = A.alloc([128, S + 3], F32, "xr")
            xc = A.alloc([128, S], F32, "xc")
            xcb = A.alloc([128, S], BF16, "xcb")
            gg = A.alloc([128, S], F32, "gg")
            b1 = [A.alloc([128, S], F32, f"b1_{i}") for i in range(2)]
            b2 = A.alloc([128, S], F32, "b2")
            b3 = A.alloc([128, S], F32, "b3")
            k.op("pool", lambda: nc.gpsimd.memset(xr[:], 0.0), w=[xr])
            pi = 0
            for cf in range(8):
                bg_step(3 if not is_ctx else 1)
                for which, c0 in ((0, cf * 128), (1, 1024 + cf * 128)):
                    for tb in range(ntb):
                        pk = ps[4 + pi % 3]
                        pi += 1
                        for kk in range(KC):
                            k.op("pe", lambda kk=kk, pk=pk, c0=c0, tb=tb: nc.tensor.matmul(
                                pk[:, 0:TB], lhsT=w_inoB[:, kk, c0:c0 + 128], rhs=uT[:, kk, tb * TB:(tb + 1) * TB],
                                start=(kk == 0), stop=(kk == KC - 1)), r=[w_inoB, uT], w=[pk])
                        if which == 0:
                            evac(xr[:, 2 + tb * TB:2 + (tb + 1) * TB], pk[:, 0:TB], [pk], [xr])
                        else:
                            k.op("act", lambda pk=pk, tb=tb: nc.scalar.activation(
                                out=gg[:, tb * TB:(tb + 1) * TB], in_=pk[:, 0:TB], func=AF.Gelu_apprx_tanh), r=[pk], w=[gg])
                k.op("dve", lambda cf=cf: V_.tensor_scalar(out=xc[:], in0=xr[:, 0:S], scalar1=convw[:, cf, 0:1], scalar2=convb[:, cf:cf + 1],
                                                           op0=ALU.mult, op1=ALU.add), r=[xr, convw, convb], w=[xc])
                for jj in range(1, 4):
                    k.op("dve", lambda cf=cf, jj=jj: V_.scalar_tensor_tensor(
                        out=xc[:], in0=xr[:, jj:jj + S], scalar=convw[:, cf, jj:jj + 1], in1=xc[:], op0=ALU.mult, op1=ALU.add),
                        r=[xr, convw, xc], w=[xc])
                k.op("act", lambda: nc.scalar.activation(out=xcb[:], in_=xc[:], func=AF.Copy), r=[xc], w=[xcb])
                for d_ in range(2):
                    H = b1[d_]
                    for gw, gb, dstb in ((gaB, gab, H), (gxB, gxb, b2)):
                        for tb in range(ntb):
                            pk = ps[pi % 4]
                            pi += 1
                            k.op("pe", lambda pk=pk, gw=gw, tb=tb, d_=d_, cf=cf: nc.tensor.matmul(
                                pk[:, 0:TB], lhsT=gw[:, d_, cf, :], rhs=xcb[:, tb * TB:(tb + 1) * TB], start=True, stop=True),
                                r=[gw, xcb], w=[pk])
                            k.op("act", lambda pk=pk, gb=gb, dstb=dstb, tb=tb, d_=d_, cf=cf: nc.scalar.activation(
                                out=dstb[:, tb * TB:(tb + 1) * TB], in_=pk[:, 0:TB], func=AF.Sigmoid, bias=gb[:, d_, cf:cf + 1], scale=1.0),
                                r=[pk, gb], w=[dstb])
                    k.op("act", lambda H=H, d_=d_, cf=cf: nc.scalar.activation(out=b3[:], in_=H[:], func=AF.Exp, scale=nsp8[:, d_, cf:cf + 1]),
                         r=[H, nsp8], w=[b3])
                    k.op("act", lambda H=H, d_=d_, cf=cf: nc.scalar.activation(out=H[:], in_=H[:], func=AF.Exp, scale=nsp16[:, d_, cf:cf + 1]),
                         r=[H, nsp16], w=[H])
                    k.op("act", lambda H=H: nc.scalar.activation(out=H[:], in_=H[:], func=AF.Sqrt, bias=oneT[:, 0:1], scale=-1.0),
                         r=[H, oneT], w=[H])
                    k.op("dve", lambda: V_.tensor_tensor(out=b2[:], in0=b2[:], in1=xc[:], op=ALU.mult), r=[b2, xc], w=[b2])
                    k.op("dve", lambda H=H: V_.tensor_tensor(out=b2[:], in0=b2[:], in1=H[:], op=ALU.mult), r=[b2, H], w=[b2])
                    if is_ctx:
                        init = 0.0
                        rinit = []
                    else:
                        init = h0s[:, d_, cf:cf + 1]
                        rinit = [h0s]
                    if d_ == 0:
                        k.op("dve", lambda H=H, init=init: V_.tensor_tensor_scan(
                            out=H[:], data0=b3[:], data1=b2[:], initial=init, op0=ALU.mult, op1=ALU.add), r=[b3, b2] + rinit, w=[H])
                        if is_ctx:
                            k.op("dve", lambda H=H, cf=cf, si_=si_: V_.tensor_copy(out=stT[:, si_ - 1, 0, cf:cf + 1], in_=H[:, S - 1:S]), r=[H], w=[stT])
                    else:
                        k.op("dve", lambda H=H, init=init: V_.tensor_tensor_scan(
                            out=H[:, ::-1], data0=b3[:, ::-1], data1=b2[:, ::-1], initial=init, op0=ALU.mult, op1=ALU.add),
                            r=[b3, b2] + rinit, w=[H])
                        if is_ctx:
                            k.op("dve", lambda H=H, cf=cf, si_=si_: V_.tensor_copy(out=stT[:, si_ - 1, 1, cf:cf + 1], in_=H[:, 0:1]), r=[H], w=[stT])
                k.op("dve", lambda: V_.tensor_tensor(out=b2[:], in0=b1[0][:], in1=b1[1][:], op=ALU.add), r=[b1[0], b1[1]], w=[b2])
                k.op("dve", lambda cf=cf: V_.tensor_tensor(out=zT[:, cf, :], in0=b2[:], in1=gg[:], op=ALU.mult), r=[b2, gg], w=[zT])
            A.release(m_mid)
            outproj_ln1(l, zT, w_out_o, gate_bc, t0, ntl, x_src, dst)
            A.release(m_seq)
        k.op("pe", lambda: nc.tensor.transpose(out=ps[4][0:32, 0:128], in_=stT[:].rearrange("p s d c -> p (s d c)"), identity=identF[:]),
             r=[stT, identF], w=[ps[4]])
        st_o = A.alloc([32, 128], F32, "st_o")
        k.op("dve", lambda: V_.tensor_copy(out=st_o[:], in_=ps[4][0:32, 0:128]), r=[ps[4]], w=[st_o])
        k.dma("sp", ns_out[:].rearrange("s d (c p) -> (s d c) p", p=128), st_o[:], r=[st_o], w=[ns_out])
        A.release(m_ph)

    bg["c"] = BgConv(1)
    odd_mixer_phase(1, X2, y_out if stop_after == "x1_1" else X1)
    if stop_after == "x1_1":
        return _finish(k, y_out, [])
    peer_phase(1, X1, y_out)
    return _finish(k, y_out, [])


def _finish(k, y_out, dbg):
    for t, ap, row in dbg:
        p, n = ap.shape
        q = "sp" if ap.dtype == F32 else "pool"
        k.dma(q, y_out[row * 128:row * 128 + p, 0:n], ap, r=[t], w=[y_out])
    k.barrier(["sp"])
    return k


Q_PERM = [0, 4, 1, 5, 2, 6, 3, 7]


def _shared(inp):
    f = lambda a: np.ascontiguousarray(np.asarray(a, dtype=np.float32))
    sh = {}
    sh["ada_w"] = f(inp["ada_w"]); sh["ada_b"] = f(inp["ada_b"])
    sh["ada_bT"] = f(np.asarray(inp["ada_b"]).reshape(2, 48, 128).transpose(0, 2, 1))
    for n in ("ln1_g", "ln1_b", "ln2_g", "ln2_b"):
        sh[n] = f(inp[n])
    wi = np.asarray(inp["even_w_in"][0], np.float32)
    qcols = np.concatenate([np.arange(h * 64, (h + 1) * 64) for h in Q_PERM])
    sh["w_in_e"] = f(np.concatenate([wi[:, qcols], wi[:, 512:]], axis=1))
    sh["sink"] = f(inp["attn_sink"]).reshape(1, 8)
    sh["pool_w"] = f(inp["pool_w"][0])
    sh["pool_sc"] = f(np.asarray(inp["pool_scale"][0]).reshape(4, 128).T)
    sh["w_out_e"] = f(inp["even_w_out"][0])
    sh["w_in_o"] = f(inp["odd_w_in"][0])
    sh["conv_w"] = f(np.asarray(inp["conv_w"][0]).reshape(4, 8, 128).transpose(2, 1, 0))
    sh["conv_b"] = f(np.asarray(inp["conv_b"][0]).reshape(8, 128).T)
    sh["ga_w"] = f(inp["gate_a_w"][0]); sh["gx_w"] = f(inp["gate_x_w"][0])
    sh["ga_b"] = f(np.asarray(inp["gate_a_b"][0]).reshape(2, 8, 128).transpose(2, 0, 1))
    sh["gx_b"] = f(np.asarray(inp["gate_x_b"][0]).reshape(2, 8, 128).transpose(2, 0, 1))
    sh["lam"] = f(np.asarray(inp["lru_lambda"][0]).reshape(2, 8, 128).transpose(2, 0, 1))
    sh["w_out_o"] = f(inp["odd_w_out"][0])
    sh["wq"] = f(inp["peer_wq"])
    sh["k1T"] = f(np.asarray(inp["peer_k1"]).transpose(0, 2, 1))
    sh["k2T"] = f(np.asarray(inp["peer_k2"]).transpose(0, 2, 1))
    U = np.asarray(inp["peer_u"], np.float32)
    sh["peer_ut"] = f(U.reshape(2, 128, 128, 8, 128).transpose(0, 1, 4, 3, 2).reshape(2, 128, 128, 1024))
    sh["peer_v"] = f(inp["peer_v"])
    sh.update(_consts())
    return sh


def _percore(inp, b):
    f = lambda a: np.ascontiguousarray(np.asarray(a, dtype=np.float32))
    m = {}
    m["x"] = f(np.concatenate([inp["x_sample"][b], inp["x_prompt"][2 * b], inp["x_prompt"][2 * b + 1]], axis=0))
    m["ck"] = f(np.asarray(inp["cache_attn_k"][b, 0]).reshape(256, 128))
    m["cv"] = f(np.asarray(inp["cache_attn_v"][b, 0]).reshape(256, 128))
    m["h0"] = f(np.asarray(inp["state_lru"][b, 0]).reshape(2, 8, 128).transpose(2, 0, 1))
    cond = np.stack([np.asarray(inp["c"][b]), np.asarray(inp["c_ctx"])], 0).astype(np.float32)
    m["condT"] = f(cond.reshape(2, 8, 128).transpose(2, 1, 0))
    return m


def kernel(**inputs):
    nk = build()
    sh = _shared(inputs)
    in_maps = []
    for b in range(8):
        m = dict(sh)
        m.update(_percore(inputs, b))
        in_maps.append(m)
    res = run_bass_kernel_spmd(nk.nc, in_maps, core_ids=list(range(8)))
    y_prompt = np.zeros((16, 256, D), np.float32)
    y_sample = np.zeros((8, 2048, D), np.float32)
    new_k = np.zeros((16, 1, 256, 2, 64), np.float32)
    new_v = np.zeros((16, 1, 256, 2, 64), np.float32)
    new_s = np.zeros((16, 1, 2, D), np.float32)
    for b in range(8):
        r = res.results[b]
        y = r["y"]
        y_sample[b] = y[0:2048]
        y_prompt[2 * b] = y[2048:2304]
        y_prompt[2 * b + 1] = y[2304:2560]
        new_k[2 * b, 0] = r["new_k"][0:256].reshape(256, 2, 64)
        new_k[2 * b + 1, 0] = r["new_k"][256:512].reshape(256, 2, 64)
        new_v[2 * b, 0] = r["new_v"][0:256].reshape(256, 2, 64)
        new_v[2 * b + 1, 0] = r["new_v"][256:512].reshape(256, 2, 64)
        new_s[2 * b, 0] = r["new_s"][0]
        new_s[2 * b + 1, 0] = r["new_s"][1]
    return (y_prompt, y_sample, new_k, new_v, new_s)
```
